# Optimizing a Trainium2 kernel written in Bass

```python
import math
import jax, jax.numpy as jnp
from jax import lax
import numpy as np

D_MODEL = 2048
BATCH = 2
SEQ = 16384
DEPTH = 1

CONV_CH = 1024
CONV_KERNEL = 31
N_HEADS = 8
HEAD_DIM = 128
ATTN_WIDTH = N_HEADS * HEAD_DIM
ROT_DIM = HEAD_DIM // 4
ROPE_THETA = 500000.0
MOBA_BLOCK = 256
MOBA_TOPK = 3
Q_CHUNK = 64
N_BRANCH = 2
D_FF = 5632
FFN_CONV = 3
LN_EPS = 1e-5
DEEPNORM_ALPHA = (2.0 * DEPTH) ** 0.25
DEEPNORM_BETA = (8.0 * DEPTH) ** -0.25

IN_SPLITS = [2 * CONV_CH,
             2 * CONV_CH + ATTN_WIDTH,
             2 * CONV_CH + 2 * ATTN_WIDTH,
             2 * CONV_CH + 3 * ATTN_WIDTH]
IN_COLS = 2 * CONV_CH + 3 * ATTN_WIDTH + N_BRANCH * D_MODEL

kernel_name = "hybrid_conformer_moba_convffn_deepnorm_adaln"


def layer_norm(x, g, b):
    xf = x.astype(jnp.float32)
    mu = xf.mean(-1, keepdims=True)
    var = jnp.square(xf - mu).mean(-1, keepdims=True)
    return ((xf - mu) * lax.rsqrt(var + LN_EPS)).astype(x.dtype) * g + b


def causal_depthwise_conv(x, w, b):
    k = w.shape[0]
    xp = jnp.pad(x, ((0, 0), (k - 1, 0), (0, 0)))
    y = lax.conv_general_dilated(xp, w[:, None, :].astype(x.dtype), window_strides=(1,), padding="VALID",
                                 dimension_numbers=("NWC", "WIO", "NWC"),
                                 feature_group_count=x.shape[-1])
    return y + b


def rope_tables(seq_len, dtype):
    pos = jnp.arange(seq_len, dtype=jnp.float32)
    inv_freq = ROPE_THETA ** (-jnp.arange(0, ROT_DIM, 2, dtype=jnp.float32) / ROT_DIM)
    ang = pos[:, None] * inv_freq[None, :]
    return jnp.cos(ang).astype(dtype), jnp.sin(ang).astype(dtype)


def apply_partial_rope(x, cos, sin):
    half = ROT_DIM // 2
    x1, x2, rest = x[..., :half], x[..., half:ROT_DIM], x[..., ROT_DIM:]
    c = cos[None, :, None, :]
    s = sin[None, :, None, :]
    return jnp.concatenate([x1 * c - x2 * s, x1 * s + x2 * c, rest], axis=-1)


def moba_attention(q, k, v):
    b, h, s, dh = q.shape
    sp = -(-s // MOBA_BLOCK) * MOBA_BLOCK
    pad = ((0, 0), (0, 0), (0, sp - s), (0, 0))
    q, k, v = jnp.pad(q, pad), jnp.pad(k, pad), jnp.pad(v, pad)
    nb = sp // MOBA_BLOCK
    ks = min(MOBA_TOPK, nb)
    kb = k.reshape(b, h, nb, MOBA_BLOCK, dh)
    vb = v.reshape(b, h, nb, MOBA_BLOCK, dh)

    k_mean = kb.astype(jnp.float32).mean(axis=3)
    gate = jnp.einsum("bhsd,bhnd->bhsn", q.astype(jnp.float32), k_mean)
    q_block = jnp.arange(sp) // MOBA_BLOCK
    fully_past = jnp.arange(nb)[None, :] < q_block[:, None]
    gate = jnp.where(fully_past, gate, -jnp.inf)
    _, sel = lax.top_k(gate, ks)

    n_chunks = sp // Q_CHUNK
    q_c = q.reshape(b, h, n_chunks, Q_CHUNK, dh).transpose(2, 0, 1, 3, 4)
    sel_c = sel.reshape(b, h, n_chunks, Q_CHUNK, ks).transpose(2, 0, 1, 3, 4)
    gather_blocks = jax.vmap(jax.vmap(lambda blocks, idx: blocks[idx]))
    scale = HEAD_DIM ** -0.5

    def chunk_attend(args):
        ci, qc, selc = args
        q_start = ci * Q_CHUNK
        blk = q_start // MOBA_BLOCK
        q_pos = q_start + jnp.arange(Q_CHUNK)
        k_sel = gather_blocks(kb, selc)
        v_sel = gather_blocks(vb, selc)
        k_own = lax.dynamic_index_in_dim(kb, blk, axis=2, keepdims=False)
        v_own = lax.dynamic_index_in_dim(vb, blk, axis=2, keepdims=False)
        s_sel = jnp.einsum("bhqd,bhqntd->bhqnt", qc, k_sel,
                           preferred_element_type=jnp.float32).reshape(b, h, Q_CHUNK, ks * MOBA_BLOCK) * scale
        s_own = jnp.einsum("bhqd,bhtd->bhqt", qc, k_own, preferred_element_type=jnp.float32) * scale
        sel_ok = jnp.repeat(jnp.arange(ks) < blk, MOBA_BLOCK)
        own_ok = (blk * MOBA_BLOCK + jnp.arange(MOBA_BLOCK))[None, :] <= q_pos[:, None]
        logits = jnp.concatenate([jnp.where(sel_ok, s_sel, -jnp.inf),
                                  jnp.where(own_ok, s_own, -jnp.inf)], axis=-1)
        p = jax.nn.softmax(logits, axis=-1).astype(v.dtype)
        p_sel = p[..., :ks * MOBA_BLOCK].reshape(b, h, Q_CHUNK, ks, MOBA_BLOCK)
        p_own = p[..., ks * MOBA_BLOCK:]
        return (jnp.einsum("bhqnt,bhqntd->bhqd", p_sel, v_sel)
                + jnp.einsum("bhqt,bhtd->bhqd", p_own, v_own))

    out = lax.map(chunk_attend, (jnp.arange(n_chunks), q_c, sel_c))
    out = out.transpose(1, 2, 0, 3, 4).reshape(b, h, sp, dh)
    return out[:, :, :s]


def token_mixer(u, cos, sin, w_in, conv_dw_w, conv_dw_b, conv_ln_g, conv_ln_b,
                w_conv_out, b_conv_out, w_attn_out, w_out):
    b, s, _ = u.shape
    proj = u @ w_in
    glu_in, q, k, v, gate_logits = jnp.split(proj, IN_SPLITS, axis=-1)

    a_lin, a_gate = jnp.split(glu_in, 2, axis=-1)
    hc = a_lin * jax.nn.sigmoid(a_gate)
    hc = causal_depthwise_conv(hc, conv_dw_w, conv_dw_b)
    hc = jax.nn.silu(layer_norm(hc, conv_ln_g, conv_ln_b))
    y_conv = hc @ w_conv_out + b_conv_out

    q = apply_partial_rope(q.reshape(b, s, N_HEADS, HEAD_DIM), cos, sin).transpose(0, 2, 1, 3)
    k = apply_partial_rope(k.reshape(b, s, N_HEADS, HEAD_DIM), cos, sin).transpose(0, 2, 1, 3)
    v = v.reshape(b, s, N_HEADS, HEAD_DIM).transpose(0, 2, 1, 3)
    o = moba_attention(q, k, v).transpose(0, 2, 1, 3).reshape(b, s, ATTN_WIDTH)
    y_attn = o @ w_attn_out

    g_conv, g_attn = jnp.split(jax.nn.sigmoid(gate_logits), N_BRANCH, axis=-1)
    return (g_conv * y_conv + g_attn * y_attn) @ w_out


def conv_ffn(u, w_up, ffn_dw_w, ffn_dw_b, w_down):
    hid = u @ w_up
    a, val = jnp.split(hid, 2, axis=-1)
    a = causal_depthwise_conv(a, ffn_dw_w, ffn_dw_b)
    return (jax.nn.silu(a) * val) @ w_down


def setup_inputs(seed: int = 0) -> dict:
    key = jax.random.key(seed)
    ks = jax.random.split(key, 24)
    f32 = jnp.float32
    L = DEPTH

    def nrm(k, shape, scale):
        return jax.random.normal(k, shape, f32) * scale

    return {
        "x": nrm(ks[0], (BATCH, SEQ, D_MODEL), 1.0),
        "c": nrm(ks[1], (BATCH, D_MODEL), 1.0),
        "w_ada": nrm(ks[2], (L, D_MODEL, 6 * D_MODEL), 0.1 * D_MODEL ** -0.5),
        "b_ada": nrm(ks[3], (L, 6 * D_MODEL), 0.02),
        "w_in": nrm(ks[4], (L, D_MODEL, IN_COLS), D_MODEL ** -0.5),
        "conv_dw_w": nrm(ks[5], (L, CONV_KERNEL, CONV_CH), CONV_KERNEL ** -0.5),
        "conv_dw_b": nrm(ks[6], (L, CONV_CH), 0.02),
        "conv_ln_g": 1.0 + nrm(ks[7], (L, CONV_CH), 0.02),
        "conv_ln_b": nrm(ks[8], (L, CONV_CH), 0.02),
        "w_conv_out": nrm(ks[9], (L, CONV_CH, D_MODEL), CONV_CH ** -0.5),
        "b_conv_out": nrm(ks[10], (L, D_MODEL), 0.02),
        "w_attn_out": nrm(ks[11], (L, ATTN_WIDTH, D_MODEL), ATTN_WIDTH ** -0.5),
        "w_out": nrm(ks[12], (L, D_MODEL, D_MODEL), D_MODEL ** -0.5 * DEEPNORM_BETA),
        "ln1_g": 1.0 + nrm(ks[13], (L, D_MODEL), 0.02),
        "ln1_b": nrm(ks[14], (L, D_MODEL), 0.02),
        "w_up": nrm(ks[15], (L, D_MODEL, 2 * D_FF), D_MODEL ** -0.5),
        "ffn_dw_w": nrm(ks[16], (L, FFN_CONV, D_FF), FFN_CONV ** -0.5),
        "ffn_dw_b": nrm(ks[17], (L, D_FF), 0.02),
        "w_down": nrm(ks[18], (L, D_FF, D_MODEL), D_FF ** -0.5 * DEEPNORM_BETA),
        "ln2_g": 1.0 + nrm(ks[19], (L, D_MODEL), 0.02),
        "ln2_b": nrm(ks[20], (L, D_MODEL), 0.02),
    }


def reference(x, c, w_ada, b_ada, w_in, conv_dw_w, conv_dw_b, conv_ln_g, conv_ln_b,
              w_conv_out, b_conv_out, w_attn_out, w_out, ln1_g, ln1_b,
              w_up, ffn_dw_w, ffn_dw_b, w_down, ln2_g, ln2_b):
    cos, sin = rope_tables(x.shape[1], x.dtype)
    c_act = jax.nn.silu(c)
    for l in range(DEPTH):
        mod = c_act @ w_ada[l] + b_ada[l]
        shift1, scale1, gate1, shift2, scale2, gate2 = jnp.split(mod[:, None, :], 6, axis=-1)
        u = x * (1 + scale1) + shift1
        y = token_mixer(u, cos, sin, w_in[l], conv_dw_w[l], conv_dw_b[l], conv_ln_g[l], conv_ln_b[l],
                        w_conv_out[l], b_conv_out[l], w_attn_out[l], w_out[l])
        x = layer_norm(DEEPNORM_ALPHA * x + (1 + gate1) * y, ln1_g[l], ln1_b[l])
        u = x * (1 + scale2) + shift2
        y = conv_ffn(u, w_up[l], ffn_dw_w[l], ffn_dw_b[l], w_down[l])
        x = layer_norm(DEEPNORM_ALPHA * x + (1 + gate2) * y, ln2_g[l], ln2_b[l])
    return x
```

```python
import numpy as np
import ml_dtypes
from contextlib import ExitStack
import concourse.bass as bass
import concourse.mybir as mybir
from concourse.bass_utils import run_bass_kernel_spmd

F32 = mybir.dt.float32
BF16 = mybir.dt.bfloat16
AF = mybir.ActivationFunctionType
ALU = mybir.AluOpType
AX = mybir.AxisListType

D = 2048
SEQ = 16384
CH = 4096
HALO = 512
NOWN = CH + HALO
NALL = SEQ + HALO
NT_ALL = NALL // 512
NT_OWN = NOWN // 512
NBLK = NALL // 256
NKT = NALL // 128
DFF = 5632
ALPHA = 2.0 ** 0.25
EPS_LN = 1e-5
EPS_DN = 1e-5 / (ALPHA * ALPHA)
QSCALE = 128.0 ** -0.5
NEG = -30000.0
ENG = ("sync", "scalar", "vector", "gpsimd", "tensor")

O_LN1G, O_LN1B, O_LN2G, O_LN2B, O_BCO, O_CDB, O_CLG, O_CLB, O_CW, O_FW, O_FB, O_FLAG, NPV = \
    0, 16, 32, 48, 64, 80, 88, 96, 104, 352, 484, 528, 529
P_SC1, P_SH1, P_G1S, P_A2, P_B2, P_G2S, P_TMP, NPAR = 0, 16, 32, 48, 64, 80, 96, 128


class Sem:
    __slots__ = ("h", "v")


class Prog:
    def __init__(self, nc, stack):
        self.nc = nc
        self.stack = stack
        self.q = {e: [] for e in ENG}
        self.waited = {e: {} for e in ENG}
        self.pg = {}
        self.dma_out = []
        self.nsem = 0

    def sem(self, name):
        s = Sem()
        self.nsem += 1
        s.h = self.stack.enter_context(self.nc.semaphore(f"{name}_{self.nsem}"))
        s.v = 0
        return s

    def new_phase(self, tag):
        self.pg = {e: self.sem(f"pg{tag}{e[:2]}") for e in ("scalar", "vector", "gpsimd", "tensor")}
        self.waited = {e: {} for e in ENG}
        self.dma_out = []

    def _waits(self, eng, waits):
        for w in waits:
            if w is None:
                continue
            s, v = w
            if v <= 0 or self.waited[eng].get(id(s), 0) >= v:
                continue
            self.waited[eng][id(s)] = v
            self.q[eng].append(lambda e, s=s, v=v: e.wait_ge(s.h, v))

    def op(self, eng, fn, waits=(), mark=True):
        self._waits(eng, waits)
        if mark:
            s = self.pg[eng]
            s.v += 1
            self.q[eng].append(lambda e, fn=fn, s=s: fn(e).then_inc(s.h, 1))
            return (s, s.v)
        self.q[eng].append(lambda e, fn=fn: fn(e))
        return None

    def dma(self, eng, out, in_, sem, waits=(), **kw):
        self._waits(eng, waits)
        sem.v += 16
        self.q[eng].append(lambda e, out=out, in_=in_, sem=sem, kw=kw: e.dma_start(out=out, in_=in_, **kw).then_inc(sem.h, 16))
        tok = (sem, sem.v)
        self.dma_out.append(tok)
        return tok

    def barrier(self, tag):
        b = self.sem(f"bar{tag}")
        for e in ENG:
            if e in self.pg and self.pg[e].v > 0:
                self._waits(e, [(self.pg[e], self.pg[e].v)])
        last = {}
        for t in self.dma_out:
            last[id(t[0])] = t
        self._waits("sync", list(last.values()))
        for e in ENG:
            self.q[e].append(lambda en, b=b: en.sem_inc(b.h, 1))
        for e in ENG:
            self.q[e].append(lambda en, b=b: en.wait_ge(b.h, len(ENG)))

    def flush(self):
        with self.nc.Block() as block:
            for name in ENG:
                ops = self.q[name]
                if not ops:
                    continue

                def run(e, ops=ops):
                    for f in ops:
                        f(e)
                getattr(block, name)(run)
        self.q = {e: [] for e in ENG}


class Ring:
    def __init__(self, P, name, slots):
        self.slots = slots
        self.sems = [P.sem(f"{name}{i}") for i in range(len(slots))]
        self.free = [[] for _ in slots]
        self.i = 0

    def next(self):
        i = self.i % len(self.slots)
        self.i += 1
        return i


class BankRing:
    def __init__(self, banks):
        self.banks = list(banks)
        self.free = {b: [] for b in banks}
        self.i = 0

    def next(self):
        b = self.banks[self.i % len(self.banks)]
        self.i += 1
        return b


def build():
    nc = bass.Bass("TRN2", target_bir_lowering=False)

    def din(name, shape, dt):
        return nc.dram_tensor(name, shape, dt, kind="ExternalInput").ap()

    def dint(name, shape, dt):
        return nc.dram_tensor(name, shape, dt, kind="Internal").ap()

    xa = din("xa", [NALL, D], F32)
    cc = din("cc", [128, 16], F32)
    w_ada = din("w_ada", [D, 6 * D], F32)
    b_ada = din("b_ada", [1, 6 * D], F32)
    w_in = din("w_in", [D, 9216], F32)
    w_co = din("w_conv_out", [1024, D], F32)
    w_ao = din("w_attn_out", [1024, D], F32)
    w_o = din("w_out", [D, D], F32)
    w_up = din("w_up", [D, 2 * DFF], F32)
    w_dn = din("w_down", [DFF, D], F32)
    pvd = din("pv", [128, NPV], F32)
    ropeC = din("ropeC", [32, NALL], F32)
    ropeS = din("ropeS", [32, NALL], F32)
    pbd = din("pb", [128, 18 * NBLK], F32)
    vald = din("val", [128, 18 * NBLK], F32)
    ealld = din("eall", [NBLK, NBLK * 128], BF16)
    trid = din("tri", [128, 4 * 512], BF16)
    identd = din("ident", [128, 128], F32)
    identbd = din("identb", [128, 128], BF16)
    permd = din("perm", [32, 32], F32)
    outd = nc.dram_tensor("out", [CH, D], F32, kind="ExternalOutput").ap()

    ws_in = dint("ws_in", [72, 128, 16, 128], BF16)
    ws_co = dint("ws_co", [16, 128, 8, 128], BF16)
    ws_ao = dint("ws_ao", [16, 128, 8, 128], BF16)
    ws_o = dint("ws_o", [16, 128, 16, 128], BF16)
    ws_up = dint("ws_up", [88, 128, 16, 128], BF16)
    ws_dn = dint("ws_dn", [16, 128, 44, 128], BF16)
    ksc = dint("ksc", [8, 128, NALL], BF16)
    vsc = dint("vsc", [8, 128, NKT, 128], BF16)
    qsc = dint("qsc", [8, 128, NOWN], BF16)
    osc = dint("osc", [8, 128, NOWN], BF16)
    modsc = dint("modsc", [1, 6 * D], F32)

    with ExitStack() as gst:
        P = Prog(nc, gst)

        def gsb(name, shape, dt):
            return gst.enter_context(nc.sbuf_tensor(name, shape, dt))

        ps = gst.enter_context(nc.psum_tensor("ps", [128, 8, 512], F32))
        pv = gsb("pvs", [128, NPV], F32)
        par = gsb("par", [128, NPAR], F32)
        ident = gsb("ident_s", [128, 128], F32)
        identb = gsb("identb_s", [128, 128], BF16)
        onesb = gsb("onesb", [128, 128], BF16)
        ones1k = gsb("ones1k", [128, 128], F32)
        ones2k = gsb("ones2k", [128, 128], F32)
        perm = gsb("perm_s", [32, 32], F32)
        kmT = gsb("kmT", [128, 8, NBLK], BF16)

        with ExitStack() as st:
            def sb(name, shape, dt):
                return st.enter_context(nc.sbuf_tensor(name, shape, dt))
            cact = sb("cact", [128, 16], F32)
            wa = sb("wa", [128, 2, 4096], F32)
            bada = sb("bada", [1, 6 * D], F32)
            modrow = sb("modrow", [1, 4096], F32)
            modT = sb("modT", [128, 96], F32)
            stg = sb("stg", [128, 4, 1024], F32)
            stb = sb("stb", [128, 4, 1024], BF16)
            P.new_phase("a")
            sc = P.sem("ldc")
            P.dma("sync", pv[:], pvd, sc)
            P.dma("sync", ident[:], identd, sc)
            P.dma("sync", identb[:], identbd, sc)
            P.dma("sync", perm[:], permd, sc)
            P.dma("sync", cact[:], cc, sc)
            tconst = P.dma("sync", bada[:], b_ada, sc)
            P.op("vector", lambda e: e.memset(onesb[:], 1.0))
            P.op("vector", lambda e: e.memset(ones1k[:], 1.0 / 1024.0))
            P.op("vector", lambda e: e.memset(ones2k[:], 1.0 / 2048.0))
            tca = P.op("scalar", lambda e: e.activation(out=cact[:], in_=cact[:], func=AF.Silu), waits=[tconst])
            wring = Ring(P, "wa", [wa[:, 0, :], wa[:, 1, :]])
            smod = P.sem("modw")
            ev = None
            store = None
            for p in range(3):
                tk = None
                for k in range(16):
                    i = wring.next()
                    tl = P.dma("sync", wring.slots[i], w_ada[k * 128:(k + 1) * 128, p * 4096:(p + 1) * 4096],
                               wring.sems[i], waits=wring.free[i])
                    for b in range(8):
                        tk = P.op("tensor", lambda e, b=b, i=i, k=k: e.matmul(
                            ps[0:1, b, :], lhsT=cact[:, k:k + 1], rhs=wring.slots[i][:, b * 512:(b + 1) * 512],
                            start=(k == 0), stop=(k == 15)), waits=[tl, tca, ev], mark=(b == 7))
                    wring.free[i] = [tk]
                for b in range(8):
                    ev = P.op("vector", lambda e, b=b, p=p: e.tensor_tensor(
                        out=modrow[0:1, b * 512:(b + 1) * 512], in0=ps[0:1, b, :],
                        in1=bada[0:1, p * 4096 + b * 512:p * 4096 + (b + 1) * 512], op=ALU.add),
                        waits=[tk, tconst, store])
                tt_ = None
                for c in range(32):
                    tt_ = P.op("tensor", lambda e, c=c: e.transpose(ps[:, 7, c:c + 1], modrow[0:1, c * 128:(c + 1) * 128],
                                                                     ident[0:1, 0:1]), waits=[ev, tconst])
                ev = P.op("vector", lambda e, p=p: e.tensor_copy(out=modT[:, p * 32:(p + 1) * 32], in_=ps[:, 7, 0:32]), waits=[tt_])
                store = ev
            tmod = ev
            w0 = [tmod, tconst]
            P.op("vector", lambda e: e.tensor_scalar(out=par[:, P_SC1:P_SC1 + 16], in0=modT[:, 16:32], scalar1=1.0,
                                                     scalar2=None, op0=ALU.add), waits=w0)
            P.op("vector", lambda e: e.tensor_copy(out=par[:, P_SH1:P_SH1 + 16], in_=modT[:, 0:16]))
            P.op("vector", lambda e: e.tensor_scalar(out=par[:, P_G1S:P_G1S + 16], in0=modT[:, 32:48], scalar1=1.0,
                                                     scalar2=1.0 / ALPHA, op0=ALU.add, op1=ALU.mult))
            P.op("vector", lambda e: e.tensor_scalar(out=par[:, P_G2S:P_G2S + 16], in0=modT[:, 80:96], scalar1=1.0,
                                                     scalar2=1.0 / ALPHA, op0=ALU.add, op1=ALU.mult))
            t1 = P.op("vector", lambda e: e.tensor_scalar(out=par[:, P_TMP:P_TMP + 16], in0=modT[:, 64:80], scalar1=1.0,
                                                          scalar2=None, op0=ALU.add))
            t2 = P.op("vector", lambda e: e.tensor_tensor(out=par[:, P_A2:P_A2 + 16], in0=pv[:, O_LN1G:O_LN1G + 16],
                                                          in1=par[:, P_TMP:P_TMP + 16], op=ALU.mult), waits=[t1])
            t3 = P.op("vector", lambda e: e.tensor_tensor(out=par[:, P_B2:P_B2 + 16], in0=pv[:, O_LN1B:O_LN1B + 16],
                                                          in1=par[:, P_TMP:P_TMP + 16], op=ALU.mult), waits=[t1])
            P.op("vector", lambda e: e.tensor_tensor(out=par[:, P_B2:P_B2 + 16], in0=par[:, P_B2:P_B2 + 16],
                                                     in1=modT[:, 48:64], op=ALU.add), waits=[t3])
            sring = Ring(P, "stg", [stg[:, i, :] for i in range(4)])
            bring = Ring(P, "stb", [stb[:, i, :] for i in range(4)])
            ci = 0
            for (src, dst, K, N) in ((w_in, ws_in, D, 9216), (w_co, ws_co, 1024, D), (w_ao, ws_ao, 1024, D),
                                     (w_o, ws_o, D, D), (w_up, ws_up, D, 2 * DFF), (w_dn, ws_dn, DFF, D)):
                for kk in range(K // 128):
                    for cs in range(N // 1024):
                        i = sring.next()
                        j = bring.next()
                        tl = P.dma("sync", sring.slots[i], src[kk * 128:(kk + 1) * 128, cs * 1024:(cs + 1) * 1024],
                                   sring.sems[i], waits=sring.free[i])
                        eng = ("vector", "gpsimd")[ci % 2]
                        ci += 1
                        tcst = P.op(eng, lambda e, i=i, j=j: e.tensor_copy(out=bring.slots[j], in_=sring.slots[i]),
                                    waits=[tl] + bring.free[j])
                        sring.free[i] = [tcst]
                        ts = P.dma("scalar", dst[cs * 8:(cs + 1) * 8, :, kk, :].rearrange("c p d -> p c d"),
                                   bring.slots[j].rearrange("p (c d) -> p c d", d=128), bring.sems[j], waits=[tcst])
                        bring.free[j] = [ts]
            P.barrier("a")
            P.flush()

        with ExitStack() as st:
            def sb(name, shape, dt):
                return st.enter_context(nc.sbuf_tensor(name, shape, dt))
            wq = sb("wq", [128, 8, 16, 128], BF16)
            wk = sb("wk", [128, 8, 16, 128], BF16)
            wv = sb("wv", [128, 8, 16, 128], BF16)
            xin = sb("xin", [128, 4, D], F32)
            uT = sb("uT", [128, 2, 16, 512], BF16)
            kf = sb("kf", [128, 3, 512], F32)
            r1 = sb("r1", [32, 3, 512], F32)
            r2 = sb("r2", [32, 3, 512], F32)
            kbf = sb("kbf", [128, 4, 512], BF16)
            ctb = sb("ctb", [32, 2, 512], F32)
            stb2 = sb("stb2", [32, 2, 512], F32)
            vbf = sb("vbf", [128, 3, 1024], BF16)
            kms = sb("kms", [128, 8, NBLK], F32)
            P.new_phase("k")
            sw = P.sem("ldw")
            P.dma("sync", wq[:], ws_in[16:24].rearrange("c p k d -> p c k d"), sw)
            P.dma("sync", wk[:], ws_in[24:32].rearrange("c p k d -> p c k d"), sw)
            tw = P.dma("sync", wv[:], ws_in[32:40].rearrange("c p k d -> p c k d"), sw)
            xring = Ring(P, "xin", [xin[:, i, :] for i in range(4)])
            tpb = BankRing([0, 1, 2])
            prb = BankRing([3, 4, 5])
            swb = BankRing([6, 7])
            kfr = Ring(P, "kf", [0, 1, 2])
            kbr = Ring(P, "kbf", [kbf[:, i, :] for i in range(4)])
            vbr = Ring(P, "vbf", [vbf[:, i, :] for i in range(3)])
            ropr = Ring(P, "rope", [0, 1])
            uT_free = [[], []]
            evi = 0
            for t in range(NT_ALL):
                tok0 = t * 512
                u = t % 2
                xt = []
                for sub in range(4):
                    i = xring.next()
                    tl = P.dma("sync", xring.slots[i], xa[tok0 + sub * 128:tok0 + (sub + 1) * 128, :], xring.sems[i],
                               waits=xring.free[i])
                    xt.append((i, tl))
                ri = ropr.next()
                P.dma("gpsimd", ctb[:, ri, :], ropeC[:, tok0:tok0 + 512], ropr.sems[ri], waits=ropr.free[ri])
                trope = P.dma("gpsimd", stb2[:, ri, :], ropeS[:, tok0:tok0 + 512], ropr.sems[ri], waits=ropr.free[ri])
                uready = []
                tpe = None
                for c in range(16):
                    b = tpb.next()
                    for sub in range(4):
                        i, tl = xt[sub]
                        tpe = P.op("tensor", lambda e, b=b, sub=sub, i=i, c=c: e.transpose(
                            ps[:, b, sub * 128:(sub + 1) * 128], xin[:, i, c * 128:(c + 1) * 128], ident[:]),
                            waits=[tl] + tpb.free[b], mark=(sub == 3))
                    if evi % 2 == 0:
                        te = P.op("scalar", lambda e, b=b, c=c, u=u: e.activation(
                            out=uT[:, u, c, :], in_=ps[:, b, :], func=AF.Identity,
                            scale=par[:, P_SC1 + c:P_SC1 + c + 1], bias=par[:, P_SH1 + c:P_SH1 + c + 1]),
                            waits=[tpe] + uT_free[u])
                    else:
                        te = P.op("vector", lambda e, b=b, c=c, u=u: e.tensor_scalar(
                            out=uT[:, u, c, :], in0=ps[:, b, :], scalar1=par[:, P_SC1 + c:P_SC1 + c + 1],
                            scalar2=par[:, P_SH1 + c:P_SH1 + c + 1], op0=ALU.mult, op1=ALU.add),
                            waits=[tpe] + uT_free[u])
                    evi += 1
                    tpb.free[b] = [te]
                    uready = (uready + [te])[-2:]
                for (i, tl) in xt:
                    xring.free[i] = [tpe]
                last_rope = None
                for kind in (("k", "q") if t < NT_OWN else ("k",)):
                    wmat = wk if kind == "k" else wq
                    for h in range(8):
                        b = prb.next()
                        tpe = None
                        for k in range(16):
                            tpe = P.op("tensor", lambda e, b=b, h=h, k=k, u=u, wmat=wmat: e.matmul(
                                ps[:, b, :], lhsT=wmat[:, h, k, :], rhs=uT[:, u, k, :], start=(k == 0), stop=(k == 15)),
                                waits=uready + [tw] + prb.free[b], mark=(k == 15))
                        fi = kfr.next()
                        ta = P.op("scalar", lambda e, b=b, fi=fi: e.activation(out=kf[:, fi, :], in_=ps[:, b, :], func=AF.Copy),
                                  waits=[tpe] + kfr.free[fi])
                        prb.free[b] = [ta]
                        sbk = swb.next()
                        tsw = P.op("tensor", lambda e, sbk=sbk, fi=fi: e.matmul(
                            ps[0:32, sbk, :], lhsT=perm[:, :], rhs=kf[0:32, fi, :], start=True, stop=True),
                            waits=[ta] + swb.free[sbk])
                        ta1 = P.op("vector", lambda e, fi=fi, ri=ri: e.tensor_tensor(
                            out=r1[:, fi, :], in0=kf[0:32, fi, :], in1=ctb[:, ri, :], op=ALU.mult), waits=[ta, trope])
                        ta2 = P.op("vector", lambda e, fi=fi, ri=ri, sbk=sbk: e.tensor_tensor(
                            out=r2[:, fi, :], in0=ps[0:32, sbk, :], in1=stb2[:, ri, :], op=ALU.mult), waits=[tsw])
                        swb.free[sbk] = [ta2]
                        ta3 = P.op("vector", lambda e, fi=fi: e.tensor_tensor(
                            out=kf[0:32, fi, :], in0=r1[:, fi, :], in1=r2[:, fi, :], op=ALU.add), waits=[ta1, ta2, tsw])
                        last_rope = ta3
                        ki = kbr.next()
                        if kind == "k":
                            ta4 = P.op("vector", lambda e, fi=fi, h=h, t=t: e.tensor_reduce(
                                out=kms[:, h, 2 * t:2 * t + 2], in_=kf[:, fi, :].rearrange("p (a b) -> p a b", b=256),
                                axis=AX.X, op=ALU.add), waits=[ta3])
                            tcs = P.op("gpsimd", lambda e, fi=fi, ki=ki: e.tensor_copy(out=kbf[:, ki, :], in_=kf[:, fi, :]),
                                       waits=[ta3] + kbr.free[ki])
                            kfr.free[fi] = [tcs, ta4]
                            ts = P.dma("sync", ksc[h, :, tok0:tok0 + 512], kbf[:, ki, :], kbr.sems[ki], waits=[tcs])
                        else:
                            tcs = P.op("gpsimd", lambda e, fi=fi, ki=ki: e.tensor_scalar(
                                out=kbf[:, ki, :], in0=kf[:, fi, :], scalar1=QSCALE, scalar2=None, op0=ALU.mult),
                                waits=[ta3] + kbr.free[ki])
                            kfr.free[fi] = [tcs]
                            ts = P.dma("sync", qsc[h, :, tok0:tok0 + 512], kbf[:, ki, :], kbr.sems[ki], waits=[tcs])
                        kbr.free[ki] = [ts]
                ropr.free[ri] = [last_rope]
                for sub in range(4):
                    vi = vbr.next()
                    tvs = []
                    for half in range(2):
                        b = prb.next()
                        tpe = None
                        for k in range(16):
                            tpe = P.op("tensor", lambda e, b=b, k=k, u=u, sub=sub, half=half: e.matmul(
                                ps[:, b, :].rearrange("p (c d) -> p c d", d=128),
                                lhsT=uT[:, u, k, sub * 128:(sub + 1) * 128], rhs=wv[:, half * 4:(half + 1) * 4, k, :],
                                start=(k == 0), stop=(k == 15)), waits=uready + [tw] + prb.free[b], mark=(k == 15))
                        if half == 0:
                            te = P.op("scalar", lambda e, b=b, vi=vi: e.activation(
                                out=vbf[:, vi, 0:512], in_=ps[:, b, :], func=AF.Copy), waits=[tpe] + vbr.free[vi])
                        else:
                            te = P.op("vector", lambda e, b=b, vi=vi: e.tensor_copy(
                                out=vbf[:, vi, 512:1024], in_=ps[:, b, :]), waits=[tpe] + vbr.free[vi])
                        prb.free[b] = [te]
                        tvs.append(te)
                    ts = P.dma("scalar", vsc[:, :, t * 4 + sub, :].rearrange("h p d -> p h d"),
                               vbf[:, vi, :].rearrange("p (h d) -> p h d", d=128), vbr.sems[vi], waits=tvs)
                    vbr.free[vi] = [ts]
                uT_free[u] = [tpe]
            P.op("vector", lambda e: e.tensor_scalar(out=kmT[:], in0=kms[:], scalar1=1.0 / 256.0, scalar2=None, op0=ALU.mult),
                 waits=[(P.pg["vector"], P.pg["vector"].v)])
            P.barrier("k")
            P.flush()

        with ExitStack() as st:
            def sb(name, shape, dt):
                return st.enter_context(nc.sbuf_tensor(name, shape, dt))
            KT = sb("KT", [128, 2, NALL], BF16)
            VV = sb("VV", [128, 2, NKT, 128], BF16)
            eall = sb("eall_s", [NBLK, NBLK, 128], BF16)
            tri = sb("tri_s", [128, 4, 512], BF16)
            pbs = sb("pbs", [128, 18, NBLK], F32)
            vals = sb("vals", [128, 18, NBLK], F32)
            qT = sb("qT", [128, 2, 512], BF16)
            gm = sb("gm", [128, 2, NBLK], F32)
            ge = sb("ge", [128, 2, NBLK], F32)
            top8 = sb("top8", [128, 2, 8], F32)
            selT = sb("selT", [NBLK, 2, 512], BF16)
            pT = sb("pT", [128, 4, 512], BF16)
            rinv = sb("rinv", [128, 512], F32)
            oT = sb("oT", [128, 2, 512], BF16)
            P.new_phase("A")
            sc = P.sem("ldA")
            P.dma("sync", eall[:], ealld.rearrange("p (j t) -> p j t", t=128), sc)
            P.dma("sync", tri[:], trid.rearrange("p (j t) -> p j t", t=512), sc)
            P.dma("sync", pbs[:], pbd.rearrange("p (j t) -> p j t", t=NBLK), sc)
            tcon = P.dma("sync", vals[:], vald.rearrange("p (j t) -> p j t", t=NBLK), sc)
            kvr = Ring(P, "kv", [0, 1])
            qr = Ring(P, "qT", [0, 1])
            sbk = BankRing([0, 1, 2])
            ptr = Ring(P, "pT", [0, 1, 2, 3])
            orr = Ring(P, "oT", [0, 1])
            O_B, R_B, G_B, ST_B = 3, 4, 5, 6
            or_free = []
            g_free = []
            st_free = []
            selT_free = [[], []]
            ge_free = [[], []]
            gi = 0
            qi_glob = 0
            for h in range(8):
                kv = kvr.next()
                for pc in range(4):
                    a0, a1 = pc * (NALL // 4), (pc + 1) * (NALL // 4)
                    P.dma("gpsimd", KT[:, kv, a0:a1], ksc[h, :, a0:a1], kvr.sems[kv], waits=kvr.free[kv])
                for pc in range(4):
                    a0, a1 = pc * (NKT // 4), (pc + 1) * (NKT // 4)
                    tkv = P.dma("gpsimd", VV[:, kv, a0:a1, :], vsc[h, :, a0:a1, :], kvr.sems[kv], waits=kvr.free[kv])
                last_pe = None
                for mt in range(NT_OWN):
                    qi = qr.next()
                    tq = P.dma("sync", qT[:, qi, :], qsc[h, :, mt * 512:(mt + 1) * 512], qr.sems[qi], waits=qr.free[qi])
                    sp = qi_glob % 2
                    qi_glob += 1
                    tts = []
                    for sub in range(4):
                        row = 2 * mt + sub // 2
                        g = gi % 2
                        gi += 1
                        tg = P.op("tensor", lambda e, qi=qi, sub=sub, h=h: e.matmul(
                            ps[:, G_B, sub * 128:sub * 128 + NBLK], lhsT=qT[:, qi, sub * 128:(sub + 1) * 128],
                            rhs=kmT[:, h, :], start=True, stop=True), waits=[tq] + g_free)
                        d1 = P.op("vector", lambda e, g=g, sub=sub, row=row: e.tensor_tensor(
                            out=gm[:, g, :], in0=ps[:, G_B, sub * 128:sub * 128 + NBLK], in1=pbs[:, row, :], op=ALU.add),
                            waits=[tg, tcon])
                        if sub == 3:
                            g_free = [d1]
                        d2 = P.op("vector", lambda e, g=g: e.max(out=top8[:, g, :], in_=gm[:, g, :]), waits=[d1])
                        d3 = P.op("vector", lambda e, g=g: e.tensor_scalar(
                            out=ge[:, g, :], in0=gm[:, g, :], scalar1=top8[:, g, 2:3], scalar2=None, op0=ALU.is_ge),
                            waits=[d2] + ge_free[g])
                        d4 = P.op("vector", lambda e, g=g, row=row: e.tensor_tensor(
                            out=ge[:, g, :], in0=ge[:, g, :], in1=vals[:, row, :], op=ALU.mult), waits=[d3])
                        d5 = P.op("vector", lambda e, g=g: e.tensor_scalar(
                            out=ge[:, g, :], in0=ge[:, g, :], scalar1=-NEG, scalar2=NEG, op0=ALU.mult, op1=ALU.add),
                            waits=[d4])
                        d6 = P.op("vector", lambda e, g=g, row=row: e.memset(ge[:, g, row:row + 1], 0.0), waits=[d5])
                        tt = P.op("tensor", lambda e, g=g, sub=sub: e.transpose(
                            ps[0:NBLK, ST_B, sub * 128:(sub + 1) * 128], ge[:, g, :], ident[:]), waits=[d6] + st_free)
                        ge_free[g] = [tt]
                        tts.append(tt)
                    tsel = P.op("scalar", lambda e, sp=sp: e.activation(out=selT[:, sp, :], in_=ps[0:NBLK, ST_B, :], func=AF.Copy),
                                waits=tts + selT_free[sp])
                    st_free = [tsel]
                    own = list(range(0, 2)) if mt == 0 else list(range(2, 2 * mt + 2))
                    blocks = own + list(range(18, NBLK))
                    tiles = [(jj, k2) for jj in blocks for k2 in range(2)]
                    n = len(tiles)
                    Dp = 2
                    stoks = [None] * n
                    ptoks = [None] * n
                    pslot = [None] * n
                    sbank = [None] * n
                    for it in range(n + Dp):
                        if it < n:
                            jj, k2 = tiles[it]
                            kti = jj * 2 + k2
                            b = sbk.next()
                            sbank[it] = b
                            diag = (jj == 2 * mt) or (jj == 2 * mt + 1)
                            P.op("tensor", lambda e, b=b, kv=kv, kti=kti, qi=qi: e.matmul(
                                ps[:, b, :], lhsT=KT[:, kv, kti * 128:(kti + 1) * 128], rhs=qT[:, qi, :], start=True, stop=False),
                                waits=[tq, tkv] + sbk.free[b], mark=False)
                            tpe = P.op("tensor", lambda e, b=b, jj=jj, sp=sp, diag=diag: e.matmul(
                                ps[:, b, :], lhsT=eall[:, jj, :], rhs=selT[:, sp, :], start=False, stop=(not diag)),
                                waits=[tsel, tcon], mark=(not diag))
                            if diag:
                                tix = (jj - 2 * mt) * 2 + k2
                                tpe = P.op("tensor", lambda e, b=b, tix=tix: e.matmul(
                                    ps[:, b, :], lhsT=identb[:, :], rhs=tri[:, tix, :], start=False, stop=True))
                            stoks[it] = tpe
                            pi = ptr.next()
                            pslot[it] = pi
                            tex = P.op("scalar", lambda e, b=b, pi=pi: e.activation(out=pT[:, pi, :], in_=ps[:, b, :], func=AF.Exp),
                                       waits=[tpe] + ptr.free[pi])
                            sbk.free[b] = [tex]
                            ptoks[it] = tex
                        j = it - Dp
                        if j >= 0:
                            jj, k2 = tiles[j]
                            kti = jj * 2 + k2
                            pi = pslot[j]
                            P.op("tensor", lambda e, kv=kv, kti=kti, pi=pi, j=j, n=n: e.matmul(
                                ps[:, O_B, :], lhsT=VV[:, kv, kti, :], rhs=pT[:, pi, :], start=(j == 0), stop=(j == n - 1)),
                                waits=[ptoks[j]] + (or_free if j == 0 else []), mark=False)
                            tpv = P.op("tensor", lambda e, pi=pi, j=j, n=n: e.matmul(
                                ps[:, R_B, :], lhsT=onesb[:, :], rhs=pT[:, pi, :], start=(j == 0), stop=(j == n - 1)))
                            ptr.free[pi] = [tpv]
                            last_pe = tpv
                    qr.free[qi] = [last_pe]
                    selT_free[sp] = [last_pe]
                    oi = orr.next()
                    n1 = P.op("vector", lambda e: e.reciprocal(out=rinv[:], in_=ps[:, R_B, :]), waits=[last_pe])
                    n2 = P.op("vector", lambda e, oi=oi: e.tensor_tensor(out=oT[:, oi, :], in0=ps[:, O_B, :], in1=rinv[:], op=ALU.mult),
                              waits=[n1] + orr.free[oi])
                    or_free = [n2]
                    ts = P.dma("sync", osc[h, :, mt * 512:(mt + 1) * 512], oT[:, oi, :], orr.sems[oi], waits=[n2])
                    orr.free[oi] = [ts]
                kvr.free[kv] = [last_pe]
            P.barrier("A")
            P.flush()

        with ExitStack() as st:
            def sb(name, shape, dt):
                return st.enter_context(nc.sbuf_tensor(name, shape, dt))
            xin = sb("xin2", [128, 2, D], F32)
            ost = sb("ost", [128, D], F32)
            xT = sb("xT", [128, 16, 512], F32)
            uT = sb("uT2", [128, 16, 512], BF16)
            big = sb("big", [128, 44, 512], BF16)
            hcb = sb("hcb", [128, 2, 542], F32)
            hhist = sb("hhist", [128, 8, 30], F32)
            cv = sb("cv", [128, 8, 512], F32)
            tmp = sb("tmp", [128, 4, 512], F32)
            stat = sb("stat", [128, 4, 512], F32)
            abuf = sb("abuf", [128, 2, 514], F32)
            ahist = sb("ahist", [128, 44, 2], F32)
            wsl = sb("wsl", [128, 6, 16, 128], BF16)
            P.new_phase("M")
            NSL = 6
            wsems = [P.sem(f"ws{i}") for i in range(NSL)]
            wfree = [[] for _ in range(NSL)]
            reqs = []
            for t in range(NT_OWN):
                for i in range(8):
                    reqs.append((ws_in, i, 0, 16))
                    reqs.append((ws_in, 8 + i, 0, 16))
                for dc in range(16):
                    reqs.append((ws_co, dc, 0, 8))
                    reqs.append((ws_ao, dc, 0, 8))
                    reqs.append((ws_in, 40 + dc, 0, 16))
                    reqs.append((ws_in, 56 + dc, 0, 16))
                for dc in range(16):
                    reqs.append((ws_o, dc, 0, 16))
                for j in range(44):
                    reqs.append((ws_up, j, 0, 16))
                    reqs.append((ws_up, 44 + j, 0, 16))
                for dc in (range(16) if t > 0 else ()):
                    reqs.append((ws_dn, dc, 0, 16))
                    reqs.append((ws_dn, dc, 16, 16))
                    reqs.append((ws_dn, dc, 32, 12))
            wstate = {"issued": 0, "next": 0, "toks": {}}

            def wget():
                i = wstate["next"]
                wstate["next"] += 1
                lim = min(len(reqs), i + NSL)
                while wstate["issued"] < lim:
                    j = wstate["issued"]
                    dst, c, k0, nk = reqs[j]
                    s = j % NSL
                    wstate["toks"][j] = P.dma("sync", wsl[:, s, 0:nk, :], dst[c, :, k0:k0 + nk, :], wsems[s], waits=wfree[s])
                    wstate["issued"] += 1
                return i % NSL, wstate["toks"].pop(i)

            mmb = BankRing([0, 1, 2, 3, 4, 5])
            tmr = Ring(P, "tmp", [0, 1, 2, 3])
            xring = Ring(P, "xin2", [0, 1])
            hcr = Ring(P, "hcb", [0, 1])
            abr = Ring(P, "abuf", [0, 1])
            s_o = P.sem("ldo")
            s_out = P.sem("stout")
            ost_free = []
            flag = pv[:, O_FLAG:O_FLAG + 1]
            vtok = lambda: (P.pg["vector"], P.pg["vector"].v)
            atok = lambda: (P.pg["scalar"], P.pg["scalar"].v)
            ptok = lambda: (P.pg["tensor"], P.pg["tensor"].v)
            P.op("vector", lambda e: e.memset(hhist[:], 0.0))
            P.op("vector", lambda e: e.memset(ahist[:], 0.0))
            P.op("vector", lambda e: e.memset(hcb[:], 0.0))
            P.op("vector", lambda e: e.memset(abuf[:], 0.0))
            evi = 0

            def group(nk, lhs_fn, rhs_fn, extra_waits):
                b = mmb.next()
                tpe = None
                for k in range(nk):
                    tpe = P.op("tensor", lambda e, b=b, k=k: e.matmul(ps[:, b, :], lhsT=lhs_fn(k), rhs=rhs_fn(k),
                                                                     start=(k == 0), stop=(k == nk - 1)),
                               waits=list(extra_waits) + mmb.free[b], mark=(k == nk - 1))
                return b, tpe

            def wgroup(nk, rhs_fn, extra_waits):
                s, tl = wget()
                b, tpe = group(nk, lambda k, s=s: wsl[:, s, k, :], rhs_fn, list(extra_waits) + [tl])
                wfree[s] = [tpe]
                return b, tpe

            def ln_stats(nch, src_fn, ones_t, eps, waits):
                MB, QB = 6, 7
                tm = None
                tq = None
                for c in range(nch):
                    ti = tmr.next()
                    tsq = P.op("scalar", lambda e, c=c, ti=ti: e.activation(out=tmp[:, ti, :], in_=src_fn(c), func=AF.Square),
                               waits=list(waits) + tmr.free[ti])
                    tm = P.op("tensor", lambda e, c=c: e.matmul(ps[:, MB, :], lhsT=ones_t[:, :], rhs=src_fn(c),
                                                                start=(c == 0), stop=(c == nch - 1)), waits=list(waits))
                    tq = P.op("tensor", lambda e, c=c, ti=ti: e.matmul(ps[:, QB, :], lhsT=ones_t[:, :], rhs=tmp[:, ti, :],
                                                                      start=(c == 0), stop=(c == nch - 1)), waits=[tsq])
                    tmr.free[ti] = [tq]
                a = P.op("vector", lambda e: e.tensor_copy(out=stat[:, 0, :], in_=ps[:, MB, :]), waits=[tm, tq])
                a = P.op("vector", lambda e: e.tensor_tensor(out=stat[:, 2, :], in0=stat[:, 0, :], in1=stat[:, 0, :], op=ALU.mult), waits=[a])
                a = P.op("vector", lambda e: e.tensor_tensor(out=stat[:, 2, :], in0=ps[:, QB, :], in1=stat[:, 2, :], op=ALU.subtract), waits=[a])
                a = P.op("vector", lambda e: e.tensor_scalar(out=stat[:, 2, :], in0=stat[:, 2, :], scalar1=0.0, scalar2=eps,
                                                             op0=ALU.max, op1=ALU.add), waits=[a])
                a2 = P.op("scalar", lambda e: e.activation(out=stat[:, 2, :], in_=stat[:, 2, :], func=AF.Sqrt), waits=[a])
                a = P.op("vector", lambda e: e.reciprocal(out=stat[:, 1, :], in_=stat[:, 2, :]), waits=[a2])
                return a

            for t in range(NT_OWN):
                tok0 = t * 512
                big_w = [ptok(), vtok(), atok()]
                to = None
                for h in range(8):
                    to = P.dma("gpsimd", big[:, 24 + h, :], osc[h, :, tok0:tok0 + 512], s_o, waits=big_w)
                xw = [ptok(), vtok(), atok()]
                for sub in range(4):
                    i = xring.next()
                    tl = P.dma("sync", xin[:, i, :], xa[tok0 + sub * 128:tok0 + (sub + 1) * 128, :], xring.sems[i],
                               waits=xring.free[i])
                    tpe = None
                    for c4 in range(4):
                        b = mmb.next()
                        for cq in range(4):
                            c = c4 * 4 + cq
                            tpe = P.op("tensor", lambda e, b=b, cq=cq, i=i, c=c: e.transpose(
                                ps[:, b, cq * 128:(cq + 1) * 128], xin[:, i, c * 128:(c + 1) * 128], ident[:]),
                                waits=[tl] + mmb.free[b], mark=(cq == 3))
                        te = P.op("vector", lambda e, b=b, c4=c4, sub=sub: e.tensor_copy(
                            out=xT[:, c4 * 4:(c4 + 1) * 4, sub * 128:(sub + 1) * 128],
                            in_=ps[:, b, :].rearrange("p (c d) -> p c d", d=128)), waits=[tpe] + xw)
                        mmb.free[b] = [te]
                    xring.free[i] = [tpe]
                tx = vtok()
                for c in range(16):
                    if c % 2 == 0:
                        P.op("scalar", lambda e, c=c: e.activation(out=uT[:, c, :], in_=xT[:, c, :], func=AF.Identity,
                                                                   scale=par[:, P_SC1 + c:P_SC1 + c + 1],
                                                                   bias=par[:, P_SH1 + c:P_SH1 + c + 1]), waits=[tx] + xw)
                    else:
                        P.op("vector", lambda e, c=c: e.tensor_scalar(out=uT[:, c, :], in0=xT[:, c, :],
                                                                      scalar1=par[:, P_SC1 + c:P_SC1 + c + 1],
                                                                      scalar2=par[:, P_SH1 + c:P_SH1 + c + 1],
                                                                      op0=ALU.mult, op1=ALU.add), waits=[tx] + xw)
                tu = [vtok(), atok()]
                for i in range(8):
                    bl, tl_ = wgroup(16, lambda k: uT[:, k, :], tu)
                    bg, tg_ = wgroup(16, lambda k: uT[:, k, :], tu)
                    ti = tmr.next()
                    hi = hcr.next()
                    s1 = P.op("scalar", lambda e, bg=bg, ti=ti: e.activation(out=tmp[:, ti, :], in_=ps[:, bg, :], func=AF.Sigmoid),
                              waits=[tg_] + tmr.free[ti])
                    mmb.free[bg] = [s1]
                    h0 = P.op("vector", lambda e, hi=hi, i=i: e.tensor_copy(out=hcb[:, hi, 0:30], in_=hhist[:, i, :]),
                              waits=hcr.free[hi])
                    h1 = P.op("vector", lambda e, bl=bl, ti=ti, hi=hi: e.tensor_tensor(
                        out=hcb[:, hi, 30:542], in0=ps[:, bl, :], in1=tmp[:, ti, :], op=ALU.mult), waits=[tl_, s1, h0])
                    mmb.free[bl] = [h1]
                    tmr.free[ti] = [h1]
                    h2 = P.op("vector", lambda e, hi=hi, i=i: e.tensor_copy(out=hhist[:, i, :], in_=hcb[:, hi, 512:542]), waits=[h1])
                    if t == 0:
                        h2 = P.op("vector", lambda e, i=i: e.tensor_scalar(out=hhist[:, i, :], in0=hhist[:, i, :], scalar1=flag,
                                                                           scalar2=None, op0=ALU.mult), waits=[h2])
                    a = P.op("vector", lambda e, hi=hi, i=i: e.tensor_scalar(
                        out=cv[:, i, :], in0=hcb[:, hi, 0:512], scalar1=pv[:, O_CW + i * 31:O_CW + i * 31 + 1],
                        scalar2=pv[:, O_CDB + i:O_CDB + i + 1], op0=ALU.mult, op1=ALU.add), waits=[h1, h2] + big_w)
                    for j in range(1, 31):
                        a = P.op("vector", lambda e, hi=hi, i=i, j=j: e.scalar_tensor_tensor(
                            out=cv[:, i, :], in0=hcb[:, hi, j:j + 512], scalar=pv[:, O_CW + i * 31 + j:O_CW + i * 31 + j + 1],
                            in1=cv[:, i, :], op0=ALU.mult, op1=ALU.add), waits=[a])
                    hcr.free[hi] = [a]
                tcv = vtok()
                tst = ln_stats(8, lambda c: cv[:, c, :], ones1k, EPS_LN, [tcv])
                for i in range(8):
                    a = P.op("vector", lambda e, i=i: e.tensor_tensor(out=cv[:, i, :], in0=cv[:, i, :], in1=stat[:, 0, :], op=ALU.subtract),
                             waits=[tst, ptok()])
                    a = P.op("vector", lambda e, i=i: e.tensor_tensor(out=cv[:, i, :], in0=cv[:, i, :], in1=stat[:, 1, :], op=ALU.mult), waits=[a])
                    P.op("scalar", lambda e, i=i: e.activation(out=big[:, i, :], in_=cv[:, i, :], func=AF.Silu,
                                                               scale=pv[:, O_CLG + i:O_CLG + i + 1], bias=pv[:, O_CLB + i:O_CLB + i + 1]),
                         waits=[a] + big_w)
                thn = atok()
                for dc in range(16):
                    b1, t1_ = wgroup(8, lambda k: big[:, k, :], [thn])
                    b2, t2_ = wgroup(8, lambda k: big[:, 24 + k, :], [to])
                    b3, t3_ = wgroup(16, lambda k: uT[:, k, :], tu)
                    b4, t4_ = wgroup(16, lambda k: uT[:, k, :], tu)
                    i3 = tmr.next()
                    s3 = P.op("scalar", lambda e, b3=b3, i3=i3: e.activation(out=tmp[:, i3, :], in_=ps[:, b3, :], func=AF.Sigmoid),
                              waits=[t3_] + tmr.free[i3])
                    mmb.free[b3] = [s3]
                    i4 = tmr.next()
                    s4 = P.op("scalar", lambda e, b4=b4, i4=i4: e.activation(out=tmp[:, i4, :], in_=ps[:, b4, :], func=AF.Sigmoid),
                              waits=[t4_] + tmr.free[i4])
                    mmb.free[b4] = [s4]
                    a1 = P.op("vector", lambda e, b1=b1, i3=i3, dc=dc: e.scalar_tensor_tensor(
                        out=tmp[:, i3, :], in0=ps[:, b1, :], scalar=pv[:, O_BCO + dc:O_BCO + dc + 1], in1=tmp[:, i3, :],
                        op0=ALU.add, op1=ALU.mult), waits=[t1_, s3])
                    mmb.free[b1] = [a1]
                    a2 = P.op("vector", lambda e, b2=b2, i4=i4: e.tensor_tensor(
                        out=tmp[:, i4, :], in0=ps[:, b2, :], in1=tmp[:, i4, :], op=ALU.mult), waits=[t2_, s4])
                    mmb.free[b2] = [a2]
                    a3 = P.op("vector", lambda e, i3=i3, i4=i4, dc=dc: e.tensor_tensor(
                        out=big[:, 8 + dc, :], in0=tmp[:, i3, :], in1=tmp[:, i4, :], op=ALU.add), waits=[a1, a2] + big_w)
                    tmr.free[i3] = [a3]
                    tmr.free[i4] = [a3]
                tm_ = vtok()
                for dc in range(16):
                    b, tp_ = wgroup(16, lambda k: big[:, 8 + k, :], [tm_])
                    a = P.op("vector", lambda e, b=b, dc=dc: e.scalar_tensor_tensor(
                        out=xT[:, dc, :], in0=ps[:, b, :], scalar=par[:, P_G1S + dc:P_G1S + dc + 1], in1=xT[:, dc, :],
                        op0=ALU.mult, op1=ALU.add), waits=[tp_, tx, atok()])
                    mmb.free[b] = [a]
                tz = vtok()
                tst = ln_stats(16, lambda c: xT[:, c, :], ones2k, EPS_DN, [tz])
                for c in range(16):
                    a = P.op("vector", lambda e, c=c: e.tensor_tensor(out=xT[:, c, :], in0=xT[:, c, :], in1=stat[:, 0, :], op=ALU.subtract),
                             waits=[tst, ptok()])
                    a = P.op("vector", lambda e, c=c: e.tensor_tensor(out=xT[:, c, :], in0=xT[:, c, :], in1=stat[:, 1, :], op=ALU.mult), waits=[a])
                    a5 = P.op("scalar", lambda e, c=c: e.activation(out=uT[:, c, :], in_=xT[:, c, :], func=AF.Identity,
                                                                    scale=par[:, P_A2 + c:P_A2 + c + 1], bias=par[:, P_B2 + c:P_B2 + c + 1]),
                              waits=[a, ptok()])
                    P.op("vector", lambda e, c=c: e.tensor_scalar(out=xT[:, c, :], in0=xT[:, c, :], scalar1=pv[:, O_LN1G + c:O_LN1G + c + 1],
                                                                  scalar2=pv[:, O_LN1B + c:O_LN1B + c + 1], op0=ALU.mult, op1=ALU.add),
                         waits=[a, a5])
                tu2 = [vtok(), atok()]
                for j in range(44):
                    ba, ta_ = wgroup(16, lambda k: uT[:, k, :], tu2)
                    bv, tv_ = wgroup(16, lambda k: uT[:, k, :], tu2)
                    ai = abr.next()
                    c0 = P.op("vector", lambda e, ai=ai, j=j: e.tensor_copy(out=abuf[:, ai, 0:2], in_=ahist[:, j, :]), waits=abr.free[ai])
                    c1 = P.op("scalar", lambda e, ba=ba, ai=ai: e.activation(out=abuf[:, ai, 2:514], in_=ps[:, ba, :], func=AF.Copy),
                              waits=[ta_] + abr.free[ai])
                    mmb.free[ba] = [c1]
                    c2 = P.op("vector", lambda e, ai=ai, j=j: e.tensor_copy(out=ahist[:, j, :], in_=abuf[:, ai, 512:514]), waits=[c1, c0])
                    if t == 0:
                        c2 = P.op("vector", lambda e, j=j: e.tensor_scalar(out=ahist[:, j, :], in0=ahist[:, j, :], scalar1=flag,
                                                                           scalar2=None, op0=ALU.mult), waits=[c2])
                    ti = tmr.next()
                    a = P.op("vector", lambda e, ai=ai, ti=ti, j=j: e.tensor_scalar(
                        out=tmp[:, ti, :], in0=abuf[:, ai, 0:512], scalar1=pv[:, O_FW + 3 * j:O_FW + 3 * j + 1],
                        scalar2=pv[:, O_FB + j:O_FB + j + 1], op0=ALU.mult, op1=ALU.add), waits=[c0, c1, c2] + tmr.free[ti])
                    a = P.op("vector", lambda e, ai=ai, ti=ti, j=j: e.scalar_tensor_tensor(
                        out=tmp[:, ti, :], in0=abuf[:, ai, 1:513], scalar=pv[:, O_FW + 3 * j + 1:O_FW + 3 * j + 2], in1=tmp[:, ti, :],
                        op0=ALU.mult, op1=ALU.add), waits=[a])
                    a = P.op("vector", lambda e, ai=ai, ti=ti, j=j: e.scalar_tensor_tensor(
                        out=tmp[:, ti, :], in0=abuf[:, ai, 2:514], scalar=pv[:, O_FW + 3 * j + 2:O_FW + 3 * j + 3], in1=tmp[:, ti, :],
                        op0=ALU.mult, op1=ALU.add), waits=[a])
                    abr.free[ai] = [a]
                    s_ = P.op("scalar", lambda e, ti=ti: e.activation(out=tmp[:, ti, :], in_=tmp[:, ti, :], func=AF.Silu), waits=[a])
                    hh = P.op("vector", lambda e, bv=bv, ti=ti, j=j: e.tensor_tensor(
                        out=big[:, j, :], in0=ps[:, bv, :], in1=tmp[:, ti, :], op=ALU.mult), waits=[tv_, s_, tm_, ptok()] if j < 32 else [tv_, s_])
                    mmb.free[bv] = [hh]
                    tmr.free[ti] = [hh]
                th = vtok()
                if t == 0:
                    continue
                for dc in range(16):
                    b = mmb.next()
                    tpe = None
                    for part, (k0, nk) in enumerate(((0, 16), (16, 16), (32, 12))):
                        s, tl = wget()
                        for k in range(nk):
                            tpe = P.op("tensor", lambda e, b=b, s=s, k=k, k0=k0, part=part, nk=nk: e.matmul(
                                ps[:, b, :], lhsT=wsl[:, s, k, :], rhs=big[:, k0 + k, :],
                                start=(part == 0 and k == 0), stop=(part == 2 and k == nk - 1)),
                                waits=[tl, th] + mmb.free[b], mark=(k == nk - 1))
                        wfree[s] = [tpe]
                    a = P.op("vector", lambda e, b=b, dc=dc: e.scalar_tensor_tensor(
                        out=xT[:, dc, :], in0=ps[:, b, :], scalar=par[:, P_G2S + dc:P_G2S + dc + 1], in1=xT[:, dc, :],
                        op0=ALU.mult, op1=ALU.add), waits=[tpe])
                    mmb.free[b] = [a]
                tz = vtok()
                tst = ln_stats(16, lambda c: xT[:, c, :], ones2k, EPS_DN, [tz])
                for c in range(16):
                    a = P.op("vector", lambda e, c=c: e.tensor_tensor(out=xT[:, c, :], in0=xT[:, c, :], in1=stat[:, 0, :], op=ALU.subtract),
                             waits=[tst, ptok()])
                    a = P.op("vector", lambda e, c=c: e.tensor_tensor(out=xT[:, c, :], in0=xT[:, c, :], in1=stat[:, 1, :], op=ALU.mult), waits=[a])
                    P.op("scalar", lambda e, c=c: e.activation(out=xT[:, c, :], in_=xT[:, c, :], func=AF.Identity,
                                                               scale=pv[:, O_LN2G + c:O_LN2G + c + 1], bias=pv[:, O_LN2B + c:O_LN2B + c + 1]),
                         waits=[a])
                tfin = atok()
                for sub in range(4):
                    for c4 in range(4):
                        b = mmb.next()
                        tpe = None
                        for cq in range(4):
                            c = c4 * 4 + cq
                            tpe = P.op("tensor", lambda e, b=b, cq=cq, c=c, sub=sub: e.transpose(
                                ps[:, b, cq * 128:(cq + 1) * 128], xT[:, c, sub * 128:(sub + 1) * 128], ident[:]),
                                waits=[tfin] + mmb.free[b], mark=(cq == 3))
                        if c4 % 2 == 0:
                            te = P.op("vector", lambda e, b=b, c4=c4: e.tensor_copy(out=ost[:, c4 * 512:(c4 + 1) * 512], in_=ps[:, b, :]),
                                      waits=[tpe] + ost_free)
                        else:
                            te = P.op("scalar", lambda e, b=b, c4=c4: e.activation(out=ost[:, c4 * 512:(c4 + 1) * 512], in_=ps[:, b, :], func=AF.Copy),
                                      waits=[tpe] + ost_free)
                        mmb.free[b] = [te]
                    r0 = (t - 1) * 512 + sub * 128
                    ts = P.dma("sync", outd[r0:r0 + 128, :], ost[:, :], s_out, waits=[vtok(), atok()])
                    ost_free = [ts]
            P.barrier("M")
            P.flush()
    return nc


def _bf16(a):
    return np.asarray(a, dtype=np.float32).astype(ml_dtypes.bfloat16)


def _host_consts():
    ident = np.eye(128, dtype=np.float32)
    perm = np.zeros((32, 32), np.float32)
    for m in range(32):
        perm[(m + 16) % 32, m] = 1.0
    eall = np.zeros((NBLK, NBLK, 128), np.float32)
    for j in range(NBLK):
        eall[j, j, :] = 1.0
    tri = np.zeros((128, 4, 512), np.float32)
    tt = np.arange(128)[:, None]
    for half in range(2):
        for k2 in range(2):
            ql = np.arange(256)[None, :]
            m = (k2 * 128 + tt) > ql
            tri[:, half * 2 + k2, half * 256:(half + 1) * 256] = np.where(m, NEG, 0.0)
    return ident, perm, _bf16(eall.reshape(NBLK, NBLK * 128)), _bf16(tri.reshape(128, 4 * 512))


def kernel(x, c, w_ada, b_ada, w_in, conv_dw_w, conv_dw_b, conv_ln_g, conv_ln_b, w_conv_out, b_conv_out, w_attn_out,
           w_out, ln1_g, ln1_b, w_up, ffn_dw_w, ffn_dw_b, w_down, ln2_g, ln2_b):
    f32 = np.float32
    x = np.asarray(x, f32)
    c = np.asarray(c, f32)
    ident, perm, eall, tri = _host_consts()

    def fm(v, nch):
        return np.ascontiguousarray(np.asarray(v, f32).reshape(nch, 128).T)

    L = 0
    cw = np.asarray(conv_dw_w, f32)[L]
    fw = np.asarray(ffn_dw_w, f32)[L]
    pv_base = np.concatenate([
        fm(ln1_g[L], 16), fm(ln1_b[L], 16), fm(ln2_g[L], 16), fm(ln2_b[L], 16), fm(b_conv_out[L], 16),
        fm(conv_dw_b[L], 8), fm(conv_ln_g[L], 8), fm(conv_ln_b[L], 8),
        np.ascontiguousarray(cw.reshape(31, 8, 128).transpose(2, 1, 0)).reshape(128, 8 * 31),
        np.ascontiguousarray(fw.reshape(3, 44, 128).transpose(2, 1, 0)).reshape(128, 44 * 3),
        fm(ffn_dw_b[L], 44)], axis=1).astype(f32)
    inv_freq = (np.float32(500000.0) ** (-np.arange(0, 32, 2, dtype=f32) / np.float32(32))).astype(f32)
    shared = {
        "w_ada": np.ascontiguousarray(np.asarray(w_ada, f32)[L]), "b_ada": np.ascontiguousarray(np.asarray(b_ada, f32)[L][None, :]),
        "w_in": np.ascontiguousarray(np.asarray(w_in, f32)[L]), "w_conv_out": np.ascontiguousarray(np.asarray(w_conv_out, f32)[L]),
        "w_attn_out": np.ascontiguousarray(np.asarray(w_attn_out, f32)[L]), "w_out": np.ascontiguousarray(np.asarray(w_out, f32)[L]),
        "w_up": np.ascontiguousarray(np.asarray(w_up, f32)[L]), "w_down": np.ascontiguousarray(np.asarray(w_down, f32)[L]),
        "eall": eall, "tri": tri, "ident": ident, "identb": _bf16(ident), "perm": perm,
    }
    in_maps = []
    for core in range(8):
        s, r = core // 4, core % 4
        own0 = r * CH
        others = [q for q in range(4) if q != r]
        pos = np.concatenate([np.arange(own0 - HALO, own0 + CH)] + [np.arange(q * CH, (q + 1) * CH) for q in others])
        valid_tok = pos >= 0
        xa = np.zeros((NALL, D), f32)
        xa[valid_tok] = x[s][pos[valid_tok]]
        posf = np.where(valid_tok, pos, 0).astype(f32)
        ang = posf[None, :] * inv_freq[:, None]
        cs_, sn_ = np.cos(ang).astype(f32), np.sin(ang).astype(f32)
        ropeC = np.concatenate([cs_, cs_], 0)
        ropeS = np.concatenate([-sn_, sn_], 0)
        gblk = pos[::256] // 256
        gblk = np.where(pos[::256] >= 0, gblk, -10)
        pb = np.zeros((18, NBLK), f32)
        val = np.zeros((18, NBLK), f32)
        for row in range(18):
            gb = gblk[row]
            for jj in range(NBLK):
                ok = (jj >= 2) and (gblk[jj] >= 0) and (gblk[jj] < gb)
                val[row, jj] = 1.0 if ok else 0.0
                pb[row, jj] = 0.0 if ok else -1e30
        pvc = np.concatenate([pv_base, np.full((128, 1), 0.0 if r == 0 else 1.0, f32)], axis=1)
        m = dict(shared)
        m.update({
            "xa": xa, "cc": fm(c[s], 16), "pv": np.ascontiguousarray(pvc),
            "ropeC": np.ascontiguousarray(ropeC), "ropeS": np.ascontiguousarray(ropeS),
            "pb": np.ascontiguousarray(np.broadcast_to(pb.reshape(1, -1), (128, 18 * NBLK))),
            "val": np.ascontiguousarray(np.broadcast_to(val.reshape(1, -1), (128, 18 * NBLK))),
        })
        in_maps.append(m)
    nc = build()
    res = run_bass_kernel_spmd(nc, in_maps, core_ids=list(range(8)))
    out = np.zeros((2, SEQ, D), f32)
    for core in range(8):
        s, r = core // 4, core % 4
        out[s, r * CH:(r + 1) * CH] = np.asarray(res.results[core]["out"], f32)
    return out
```

```python
import numpy as np
import ml_dtypes
from contextlib import ExitStack
import concourse.bass as bass
import concourse.mybir as mybir
from concourse.bass_utils import run_bass_kernel_spmd

F32 = mybir.dt.float32
BF16 = mybir.dt.bfloat16
AF = mybir.ActivationFunctionType
ALU = mybir.AluOpType
AX = mybir.AxisListType

D = 2048
SEQ = 16384
CH = 4096
HALO = 512
NOWN = CH + HALO
NALL = SEQ + HALO
NT_ALL = NALL // 512
NT_OWN = NOWN // 512
NBLK = NALL // 256
NKT = NALL // 128
DFF = 5632
ALPHA = 2.0 ** 0.25
EPS_LN = 1e-5
EPS_DN = 1e-5 / (ALPHA * ALPHA)
QSCALE = 128.0 ** -0.5
NEG = -30000.0
ENG = ("sync", "scalar", "vector", "gpsimd", "tensor")

O_LN1G, O_LN1B, O_LN2G, O_LN2B, O_BCO, O_CDB, O_CLG, O_CLB, O_CW, O_FW, O_FB, O_FLAG, NPV = \
    0, 16, 32, 48, 64, 80, 88, 96, 104, 352, 484, 528, 529
P_SC1, P_SH1, P_G1S, P_A2, P_B2, P_G2S, P_TMP, NPAR = 0, 16, 32, 48, 64, 80, 96, 128


class Sem:
    __slots__ = ("h", "v")


class Prog:
    def __init__(self, nc, stack):
        self.nc = nc
        self.stack = stack
        self.q = {e: [] for e in ENG}
        self.waited = {e: {} for e in ENG}
        self.pg = {}
        self.dma_out = []
        self.nsem = 0
        self.abs = {e: [] for e in ENG}
        self.semval = {}
        self.check = False

    def sem(self, name):
        s = Sem()
        self.nsem += 1
        s.h = self.stack.enter_context(self.nc.semaphore(f"{name}_{self.nsem}"))
        s.v = 0
        return s

    def new_phase(self, tag):
        self.pg = {e: self.sem(f"pg{tag}{e[:2]}") for e in ("scalar", "vector", "gpsimd", "tensor")}
        self.waited = {e: {} for e in ENG}
        self.dma_out = []

    def _waits(self, eng, waits):
        for w in waits:
            if w is None:
                continue
            s, v = w
            if v <= 0 or self.waited[eng].get(id(s), 0) >= v:
                continue
            self.waited[eng][id(s)] = v
            self.abs[eng].append(('w', id(s), v))
            self.q[eng].append(lambda e, s=s, v=v: e.wait_ge(s.h, v))

    def op(self, eng, fn, waits=(), mark=True):
        self._waits(eng, waits)
        if mark:
            s = self.pg[eng]
            s.v += 1
            self.abs[eng].append(('i', id(s), 1))
            self.q[eng].append(lambda e, fn=fn, s=s: fn(e).then_inc(s.h, 1))
            return (s, s.v)
        self.q[eng].append(lambda e, fn=fn: fn(e))
        return None

    def dma(self, eng, out, in_, sem, waits=(), **kw):
        self._waits(eng, waits)
        sem.v += 16
        self.abs[eng].append(('i', id(sem), 16))
        self.q[eng].append(lambda e, out=out, in_=in_, sem=sem, kw=kw: e.dma_start(out=out, in_=in_, **kw).then_inc(sem.h, 16))
        tok = (sem, sem.v)
        self.dma_out.append(tok)
        return tok

    def barrier(self, tag):
        b = self.sem(f"bar{tag}")
        for e in ENG:
            if e in self.pg and self.pg[e].v > 0:
                self._waits(e, [(self.pg[e], self.pg[e].v)])
        last = {}
        for t in self.dma_out:
            last[id(t[0])] = t
        self._waits("sync", list(last.values()))
        for e in ENG:
            self.abs[e].append(('i', id(b), 1))
            self.q[e].append(lambda en, b=b: en.sem_inc(b.h, 1))
        for e in ENG:
            self.abs[e].append(('w', id(b), len(ENG)))
            self.q[e].append(lambda en, b=b: en.wait_ge(b.h, len(ENG)))

    def simulate(self):
        pc = {e: 0 for e in ENG}
        val = self.semval
        prog = True
        while prog:
            prog = False
            for e in ENG:
                q = self.abs[e]
                while pc[e] < len(q):
                    k, sid, v = q[pc[e]]
                    if k == 'w':
                        if val.get(sid, 0) < v:
                            break
                    else:
                        val[sid] = val.get(sid, 0) + v
                    pc[e] += 1
                    prog = True
        stuck = {e: (pc[e], len(self.abs[e])) for e in ENG if pc[e] < len(self.abs[e])}
        self.abs = {e: [] for e in ENG}
        return stuck

    def flush(self):
        if self.check:
            st = self.simulate()
            print("SIM stuck:", st)
        with self.nc.Block() as block:
            for name in ENG:
                ops = self.q[name]
                if not ops:
                    continue

                def run(e, ops=ops):
                    for f in ops:
                        f(e)
                getattr(block, name)(run)
        self.q = {e: [] for e in ENG}


class Ring:
    def __init__(self, P, name, slots):
        self.slots = slots
        self.sems = [P.sem(f"{name}{i}") for i in range(len(slots))]
        self.free = [[] for _ in slots]
        self.i = 0

    def next(self):
        i = self.i % len(self.slots)
        self.i += 1
        return i


class BankRing:
    def __init__(self, banks):
        self.banks = list(banks)
        self.free = {b: [] for b in banks}
        self.i = 0

    def next(self):
        b = self.banks[self.i % len(self.banks)]
        self.i += 1
        return b


def build():
    nc = bass.Bass("TRN2", target_bir_lowering=False)

    def din(name, shape, dt):
        return nc.dram_tensor(name, shape, dt, kind="ExternalInput").ap()

    def dint(name, shape, dt):
        return nc.dram_tensor(name, shape, dt, kind="Internal").ap()

    xa = din("xa", [NALL, D], F32)
    cc = din("cc", [128, 16], F32)
    w_ada = din("w_ada", [D, 6 * D], F32)
    b_ada = din("b_ada", [1, 6 * D], F32)
    w_in = din("w_in", [D, 9216], F32)
    w_co = din("w_conv_out", [1024, D], F32)
    w_ao = din("w_attn_out", [1024, D], F32)
    w_o = din("w_out", [D, D], F32)
    w_up = din("w_up", [D, 2 * DFF], F32)
    w_dn = din("w_down", [DFF, D], F32)
    pvd = din("pv", [128, NPV], F32)
    ropeC = din("ropeC", [32, NALL], F32)
    ropeS = din("ropeS", [32, NALL], F32)
    pbd = din("pb", [128, 18 * NBLK], F32)
    vald = din("val", [128, 18 * NBLK], F32)
    ealld = din("eall", [NBLK, NBLK * 128], BF16)
    trid = din("tri", [128, 4 * 512], BF16)
    identd = din("ident", [128, 128], F32)
    identbd = din("identb", [128, 128], BF16)
    permd = din("perm", [32, 32], F32)
    outd = nc.dram_tensor("out", [CH, D], F32, kind="ExternalOutput").ap()

    ws_in = dint("ws_in", [72, 128, 16, 128], BF16)
    ws_co = dint("ws_co", [16, 128, 8, 128], BF16)
    ws_ao = dint("ws_ao", [16, 128, 8, 128], BF16)
    ws_o = dint("ws_o", [16, 128, 16, 128], BF16)
    ws_up = dint("ws_up", [88, 128, 16, 128], BF16)
    ws_dn = dint("ws_dn", [16, 128, 44, 128], BF16)
    ksc = dint("ksc", [8, 128, NALL], BF16)
    vsc = dint("vsc", [8, 128, NKT, 128], BF16)
    qsc = dint("qsc", [8, 128, NOWN], BF16)
    osc = dint("osc", [8, 128, NOWN], BF16)
    modsc = dint("modsc", [1, 6 * D], F32)

    with ExitStack() as gst:
        P = Prog(nc, gst)

        def gsb(name, shape, dt):
            return gst.enter_context(nc.sbuf_tensor(name, shape, dt))

        ps = gst.enter_context(nc.psum_tensor("ps", [128, 8, 512], F32))
        pv = gsb("pvs", [128, NPV], F32)
        par = gsb("par", [128, NPAR], F32)
        ident = gsb("ident_s", [128, 128], F32)
        identb = gsb("identb_s", [128, 128], BF16)
        onesb = gsb("onesb", [128, 128], BF16)
        ones1k = gsb("ones1k", [128, 128], F32)
        onesf = gsb("onesf", [128, 128], F32)
        ones2k = gsb("ones2k", [128, 128], F32)
        perm = gsb("perm_s", [32, 32], F32)
        kmT = gsb("kmT", [128, 8, NBLK], BF16)

        with ExitStack() as st:
            def sb(name, shape, dt):
                return st.enter_context(nc.sbuf_tensor(name, shape, dt))
            cact = sb("cact", [128, 16], F32)
            wa = sb("wa", [128, 2, 4096], F32)
            bada = sb("bada", [1, 6 * D], F32)
            modrow = sb("modrow", [1, 4096], F32)
            modT = sb("modT", [128, 96], F32)
            stg = sb("stg", [128, 6, 1024], F32)
            stb = sb("stb", [128, 6, 1024], BF16)
            P.new_phase("a")
            sc = P.sem("ldc")
            P.dma("sync", pv[:], pvd, sc)
            P.dma("sync", ident[:], identd, sc)
            P.dma("sync", identb[:], identbd, sc)
            P.dma("sync", perm[:], permd, sc)
            P.dma("sync", cact[:], cc, sc)
            tconst = P.dma("sync", bada[:], b_ada, sc)
            P.op("vector", lambda e: e.memset(onesb[:], 1.0))
            P.op("vector", lambda e: e.memset(ones1k[:], 1.0 / 1024.0))
            P.op("vector", lambda e: e.memset(onesf[:], 1.0))
            P.op("vector", lambda e: e.memset(ones2k[:], 1.0 / 2048.0))
            tca = P.op("scalar", lambda e: e.activation(out=cact[:], in_=cact[:], func=AF.Silu), waits=[tconst])
            wring = Ring(P, "wa", [wa[:, 0, :], wa[:, 1, :]])
            smod = P.sem("modw")
            ev = None
            store = None
            for p in range(3):
                tk = None
                for k in range(16):
                    i = wring.next()
                    tl = P.dma("sync", wring.slots[i], w_ada[k * 128:(k + 1) * 128, p * 4096:(p + 1) * 4096],
                               wring.sems[i], waits=wring.free[i])
                    for b in range(8):
                        tk = P.op("tensor", lambda e, b=b, i=i, k=k: e.matmul(
                            ps[0:1, b, :], lhsT=cact[:, k:k + 1], rhs=wring.slots[i][:, b * 512:(b + 1) * 512],
                            start=(k == 0), stop=(k == 15)), waits=[tl, tca, ev], mark=(b == 7))
                    wring.free[i] = [tk]
                for b in range(8):
                    ev = P.op("vector", lambda e, b=b, p=p: e.tensor_tensor(
                        out=modrow[0:1, b * 512:(b + 1) * 512], in0=ps[0:1, b, :],
                        in1=bada[0:1, p * 4096 + b * 512:p * 4096 + (b + 1) * 512], op=ALU.add),
                        waits=[tk, tconst, store])
                tt_ = None
                for c in range(32):
                    tt_ = P.op("tensor", lambda e, c=c: e.transpose(ps[:, 7, c:c + 1], modrow[0:1, c * 128:(c + 1) * 128],
                                                                     ident[0:1, 0:1]), waits=[ev, tconst])
                ev = P.op("vector", lambda e, p=p: e.tensor_copy(out=modT[:, p * 32:(p + 1) * 32], in_=ps[:, 7, 0:32]), waits=[tt_])
                store = ev
            tmod = ev
            w0 = [tmod, tconst]
            P.op("vector", lambda e: e.tensor_scalar(out=par[:, P_SC1:P_SC1 + 16], in0=modT[:, 16:32], scalar1=1.0,
                                                     scalar2=None, op0=ALU.add), waits=w0)
            P.op("vector", lambda e: e.tensor_copy(out=par[:, P_SH1:P_SH1 + 16], in_=modT[:, 0:16]))
            P.op("vector", lambda e: e.tensor_scalar(out=par[:, P_G1S:P_G1S + 16], in0=modT[:, 32:48], scalar1=1.0,
                                                     scalar2=1.0 / ALPHA, op0=ALU.add, op1=ALU.mult))
            P.op("vector", lambda e: e.tensor_scalar(out=par[:, P_G2S:P_G2S + 16], in0=modT[:, 80:96], scalar1=1.0,
                                                     scalar2=1.0 / ALPHA, op0=ALU.add, op1=ALU.mult))
            t1 = P.op("vector", lambda e: e.tensor_scalar(out=par[:, P_TMP:P_TMP + 16], in0=modT[:, 64:80], scalar1=1.0,
                                                          scalar2=None, op0=ALU.add))
            t2 = P.op("vector", lambda e: e.tensor_tensor(out=par[:, P_A2:P_A2 + 16], in0=pv[:, O_LN1G:O_LN1G + 16],
                                                          in1=par[:, P_TMP:P_TMP + 16], op=ALU.mult), waits=[t1])
            t3 = P.op("vector", lambda e: e.tensor_tensor(out=par[:, P_B2:P_B2 + 16], in0=pv[:, O_LN1B:O_LN1B + 16],
                                                          in1=par[:, P_TMP:P_TMP + 16], op=ALU.mult), waits=[t1])
            P.op("vector", lambda e: e.tensor_tensor(out=par[:, P_B2:P_B2 + 16], in0=par[:, P_B2:P_B2 + 16],
                                                     in1=modT[:, 48:64], op=ALU.add), waits=[t3])
            sring = Ring(P, "stg", [stg[:, i, :] for i in range(6)])
            bring = Ring(P, "stb", [stb[:, i, :] for i in range(6)])
            ci = 0
            for (src, dst, K, N) in ((w_in, ws_in, D, 9216), (w_co, ws_co, 1024, D), (w_ao, ws_ao, 1024, D),
                                     (w_o, ws_o, D, D), (w_up, ws_up, D, 2 * DFF), (w_dn, ws_dn, DFF, D)):
                for kk in range(K // 128):
                    for cs in range(N // 1024):
                        i = sring.next()
                        j = bring.next()
                        tl = P.dma("sync", sring.slots[i], src[kk * 128:(kk + 1) * 128, cs * 1024:(cs + 1) * 1024],
                                   sring.sems[i], waits=sring.free[i])
                        eng = ("vector", "gpsimd")[ci % 2]
                        ci += 1
                        tcst = P.op(eng, lambda e, i=i, j=j: e.tensor_copy(out=bring.slots[j], in_=sring.slots[i]),
                                    waits=[tl] + bring.free[j])
                        sring.free[i] = [tcst]
                        ts = P.dma("scalar", dst[cs * 8:(cs + 1) * 8, :, kk, :].rearrange("c p d -> p c d"),
                                   bring.slots[j].rearrange("p (c d) -> p c d", d=128), bring.sems[j], waits=[tcst])
                        bring.free[j] = [ts]
            P.barrier("a")
            P.flush()

        with ExitStack() as st:
            def sb(name, shape, dt):
                return st.enter_context(nc.sbuf_tensor(name, shape, dt))
            wq = sb("wq", [128, 8, 16, 128], BF16)
            wk = sb("wk", [128, 8, 16, 128], BF16)
            wv = sb("wv", [128, 8, 16, 128], BF16)
            xin = sb("xin", [128, 4, D], F32)
            uT = sb("uT", [128, 2, 16, 512], BF16)
            kf = sb("kf", [128, 3, 512], F32)
            r1 = sb("r1", [32, 3, 512], F32)
            r2 = sb("r2", [32, 3, 512], F32)
            kbf = sb("kbf", [128, 4, 512], BF16)
            ctb = sb("ctb", [32, 2, 512], F32)
            stb2 = sb("stb2", [32, 2, 512], F32)
            vbf = sb("vbf", [128, 3, 1024], BF16)
            kms = sb("kms", [128, 8, NBLK], F32)
            P.new_phase("k")
            sw = P.sem("ldw")
            P.dma("sync", wq[:], ws_in[16:24].rearrange("c p k d -> p c k d"), sw)
            P.dma("sync", wk[:], ws_in[24:32].rearrange("c p k d -> p c k d"), sw)
            tw = P.dma("sync", wv[:], ws_in[32:40].rearrange("c p k d -> p c k d"), sw)
            xring = Ring(P, "xin", [xin[:, i, :] for i in range(4)])
            tpb = BankRing([0, 1, 2])
            prb = BankRing([3, 4, 5])
            swb = BankRing([6, 7])
            kfr = Ring(P, "kf", [0, 1, 2])
            kbr = Ring(P, "kbf", [kbf[:, i, :] for i in range(4)])
            vbr = Ring(P, "vbf", [vbf[:, i, :] for i in range(3)])
            ropr = Ring(P, "rope", [0, 1])
            uT_free = [[], []]
            evi = 0
            for t in range(NT_ALL):
                tok0 = t * 512
                u = t % 2
                xt = []
                for sub in range(4):
                    i = xring.next()
                    tl = P.dma("sync", xring.slots[i], xa[tok0 + sub * 128:tok0 + (sub + 1) * 128, :], xring.sems[i],
                               waits=xring.free[i])
                    xt.append((i, tl))
                ri = ropr.next()
                P.dma("gpsimd", ctb[:, ri, :], ropeC[:, tok0:tok0 + 512], ropr.sems[ri], waits=ropr.free[ri])
                trope = P.dma("gpsimd", stb2[:, ri, :], ropeS[:, tok0:tok0 + 512], ropr.sems[ri], waits=ropr.free[ri])
                uready = []
                tpe = None
                for c in range(16):
                    b = tpb.next()
                    for sub in range(4):
                        i, tl = xt[sub]
                        tpe = P.op("tensor", lambda e, b=b, sub=sub, i=i, c=c: e.transpose(
                            ps[:, b, sub * 128:(sub + 1) * 128], xin[:, i, c * 128:(c + 1) * 128], ident[:]),
                            waits=[tl] + tpb.free[b], mark=(sub == 3))
                    if evi % 2 == 0:
                        te = P.op("scalar", lambda e, b=b, c=c, u=u: e.activation(
                            out=uT[:, u, c, :], in_=ps[:, b, :], func=AF.Identity,
                            scale=par[:, P_SC1 + c:P_SC1 + c + 1], bias=par[:, P_SH1 + c:P_SH1 + c + 1]),
                            waits=[tpe] + uT_free[u])
                    else:
                        te = P.op("vector", lambda e, b=b, c=c, u=u: e.tensor_scalar(
                            out=uT[:, u, c, :], in0=ps[:, b, :], scalar1=par[:, P_SC1 + c:P_SC1 + c + 1],
                            scalar2=par[:, P_SH1 + c:P_SH1 + c + 1], op0=ALU.mult, op1=ALU.add),
                            waits=[tpe] + uT_free[u])
                    evi += 1
                    tpb.free[b] = [te]
                    uready = (uready + [te])[-2:]
                for (i, tl) in xt:
                    xring.free[i] = [tpe]
                last_rope = None
                for kind in (("k", "q") if t < NT_OWN else ("k",)):
                    wmat = wk if kind == "k" else wq
                    for h in range(8):
                        b = prb.next()
                        tpe = None
                        for k in range(16):
                            tpe = P.op("tensor", lambda e, b=b, h=h, k=k, u=u, wmat=wmat: e.matmul(
                                ps[:, b, :], lhsT=wmat[:, h, k, :], rhs=uT[:, u, k, :], start=(k == 0), stop=(k == 15)),
                                waits=uready + [tw] + prb.free[b], mark=(k == 15))
                        fi = kfr.next()
                        ta = P.op("scalar", lambda e, b=b, fi=fi: e.activation(out=kf[:, fi, :], in_=ps[:, b, :], func=AF.Copy),
                                  waits=[tpe] + kfr.free[fi])
                        prb.free[b] = [ta]
                        sbk = swb.next()
                        tsw = P.op("tensor", lambda e, sbk=sbk, fi=fi: e.matmul(
                            ps[0:32, sbk, :], lhsT=perm[:, :], rhs=kf[0:32, fi, :], start=True, stop=True),
                            waits=[ta] + swb.free[sbk])
                        ta1 = P.op("vector", lambda e, fi=fi, ri=ri: e.tensor_tensor(
                            out=r1[:, fi, :], in0=kf[0:32, fi, :], in1=ctb[:, ri, :], op=ALU.mult), waits=[ta, trope])
                        ta2 = P.op("vector", lambda e, fi=fi, ri=ri, sbk=sbk: e.tensor_tensor(
                            out=r2[:, fi, :], in0=ps[0:32, sbk, :], in1=stb2[:, ri, :], op=ALU.mult), waits=[tsw])
                        swb.free[sbk] = [ta2]
                        ta3 = P.op("vector", lambda e, fi=fi: e.tensor_tensor(
                            out=kf[0:32, fi, :], in0=r1[:, fi, :], in1=r2[:, fi, :], op=ALU.add), waits=[ta1, ta2, tsw])
                        last_rope = ta3
                        ki = kbr.next()
                        if kind == "k":
                            ta4 = P.op("vector", lambda e, fi=fi, h=h, t=t: e.tensor_reduce(
                                out=kms[:, h, 2 * t:2 * t + 2], in_=kf[:, fi, :].rearrange("p (a b) -> p a b", b=256),
                                axis=AX.X, op=ALU.add), waits=[ta3])
                            tcs = P.op("gpsimd", lambda e, fi=fi, ki=ki: e.tensor_copy(out=kbf[:, ki, :], in_=kf[:, fi, :]),
                                       waits=[ta3] + kbr.free[ki])
                            kfr.free[fi] = [tcs, ta4]
                            ts = P.dma("sync", ksc[h, :, tok0:tok0 + 512], kbf[:, ki, :], kbr.sems[ki], waits=[tcs])
                        else:
                            tcs = P.op("gpsimd", lambda e, fi=fi, ki=ki: e.tensor_scalar(
                                out=kbf[:, ki, :], in0=kf[:, fi, :], scalar1=QSCALE, scalar2=None, op0=ALU.mult),
                                waits=[ta3] + kbr.free[ki])
                            kfr.free[fi] = [tcs]
                            ts = P.dma("sync", qsc[h, :, tok0:tok0 + 512], kbf[:, ki, :], kbr.sems[ki], waits=[tcs])
                        kbr.free[ki] = [ts]
                ropr.free[ri] = [last_rope]
                for sub in range(4):
                    vi = vbr.next()
                    tvs = []
                    for half in range(2):
                        b = prb.next()
                        tpe = None
                        for k in range(16):
                            tpe = P.op("tensor", lambda e, b=b, k=k, u=u, sub=sub, half=half: e.matmul(
                                ps[:, b, :].rearrange("p (c d) -> p c d", d=128),
                                lhsT=uT[:, u, k, sub * 128:(sub + 1) * 128], rhs=wv[:, half * 4:(half + 1) * 4, k, :],
                                start=(k == 0), stop=(k == 15)), waits=uready + [tw] + prb.free[b], mark=(k == 15))
                        if half == 0:
                            te = P.op("scalar", lambda e, b=b, vi=vi: e.activation(
                                out=vbf[:, vi, 0:512], in_=ps[:, b, :], func=AF.Copy), waits=[tpe] + vbr.free[vi])
                        else:
                            te = P.op("vector", lambda e, b=b, vi=vi: e.tensor_copy(
                                out=vbf[:, vi, 512:1024], in_=ps[:, b, :]), waits=[tpe] + vbr.free[vi])
                        prb.free[b] = [te]
                        tvs.append(te)
                    ts = P.dma("scalar", vsc[:, :, t * 4 + sub, :].rearrange("h p d -> p h d"),
                               vbf[:, vi, :].rearrange("p (h d) -> p h d", d=128), vbr.sems[vi], waits=tvs)
                    vbr.free[vi] = [ts]
                uT_free[u] = [tpe]
            P.op("vector", lambda e: e.tensor_scalar(out=kmT[:], in0=kms[:], scalar1=1.0 / 256.0, scalar2=None, op0=ALU.mult),
                 waits=[(P.pg["vector"], P.pg["vector"].v)])
            P.barrier("k")
            P.flush()

        with ExitStack() as st:
            def sb(name, shape, dt):
                return st.enter_context(nc.sbuf_tensor(name, shape, dt))
            KT = sb("KT", [128, 2, NALL], BF16)
            VV = sb("VV", [128, 2, NKT, 128], BF16)
            eall = sb("eall_s", [NBLK, NBLK, 128], BF16)
            tri = sb("tri_s", [128, 4, 512], BF16)
            pbs = sb("pbs", [128, 18, NBLK], F32)
            vals = sb("vals", [128, 18, NBLK], F32)
            qT = sb("qT", [128, 2, 512], BF16)
            gm = sb("gm", [128, 4, NBLK], F32)
            ge = sb("ge", [128, 4, NBLK], F32)
            top8 = sb("top8", [128, 4, 8], F32)
            selT = sb("selT", [NBLK, 2, 512], BF16)
            pT = sb("pT", [128, 8, 512], BF16)
            acc = sb("acc", [128, 2, 512], F32)
            rinv = sb("rinv", [128, 512], F32)
            oT = sb("oT", [128, 2, 512], BF16)
            P.new_phase("A")
            sc = P.sem("ldA")
            P.dma("sync", eall[:], ealld.rearrange("p (j t) -> p j t", t=128), sc)
            P.dma("sync", tri[:], trid.rearrange("p (j t) -> p j t", t=512), sc)
            P.dma("sync", pbs[:], pbd.rearrange("p (j t) -> p j t", t=NBLK), sc)
            tcon = P.dma("sync", vals[:], vald.rearrange("p (j t) -> p j t", t=NBLK), sc)
            kvr = Ring(P, "kv", [0, 1])
            qr = Ring(P, "qT", [0, 1])
            sbk = BankRing([0, 1, 2])
            ptr = Ring(P, "pT", list(range(8)))
            orr = Ring(P, "oT", [0, 1])
            O_B, R_B, G_B, ST_B = 3, 4, 5, 6
            S = {"or_free": [], "g_free": [], "st_free": [], "gi": 0, "last_pe": None}
            selT_free = [[], []]
            ge_free = [[], [], [], []]
            acc_free = [[], []]
            kv_tok = {}
            kv_slot = {}

            def load_kv(h):
                kv = kvr.next()
                kv_slot[h] = kv
                tkv = None
                for pc in range(4):
                    a0_, a1_ = pc * (NALL // 4), (pc + 1) * (NALL // 4)
                    P.dma("gpsimd", KT[:, kv, a0_:a1_], ksc[h, :, a0_:a1_], kvr.sems[kv], waits=kvr.free[kv])
                for pc in range(4):
                    a0_, a1_ = pc * (NKT // 4), (pc + 1) * (NKT // 4)
                    tkv = P.dma("gpsimd", VV[:, kv, a0_:a1_, :], vsc[h, :, a0_:a1_, :], kvr.sems[kv], waits=kvr.free[kv])
                kv_tok[h] = tkv

            G = {}

            QL = {}

            def qload(idx):
                h, mt = idx // NT_OWN, idx % NT_OWN
                qi = qr.next()
                tq = P.dma("sync", qT[:, qi, :], qsc[h, :, mt * 512:(mt + 1) * 512], qr.sems[qi], waits=qr.free[qi])
                QL[idx] = (qi, tq)

            GA = {}

            def gatingA(idx):
                h, mt = idx // NT_OWN, idx % NT_OWN
                qi, tq = QL[idx]
                d6s = []
                for sub in range(4):
                    row = 2 * mt + sub // 2
                    g = sub
                    tg = P.op("tensor", lambda e, qi=qi, sub=sub, h=h: e.matmul(
                        ps[:, G_B, sub * 128:sub * 128 + NBLK], lhsT=qT[:, qi, sub * 128:(sub + 1) * 128],
                        rhs=kmT[:, h, :], start=True, stop=True), waits=[tq] + S["g_free"])
                    d1 = P.op("vector", lambda e, g=g, sub=sub, row=row: e.tensor_tensor(
                        out=gm[:, g, :], in0=ps[:, G_B, sub * 128:sub * 128 + NBLK], in1=pbs[:, row, :], op=ALU.add),
                        waits=[tg, tcon])
                    S["g_free"] = [d1]
                    d2 = P.op("vector", lambda e, g=g: e.max(out=top8[:, g, :], in_=gm[:, g, :]), waits=[d1])
                    d3 = P.op("vector", lambda e, g=g: e.tensor_scalar(
                        out=ge[:, g, :], in0=gm[:, g, :], scalar1=top8[:, g, 2:3], scalar2=None, op0=ALU.is_ge),
                        waits=[d2] + ge_free[g])
                    d4 = P.op("vector", lambda e, g=g, row=row: e.tensor_tensor(
                        out=ge[:, g, :], in0=ge[:, g, :], in1=vals[:, row, :], op=ALU.mult), waits=[d3])
                    d5 = P.op("vector", lambda e, g=g: e.tensor_scalar(
                        out=ge[:, g, :], in0=ge[:, g, :], scalar1=-NEG, scalar2=NEG, op0=ALU.mult, op1=ALU.add),
                        waits=[d4])
                    d6 = P.op("vector", lambda e, g=g, row=row: e.memset(ge[:, g, row:row + 1], 0.0), waits=[d5])
                    d6s.append(d6)
                GA[idx] = d6s

            def gatingB(idx):
                qi, tq = QL.pop(idx)
                d6s = GA.pop(idx)
                sp = idx % 2
                tts = []
                for sub in range(4):
                    g = sub
                    tt = P.op("tensor", lambda e, g=g, sub=sub: e.transpose(
                        ps[0:NBLK, ST_B, sub * 128:(sub + 1) * 128], ge[:, g, :], ident[:]), waits=[d6s[sub]] + S["st_free"])
                    ge_free[g] = [tt]
                    tts.append(tt)
                tsel = P.op("vector", lambda e, sp=sp: e.tensor_copy(out=selT[:, sp, :], in_=ps[0:NBLK, ST_B, :]),
                            waits=tts + selT_free[sp])
                S["st_free"] = [tsel]
                G[idx] = (qi, tq, sp, tsel)

            NQ = 8 * NT_OWN
            load_kv(0)
            load_kv(1)
            qload(0)
            gatingA(0)
            gatingB(0)
            for idx in range(NQ):
                h, mt = idx // NT_OWN, idx % NT_OWN
                kv = kv_slot[h]
                tkv = kv_tok[h]
                qi, tq, sp, tsel = G.pop(idx)
                ai = idx % 2
                own = list(range(0, 2)) if mt == 0 else list(range(2, 2 * mt + 2))
                blocks = own + list(range(18, NBLK))
                tiles = [(jj, k2) for jj in blocks for k2 in range(2)]
                n = len(tiles)
                Dp = 2
                ptoks = [None] * n
                pslot = [None] * n
                tadd = None
                if idx + 1 < NQ:
                    qload(idx + 1)
                for it in range(n + Dp):
                    if it == n - 24 and idx + 1 < NQ:
                        gatingA(idx + 1)
                    if it == n - 6 and idx + 1 < NQ:
                        gatingB(idx + 1)
                    if it < n:
                        jj, k2 = tiles[it]
                        kti = jj * 2 + k2
                        b = sbk.next()
                        diag = (jj == 2 * mt) or (jj == 2 * mt + 1)
                        P.op("tensor", lambda e, b=b, kv=kv, kti=kti, qi=qi: e.matmul(
                            ps[:, b, :], lhsT=KT[:, kv, kti * 128:(kti + 1) * 128], rhs=qT[:, qi, :], start=True, stop=False),
                            waits=[tq, tkv] + sbk.free[b], mark=False)
                        tpe = P.op("tensor", lambda e, b=b, jj=jj, sp=sp, diag=diag: e.matmul(
                            ps[:, b, :], lhsT=eall[:, jj, :], rhs=selT[:, sp, :], start=False, stop=(not diag)),
                            waits=[tsel, tcon], mark=(not diag))
                        if diag:
                            tix = (jj - 2 * mt) * 2 + k2
                            tpe = P.op("tensor", lambda e, b=b, tix=tix: e.matmul(
                                ps[:, b, :], lhsT=identb[:, :], rhs=tri[:, tix, :], start=False, stop=True))
                        pi = ptr.next()
                        pslot[it] = pi
                        tex = P.op("scalar", lambda e, b=b, pi=pi: e.activation(out=pT[:, pi, :], in_=ps[:, b, :], func=AF.Exp),
                                   waits=[tpe] + ptr.free[pi])
                        sbk.free[b] = [tex]
                        ptoks[it] = tex
                    j = it - Dp
                    if j >= 0:
                        jj, k2 = tiles[j]
                        kti = jj * 2 + k2
                        pi = pslot[j]
                        tpv = P.op("tensor", lambda e, kv=kv, kti=kti, pi=pi, j=j, n=n: e.matmul(
                            ps[:, O_B, :], lhsT=VV[:, kv, kti, :], rhs=pT[:, pi, :], start=(j == 0), stop=(j == n - 1)),
                            waits=[ptoks[j]] + (S["or_free"] if j == 0 else []))
                        if j == 0:
                            tadd = P.op("vector", lambda e, pi=pi, ai=ai: e.tensor_copy(out=acc[:, ai, :], in_=pT[:, pi, :]),
                                        waits=[ptoks[j]] + acc_free[ai])
                        else:
                            tadd = P.op("vector", lambda e, pi=pi, ai=ai: e.tensor_tensor(
                                out=acc[:, ai, :], in0=acc[:, ai, :], in1=pT[:, pi, :], op=ALU.add), waits=[ptoks[j], tadd])
                        ptr.free[pi] = [tpv, tadd]
                        S["last_pe"] = tpv
                last_pe = S["last_pe"]
                tr = P.op("tensor", lambda e, ai=ai: e.matmul(ps[:, R_B, :], lhsT=onesf[:, :], rhs=acc[:, ai, :], start=True, stop=True),
                          waits=[tadd] + S["or_free"])
                acc_free[ai] = [tr]
                qr.free[qi] = [tr]
                selT_free[sp] = [tr]
                oi = orr.next()
                n1 = P.op("vector", lambda e: e.reciprocal(out=rinv[:], in_=ps[:, R_B, :]), waits=[tr])
                n2 = P.op("vector", lambda e, oi=oi: e.tensor_tensor(out=oT[:, oi, :], in0=ps[:, O_B, :], in1=rinv[:], op=ALU.mult),
                          waits=[n1, last_pe] + orr.free[oi])
                S["or_free"] = [n2]
                ts = P.dma("sync", osc[h, :, mt * 512:(mt + 1) * 512], oT[:, oi, :], orr.sems[oi], waits=[n2])
                orr.free[oi] = [ts]
                if mt == NT_OWN - 1:
                    kvr.free[kv] = [tr]
                    if h + 2 < 8:
                        load_kv(h + 2)
            P.barrier("A")
            P.flush()

        with ExitStack() as st:
            def sb(name, shape, dt):
                return st.enter_context(nc.sbuf_tensor(name, shape, dt))
            xin = sb("xin2", [128, 2, D], F32)
            ost = sb("ost", [128, D], F32)
            xT = sb("xT", [128, 16, 512], F32)
            uT = sb("uT2", [128, 16, 512], BF16)
            big = sb("big", [128, 44, 512], BF16)
            hcb = sb("hcb", [128, 2, 542], F32)
            hhist = sb("hhist", [128, 8, 30], F32)
            cv = sb("cv", [128, 8, 512], F32)
            tmp = sb("tmp", [128, 4, 512], F32)
            stat = sb("stat", [128, 4, 512], F32)
            abuf = sb("abuf", [128, 2, 514], F32)
            ahist = sb("ahist", [128, 44, 2], F32)
            wsl = sb("wsl", [128, 6, 16, 128], BF16)
            P.new_phase("M")
            NSL = 6
            wsems = [P.sem(f"ws{i}") for i in range(NSL)]
            wfree = [[] for _ in range(NSL)]
            reqs = []
            for t in range(NT_OWN):
                for i in range(8):
                    reqs.append((ws_in, i, 0, 16))
                    reqs.append((ws_in, 8 + i, 0, 16))
                for dc in range(16):
                    reqs.append((ws_co, dc, 0, 8))
                    reqs.append((ws_ao, dc, 0, 8))
                    reqs.append((ws_in, 40 + dc, 0, 16))
                    reqs.append((ws_in, 56 + dc, 0, 16))
                for dc in range(16):
                    reqs.append((ws_o, dc, 0, 16))
                for j in range(44):
                    reqs.append((ws_up, j, 0, 16))
                    reqs.append((ws_up, 44 + j, 0, 16))
                for dc in (range(16) if t > 0 else ()):
                    reqs.append((ws_dn, dc, 0, 16))
                    reqs.append((ws_dn, dc, 16, 16))
                    reqs.append((ws_dn, dc, 32, 12))
            wstate = {"issued": 0, "next": 0, "toks": {}}

            def wget():
                i = wstate["next"]
                wstate["next"] += 1
                lim = min(len(reqs), i + NSL)
                while wstate["issued"] < lim:
                    j = wstate["issued"]
                    dst, c, k0, nk = reqs[j]
                    s = j % NSL
                    wstate["toks"][j] = P.dma("sync", wsl[:, s, 0:nk, :], dst[c, :, k0:k0 + nk, :], wsems[s], waits=wfree[s])
                    wstate["issued"] += 1
                return i % NSL, wstate["toks"].pop(i)

            mmb = BankRing([0, 1, 2, 3, 4, 5])
            tmr = Ring(P, "tmp", [0, 1, 2, 3])
            xring = Ring(P, "xin2", [0, 1])
            hcr = Ring(P, "hcb", [0, 1])
            abr = Ring(P, "abuf", [0, 1])
            s_o = P.sem("ldo")
            s_out = P.sem("stout")
            ost_free = []
            flag = pv[:, O_FLAG:O_FLAG + 1]
            vtok = lambda: (P.pg["vector"], P.pg["vector"].v)
            atok = lambda: (P.pg["scalar"], P.pg["scalar"].v)
            ptok = lambda: (P.pg["tensor"], P.pg["tensor"].v)
            P.op("vector", lambda e: e.memset(hhist[:], 0.0))
            P.op("vector", lambda e: e.memset(ahist[:], 0.0))
            P.op("vector", lambda e: e.memset(hcb[:], 0.0))
            P.op("vector", lambda e: e.memset(abuf[:], 0.0))
            evi = 0

            def group(nk, lhs_fn, rhs_fn, extra_waits):
                b = mmb.next()
                tpe = None
                for k in range(nk):
                    tpe = P.op("tensor", lambda e, b=b, k=k: e.matmul(ps[:, b, :], lhsT=lhs_fn(k), rhs=rhs_fn(k),
                                                                     start=(k == 0), stop=(k == nk - 1)),
                               waits=list(extra_waits) + mmb.free[b], mark=(k == nk - 1))
                return b, tpe

            def wgroup(nk, rhs_fn, extra_waits):
                s, tl = wget()
                b, tpe = group(nk, lambda k, s=s: wsl[:, s, k, :], rhs_fn, list(extra_waits) + [tl])
                wfree[s] = [tpe]
                return b, tpe

            def ln_stats(nch, src_fn, ones_t, eps, waits):
                MB, QB = 6, 7
                tm = None
                tq = None
                for c in range(nch):
                    ti = tmr.next()
                    tsq = P.op("scalar", lambda e, c=c, ti=ti: e.activation(out=tmp[:, ti, :], in_=src_fn(c), func=AF.Square),
                               waits=list(waits) + tmr.free[ti])
                    tm = P.op("tensor", lambda e, c=c: e.matmul(ps[:, MB, :], lhsT=ones_t[:, :], rhs=src_fn(c),
                                                                start=(c == 0), stop=(c == nch - 1)), waits=list(waits))
                    tq = P.op("tensor", lambda e, c=c, ti=ti: e.matmul(ps[:, QB, :], lhsT=ones_t[:, :], rhs=tmp[:, ti, :],
                                                                      start=(c == 0), stop=(c == nch - 1)), waits=[tsq])
                    tmr.free[ti] = [tq]
                a = P.op("vector", lambda e: e.tensor_copy(out=stat[:, 0, :], in_=ps[:, MB, :]), waits=[tm, tq])
                a = P.op("vector", lambda e: e.tensor_tensor(out=stat[:, 2, :], in0=stat[:, 0, :], in1=stat[:, 0, :], op=ALU.mult), waits=[a])
                a = P.op("vector", lambda e: e.tensor_tensor(out=stat[:, 2, :], in0=ps[:, QB, :], in1=stat[:, 2, :], op=ALU.subtract), waits=[a])
                a = P.op("vector", lambda e: e.tensor_scalar(out=stat[:, 2, :], in0=stat[:, 2, :], scalar1=0.0, scalar2=eps,
                                                             op0=ALU.max, op1=ALU.add), waits=[a])
                a2 = P.op("scalar", lambda e: e.activation(out=stat[:, 2, :], in_=stat[:, 2, :], func=AF.Sqrt), waits=[a])
                a = P.op("vector", lambda e: e.reciprocal(out=stat[:, 1, :], in_=stat[:, 2, :]), waits=[a2])
                return a

            for t in range(NT_OWN):
                tok0 = t * 512
                big_w = [ptok(), vtok(), atok()]
                to = None
                for h in range(8):
                    to = P.dma("gpsimd", big[:, 24 + h, :], osc[h, :, tok0:tok0 + 512], s_o, waits=big_w)
                xw = [ptok(), vtok(), atok()]
                for sub in range(4):
                    i = xring.next()
                    tl = P.dma("sync", xin[:, i, :], xa[tok0 + sub * 128:tok0 + (sub + 1) * 128, :], xring.sems[i],
                               waits=xring.free[i])
                    tpe = None
                    for c4 in range(4):
                        b = mmb.next()
                        for cq in range(4):
                            c = c4 * 4 + cq
                            tpe = P.op("tensor", lambda e, b=b, cq=cq, i=i, c=c: e.transpose(
                                ps[:, b, cq * 128:(cq + 1) * 128], xin[:, i, c * 128:(c + 1) * 128], ident[:]),
                                waits=[tl] + mmb.free[b], mark=(cq == 3))
                        te = P.op("vector", lambda e, b=b, c4=c4, sub=sub: e.tensor_copy(
                            out=xT[:, c4 * 4:(c4 + 1) * 4, sub * 128:(sub + 1) * 128],
                            in_=ps[:, b, :].rearrange("p (c d) -> p c d", d=128)), waits=[tpe] + xw)
                        mmb.free[b] = [te]
                    xring.free[i] = [tpe]
                tx = vtok()
                for c in range(16):
                    if c % 2 == 0:
                        P.op("scalar", lambda e, c=c: e.activation(out=uT[:, c, :], in_=xT[:, c, :], func=AF.Identity,
                                                                   scale=par[:, P_SC1 + c:P_SC1 + c + 1],
                                                                   bias=par[:, P_SH1 + c:P_SH1 + c + 1]), waits=[tx] + xw)
                    else:
                        P.op("vector", lambda e, c=c: e.tensor_scalar(out=uT[:, c, :], in0=xT[:, c, :],
                                                                      scalar1=par[:, P_SC1 + c:P_SC1 + c + 1],
                                                                      scalar2=par[:, P_SH1 + c:P_SH1 + c + 1],
                                                                      op0=ALU.mult, op1=ALU.add), waits=[tx] + xw)
                tu = [vtok(), atok()]
                for i in range(8):
                    bl, tl_ = wgroup(16, lambda k: uT[:, k, :], tu)
                    bg, tg_ = wgroup(16, lambda k: uT[:, k, :], tu)
                    ti = tmr.next()
                    hi = hcr.next()
                    s1 = P.op("scalar", lambda e, bg=bg, ti=ti: e.activation(out=tmp[:, ti, :], in_=ps[:, bg, :], func=AF.Sigmoid),
                              waits=[tg_] + tmr.free[ti])
                    mmb.free[bg] = [s1]
                    h0 = P.op("vector", lambda e, hi=hi, i=i: e.tensor_copy(out=hcb[:, hi, 0:30], in_=hhist[:, i, :]),
                              waits=hcr.free[hi])
                    h1 = P.op("vector", lambda e, bl=bl, ti=ti, hi=hi: e.tensor_tensor(
                        out=hcb[:, hi, 30:542], in0=ps[:, bl, :], in1=tmp[:, ti, :], op=ALU.mult), waits=[tl_, s1, h0])
                    mmb.free[bl] = [h1]
                    tmr.free[ti] = [h1]
                    h2 = P.op("vector", lambda e, hi=hi, i=i: e.tensor_copy(out=hhist[:, i, :], in_=hcb[:, hi, 512:542]), waits=[h1])
                    if t == 0:
                        h2 = P.op("vector", lambda e, i=i: e.tensor_scalar(out=hhist[:, i, :], in0=hhist[:, i, :], scalar1=flag,
                                                                           scalar2=None, op0=ALU.mult), waits=[h2])
                    a = P.op("vector", lambda e, hi=hi, i=i: e.tensor_scalar(
                        out=cv[:, i, :], in0=hcb[:, hi, 0:512], scalar1=pv[:, O_CW + i * 31:O_CW + i * 31 + 1],
                        scalar2=pv[:, O_CDB + i:O_CDB + i + 1], op0=ALU.mult, op1=ALU.add), waits=[h1, h2] + big_w)
                    for j in range(1, 31):
                        a = P.op("vector", lambda e, hi=hi, i=i, j=j: e.scalar_tensor_tensor(
                            out=cv[:, i, :], in0=hcb[:, hi, j:j + 512], scalar=pv[:, O_CW + i * 31 + j:O_CW + i * 31 + j + 1],
                            in1=cv[:, i, :], op0=ALU.mult, op1=ALU.add), waits=[a])
                    hcr.free[hi] = [a]
                tcv = vtok()
                tst = ln_stats(8, lambda c: cv[:, c, :], ones1k, EPS_LN, [tcv])
                for i in range(8):
                    a = P.op("vector", lambda e, i=i: e.tensor_tensor(out=cv[:, i, :], in0=cv[:, i, :], in1=stat[:, 0, :], op=ALU.subtract),
                             waits=[tst, ptok()])
                    a = P.op("vector", lambda e, i=i: e.tensor_tensor(out=cv[:, i, :], in0=cv[:, i, :], in1=stat[:, 1, :], op=ALU.mult), waits=[a])
                    P.op("scalar", lambda e, i=i: e.activation(out=big[:, i, :], in_=cv[:, i, :], func=AF.Silu,
                                                               scale=pv[:, O_CLG + i:O_CLG + i + 1], bias=pv[:, O_CLB + i:O_CLB + i + 1]),
                         waits=[a] + big_w)
                thn = atok()
                for dc in range(16):
                    b1, t1_ = wgroup(8, lambda k: big[:, k, :], [thn])
                    b2, t2_ = wgroup(8, lambda k: big[:, 24 + k, :], [to])
                    b3, t3_ = wgroup(16, lambda k: uT[:, k, :], tu)
                    b4, t4_ = wgroup(16, lambda k: uT[:, k, :], tu)
                    i3 = tmr.next()
                    s3 = P.op("scalar", lambda e, b3=b3, i3=i3: e.activation(out=tmp[:, i3, :], in_=ps[:, b3, :], func=AF.Sigmoid),
                              waits=[t3_] + tmr.free[i3])
                    mmb.free[b3] = [s3]
                    i4 = tmr.next()
                    s4 = P.op("scalar", lambda e, b4=b4, i4=i4: e.activation(out=tmp[:, i4, :], in_=ps[:, b4, :], func=AF.Sigmoid),
                              waits=[t4_] + tmr.free[i4])
                    mmb.free[b4] = [s4]
                    a1 = P.op("vector", lambda e, b1=b1, i3=i3, dc=dc: e.scalar_tensor_tensor(
                        out=tmp[:, i3, :], in0=ps[:, b1, :], scalar=pv[:, O_BCO + dc:O_BCO + dc + 1], in1=tmp[:, i3, :],
                        op0=ALU.add, op1=ALU.mult), waits=[t1_, s3])
                    mmb.free[b1] = [a1]
                    a2 = P.op("vector", lambda e, b2=b2, i4=i4: e.tensor_tensor(
                        out=tmp[:, i4, :], in0=ps[:, b2, :], in1=tmp[:, i4, :], op=ALU.mult), waits=[t2_, s4])
                    mmb.free[b2] = [a2]
                    a3 = P.op("vector", lambda e, i3=i3, i4=i4, dc=dc: e.tensor_tensor(
                        out=big[:, 8 + dc, :], in0=tmp[:, i3, :], in1=tmp[:, i4, :], op=ALU.add), waits=[a1, a2] + big_w)
                    tmr.free[i3] = [a3]
                    tmr.free[i4] = [a3]
                tm_ = vtok()
                for dc in range(16):
                    b, tp_ = wgroup(16, lambda k: big[:, 8 + k, :], [tm_])
                    a = P.op("vector", lambda e, b=b, dc=dc: e.scalar_tensor_tensor(
                        out=xT[:, dc, :], in0=ps[:, b, :], scalar=par[:, P_G1S + dc:P_G1S + dc + 1], in1=xT[:, dc, :],
                        op0=ALU.mult, op1=ALU.add), waits=[tp_, tx, atok()])
                    mmb.free[b] = [a]
                tz = vtok()
                tst = ln_stats(16, lambda c: xT[:, c, :], ones2k, EPS_DN, [tz])
                for c in range(16):
                    a = P.op("vector", lambda e, c=c: e.tensor_tensor(out=xT[:, c, :], in0=xT[:, c, :], in1=stat[:, 0, :], op=ALU.subtract),
                             waits=[tst, ptok()])
                    a = P.op("vector", lambda e, c=c: e.tensor_tensor(out=xT[:, c, :], in0=xT[:, c, :], in1=stat[:, 1, :], op=ALU.mult), waits=[a])
                    a5 = P.op("scalar", lambda e, c=c: e.activation(out=uT[:, c, :], in_=xT[:, c, :], func=AF.Identity,
                                                                    scale=par[:, P_A2 + c:P_A2 + c + 1], bias=par[:, P_B2 + c:P_B2 + c + 1]),
                              waits=[a, ptok()])
                    P.op("vector", lambda e, c=c: e.tensor_scalar(out=xT[:, c, :], in0=xT[:, c, :], scalar1=pv[:, O_LN1G + c:O_LN1G + c + 1],
                                                                  scalar2=pv[:, O_LN1B + c:O_LN1B + c + 1], op0=ALU.mult, op1=ALU.add),
                         waits=[a, a5])
                tu2 = [vtok(), atok()]
                for j in range(44):
                    ba, ta_ = wgroup(16, lambda k: uT[:, k, :], tu2)
                    bv, tv_ = wgroup(16, lambda k: uT[:, k, :], tu2)
                    ai = abr.next()
                    c0 = P.op("vector", lambda e, ai=ai, j=j: e.tensor_copy(out=abuf[:, ai, 0:2], in_=ahist[:, j, :]), waits=abr.free[ai])
                    c1 = P.op("scalar", lambda e, ba=ba, ai=ai: e.activation(out=abuf[:, ai, 2:514], in_=ps[:, ba, :], func=AF.Copy),
                              waits=[ta_] + abr.free[ai])
                    mmb.free[ba] = [c1]
                    c2 = P.op("vector", lambda e, ai=ai, j=j: e.tensor_copy(out=ahist[:, j, :], in_=abuf[:, ai, 512:514]), waits=[c1, c0])
                    if t == 0:
                        c2 = P.op("vector", lambda e, j=j: e.tensor_scalar(out=ahist[:, j, :], in0=ahist[:, j, :], scalar1=flag,
                                                                           scalar2=None, op0=ALU.mult), waits=[c2])
                    ti = tmr.next()
                    a = P.op("vector", lambda e, ai=ai, ti=ti, j=j: e.tensor_scalar(
                        out=tmp[:, ti, :], in0=abuf[:, ai, 0:512], scalar1=pv[:, O_FW + 3 * j:O_FW + 3 * j + 1],
                        scalar2=pv[:, O_FB + j:O_FB + j + 1], op0=ALU.mult, op1=ALU.add), waits=[c0, c1, c2] + tmr.free[ti])
                    a = P.op("vector", lambda e, ai=ai, ti=ti, j=j: e.scalar_tensor_tensor(
                        out=tmp[:, ti, :], in0=abuf[:, ai, 1:513], scalar=pv[:, O_FW + 3 * j + 1:O_FW + 3 * j + 2], in1=tmp[:, ti, :],
                        op0=ALU.mult, op1=ALU.add), waits=[a])
                    a = P.op("vector", lambda e, ai=ai, ti=ti, j=j: e.scalar_tensor_tensor(
                        out=tmp[:, ti, :], in0=abuf[:, ai, 2:514], scalar=pv[:, O_FW + 3 * j + 2:O_FW + 3 * j + 3], in1=tmp[:, ti, :],
                        op0=ALU.mult, op1=ALU.add), waits=[a])
                    abr.free[ai] = [a]
                    s_ = P.op("scalar", lambda e, ti=ti: e.activation(out=tmp[:, ti, :], in_=tmp[:, ti, :], func=AF.Silu), waits=[a])
                    hh = P.op("vector", lambda e, bv=bv, ti=ti, j=j: e.tensor_tensor(
                        out=big[:, j, :], in0=ps[:, bv, :], in1=tmp[:, ti, :], op=ALU.mult), waits=[tv_, s_, tm_, ptok()] if j < 32 else [tv_, s_])
                    mmb.free[bv] = [hh]
                    tmr.free[ti] = [hh]
                th = vtok()
                if t == 0:
                    continue
                for dc in range(16):
                    b = mmb.next()
                    tpe = None
                    for part, (k0, nk) in enumerate(((0, 16), (16, 16), (32, 12))):
                        s, tl = wget()
                        for k in range(nk):
                            tpe = P.op("tensor", lambda e, b=b, s=s, k=k, k0=k0, part=part, nk=nk: e.matmul(
                                ps[:, b, :], lhsT=wsl[:, s, k, :], rhs=big[:, k0 + k, :],
                                start=(part == 0 and k == 0), stop=(part == 2 and k == nk - 1)),
                                waits=[tl, th] + mmb.free[b], mark=(k == nk - 1))
                        wfree[s] = [tpe]
                    a = P.op("vector", lambda e, b=b, dc=dc: e.scalar_tensor_tensor(
                        out=xT[:, dc, :], in0=ps[:, b, :], scalar=par[:, P_G2S + dc:P_G2S + dc + 1], in1=xT[:, dc, :],
                        op0=ALU.mult, op1=ALU.add), waits=[tpe])
                    mmb.free[b] = [a]
                tz = vtok()
                tst = ln_stats(16, lambda c: xT[:, c, :], ones2k, EPS_DN, [tz])
                for c in range(16):
                    a = P.op("vector", lambda e, c=c: e.tensor_tensor(out=xT[:, c, :], in0=xT[:, c, :], in1=stat[:, 0, :], op=ALU.subtract),
                             waits=[tst, ptok()])
                    a = P.op("vector", lambda e, c=c: e.tensor_tensor(out=xT[:, c, :], in0=xT[:, c, :], in1=stat[:, 1, :], op=ALU.mult), waits=[a])
                    P.op("scalar", lambda e, c=c: e.activation(out=xT[:, c, :], in_=xT[:, c, :], func=AF.Identity,
                                                               scale=pv[:, O_LN2G + c:O_LN2G + c + 1], bias=pv[:, O_LN2B + c:O_LN2B + c + 1]),
                         waits=[a])
                tfin = atok()
                for sub in range(4):
                    for c4 in range(4):
                        b = mmb.next()
                        tpe = None
                        for cq in range(4):
                            c = c4 * 4 + cq
                            tpe = P.op("tensor", lambda e, b=b, cq=cq, c=c, sub=sub: e.transpose(
                                ps[:, b, cq * 128:(cq + 1) * 128], xT[:, c, sub * 128:(sub + 1) * 128], ident[:]),
                                waits=[tfin] + mmb.free[b], mark=(cq == 3))
                        if c4 % 2 == 0:
                            te = P.op("vector", lambda e, b=b, c4=c4: e.tensor_copy(out=ost[:, c4 * 512:(c4 + 1) * 512], in_=ps[:, b, :]),
                                      waits=[tpe] + ost_free)
                        else:
                            te = P.op("scalar", lambda e, b=b, c4=c4: e.activation(out=ost[:, c4 * 512:(c4 + 1) * 512], in_=ps[:, b, :], func=AF.Copy),
                                      waits=[tpe] + ost_free)
                        mmb.free[b] = [te]
                    r0 = (t - 1) * 512 + sub * 128
                    ts = P.dma("sync", outd[r0:r0 + 128, :], ost[:, :], s_out, waits=[vtok(), atok()])
                    ost_free = [ts]
            P.barrier("M")
            P.flush()
    return nc


def _bf16(a):
    return np.asarray(a, dtype=np.float32).astype(ml_dtypes.bfloat16)


def _host_consts():
    ident = np.eye(128, dtype=np.float32)
    perm = np.zeros((32, 32), np.float32)
    for m in range(32):
        perm[(m + 16) % 32, m] = 1.0
    eall = np.zeros((NBLK, NBLK, 128), np.float32)
    for j in range(NBLK):
        eall[j, j, :] = 1.0
    tri = np.zeros((128, 4, 512), np.float32)
    tt = np.arange(128)[:, None]
    for half in range(2):
        for k2 in range(2):
            ql = np.arange(256)[None, :]
            m = (k2 * 128 + tt) > ql
            tri[:, half * 2 + k2, half * 256:(half + 1) * 256] = np.where(m, NEG, 0.0)
    return ident, perm, _bf16(eall.reshape(NBLK, NBLK * 128)), _bf16(tri.reshape(128, 4 * 512))


def kernel(x, c, w_ada, b_ada, w_in, conv_dw_w, conv_dw_b, conv_ln_g, conv_ln_b, w_conv_out, b_conv_out, w_attn_out,
           w_out, ln1_g, ln1_b, w_up, ffn_dw_w, ffn_dw_b, w_down, ln2_g, ln2_b):
    f32 = np.float32
    x = np.asarray(x, f32)
    c = np.asarray(c, f32)
    ident, perm, eall, tri = _host_consts()

    def fm(v, nch):
        return np.ascontiguousarray(np.asarray(v, f32).reshape(nch, 128).T)

    L = 0
    cw = np.asarray(conv_dw_w, f32)[L]
    fw = np.asarray(ffn_dw_w, f32)[L]
    pv_base = np.concatenate([
        fm(ln1_g[L], 16), fm(ln1_b[L], 16), fm(ln2_g[L], 16), fm(ln2_b[L], 16), fm(b_conv_out[L], 16),
        fm(conv_dw_b[L], 8), fm(conv_ln_g[L], 8), fm(conv_ln_b[L], 8),
        np.ascontiguousarray(cw.reshape(31, 8, 128).transpose(2, 1, 0)).reshape(128, 8 * 31),
        np.ascontiguousarray(fw.reshape(3, 44, 128).transpose(2, 1, 0)).reshape(128, 44 * 3),
        fm(ffn_dw_b[L], 44)], axis=1).astype(f32)
    inv_freq = (np.float32(500000.0) ** (-np.arange(0, 32, 2, dtype=f32) / np.float32(32))).astype(f32)
    shared = {
        "w_ada": np.ascontiguousarray(np.asarray(w_ada, f32)[L]), "b_ada": np.ascontiguousarray(np.asarray(b_ada, f32)[L][None, :]),
        "w_in": np.ascontiguousarray(np.asarray(w_in, f32)[L]), "w_conv_out": np.ascontiguousarray(np.asarray(w_conv_out, f32)[L]),
        "w_attn_out": np.ascontiguousarray(np.asarray(w_attn_out, f32)[L]), "w_out": np.ascontiguousarray(np.asarray(w_out, f32)[L]),
        "w_up": np.ascontiguousarray(np.asarray(w_up, f32)[L]), "w_down": np.ascontiguousarray(np.asarray(w_down, f32)[L]),
        "eall": eall, "tri": tri, "ident": ident, "identb": _bf16(ident), "perm": perm,
    }
    in_maps = []
    for core in range(8):
        s, r = core // 4, core % 4
        own0 = r * CH
        others = [q for q in range(4) if q != r]
        pos = np.concatenate([np.arange(own0 - HALO, own0 + CH)] + [np.arange(q * CH, (q + 1) * CH) for q in others])
        valid_tok = pos >= 0
        xa = np.zeros((NALL, D), f32)
        xa[valid_tok] = x[s][pos[valid_tok]]
        posf = np.where(valid_tok, pos, 0).astype(f32)
        ang = posf[None, :] * inv_freq[:, None]
        cs_, sn_ = np.cos(ang).astype(f32), np.sin(ang).astype(f32)
        ropeC = np.concatenate([cs_, cs_], 0)
        ropeS = np.concatenate([-sn_, sn_], 0)
        gblk = pos[::256] // 256
        gblk = np.where(pos[::256] >= 0, gblk, -10)
        pb = np.zeros((18, NBLK), f32)
        val = np.zeros((18, NBLK), f32)
        for row in range(18):
            gb = gblk[row]
            for jj in range(NBLK):
                ok = (jj >= 2) and (gblk[jj] >= 0) and (gblk[jj] < gb)
                val[row, jj] = 1.0 if ok else 0.0
                pb[row, jj] = 0.0 if ok else -1e30
        pvc = np.concatenate([pv_base, np.full((128, 1), 0.0 if r == 0 else 1.0, f32)], axis=1)
        m = dict(shared)
        m.update({
            "xa": xa, "cc": fm(c[s], 16), "pv": np.ascontiguousarray(pvc),
            "ropeC": np.ascontiguousarray(ropeC), "ropeS": np.ascontiguousarray(ropeS),
            "pb": np.ascontiguousarray(np.broadcast_to(pb.reshape(1, -1), (128, 18 * NBLK))),
            "val": np.ascontiguousarray(np.broadcast_to(val.reshape(1, -1), (128, 18 * NBLK))),
        })
        in_maps.append(m)
    nc = build()
    res = run_bass_kernel_spmd(nc, in_maps, core_ids=list(range(8)))
    out = np.zeros((2, SEQ, D), f32)
    for core in range(8):
        s, r = core // 4, core % 4
        out[s, r * CH:(r + 1) * CH] = np.asarray(res.results[core]["out"], f32)
    return out
```

```python
import numpy as np
import ml_dtypes
from contextlib import ExitStack
import concourse.bass as bass
import concourse.mybir as mybir
from concourse.bass_utils import run_bass_kernel_spmd

F32 = mybir.dt.float32
BF16 = mybir.dt.bfloat16
AF = mybir.ActivationFunctionType
ALU = mybir.AluOpType
AX = mybir.AxisListType

D = 2048
SEQ = 16384
CH = 4096
HALO = 512
NOWN = CH + HALO
NALL = SEQ + HALO
NT_ALL = NALL // 512
NT_OWN = NOWN // 512
NBLK = NALL // 256
NKT = NALL // 128
DFF = 5632
ALPHA = 2.0 ** 0.25
EPS_LN = 1e-5
EPS_DN = 1e-5 / (ALPHA * ALPHA)
QSCALE = 128.0 ** -0.5
NEG = -30000.0
ENG = ("sync", "scalar", "vector", "gpsimd", "tensor")

O_LN1G, O_LN1B, O_LN2G, O_LN2B, O_BCO, O_CDB, O_CLG, O_CLB, O_CW, O_FW, O_FB, O_FLAG, NPV = \
    0, 16, 32, 48, 64, 80, 88, 96, 104, 352, 484, 528, 529
P_SC1, P_SH1, P_G1S, P_A2, P_B2, P_G2S, P_TMP, NPAR = 0, 16, 32, 48, 64, 80, 96, 128


class Sem:
    __slots__ = ("h", "v")


class Prog:
    def __init__(self, nc, stack):
        self.nc = nc
        self.stack = stack
        self.q = {e: [] for e in ENG}
        self.waited = {e: {} for e in ENG}
        self.pg = {}
        self.dma_out = []
        self.nsem = 0
        self.abs = {e: [] for e in ENG}
        self.semval = {}
        self.check = False
        self.dry = False

    def sem(self, name):
        s = Sem()
        if self.dry:
            s.h = None
            s.v = 0
            return s
        self.nsem += 1
        s.h = self.stack.enter_context(self.nc.semaphore(f"{name}_{self.nsem}"))
        s.v = 0
        return s

    def new_phase(self, tag):
        self.pg = {e: self.sem(f"pg{tag}{e[:2]}") for e in ("scalar", "vector", "gpsimd", "tensor")}
        self.waited = {e: {} for e in ENG}
        self.dma_out = []

    def _waits(self, eng, waits):
        if self.dry:
            return
        for w in waits:
            if w is None:
                continue
            s, v = w
            if v <= 0 or self.waited[eng].get(id(s), 0) >= v:
                continue
            self.waited[eng][id(s)] = v
            self.abs[eng].append(('w', id(s), v))
            self.q[eng].append(lambda e, s=s, v=v: e.wait_ge(s.h, v))

    def op(self, eng, fn, waits=(), mark=True):
        if self.dry:
            return (self.pg[eng], 0) if mark else None
        self._waits(eng, waits)
        if mark:
            s = self.pg[eng]
            s.v += 1
            self.abs[eng].append(('i', id(s), 1))
            self.q[eng].append(lambda e, fn=fn, s=s: fn(e).then_inc(s.h, 1))
            return (s, s.v)
        self.q[eng].append(lambda e, fn=fn: fn(e))
        return None

    def dma(self, eng, out, in_, sem, waits=(), **kw):
        if self.dry:
            return (sem, 0)
        self._waits(eng, waits)
        sem.v += 16
        self.abs[eng].append(('i', id(sem), 16))
        self.q[eng].append(lambda e, out=out, in_=in_, sem=sem, kw=kw: e.dma_start(out=out, in_=in_, **kw).then_inc(sem.h, 16))
        tok = (sem, sem.v)
        self.dma_out.append(tok)
        return tok

    def barrier(self, tag):
        b = self.sem(f"bar{tag}")
        for e in ENG:
            if e in self.pg and self.pg[e].v > 0:
                self._waits(e, [(self.pg[e], self.pg[e].v)])
        last = {}
        for t in self.dma_out:
            last[id(t[0])] = t
        self._waits("sync", list(last.values()))
        for e in ENG:
            self.abs[e].append(('i', id(b), 1))
            self.q[e].append(lambda en, b=b: en.sem_inc(b.h, 1))
        for e in ENG:
            self.abs[e].append(('w', id(b), len(ENG)))
            self.q[e].append(lambda en, b=b: en.wait_ge(b.h, len(ENG)))

    def simulate(self):
        pc = {e: 0 for e in ENG}
        val = self.semval
        prog = True
        while prog:
            prog = False
            for e in ENG:
                q = self.abs[e]
                while pc[e] < len(q):
                    k, sid, v = q[pc[e]]
                    if k == 'w':
                        if val.get(sid, 0) < v:
                            break
                    else:
                        val[sid] = val.get(sid, 0) + v
                    pc[e] += 1
                    prog = True
        stuck = {e: (pc[e], len(self.abs[e])) for e in ENG if pc[e] < len(self.abs[e])}
        self.abs = {e: [] for e in ENG}
        return stuck

    def flush(self):
        if self.check:
            st = self.simulate()
            print("SIM stuck:", st)
        with self.nc.Block() as block:
            for name in ENG:
                ops = self.q[name]
                if not ops:
                    continue

                def run(e, ops=ops):
                    for f in ops:
                        f(e)
                getattr(block, name)(run)
        self.q = {e: [] for e in ENG}


class Ring:
    def __init__(self, P, name, slots):
        self.slots = slots
        self.sems = [P.sem(f"{name}{i}") for i in range(len(slots))]
        self.free = [[] for _ in slots]
        self.i = 0

    def next(self):
        i = self.i % len(self.slots)
        self.i += 1
        return i


class BankRing:
    def __init__(self, banks):
        self.banks = list(banks)
        self.free = {b: [] for b in banks}
        self.i = 0

    def next(self):
        b = self.banks[self.i % len(self.banks)]
        self.i += 1
        return b


def build():
    nc = bass.Bass("TRN2", target_bir_lowering=False)

    def din(name, shape, dt):
        return nc.dram_tensor(name, shape, dt, kind="ExternalInput").ap()

    def dint(name, shape, dt):
        return nc.dram_tensor(name, shape, dt, kind="Internal").ap()

    xa = din("xa", [NALL, D], F32)
    cc = din("cc", [128, 16], F32)
    w_ada = din("w_ada", [D, 6 * D], F32)
    b_ada = din("b_ada", [1, 6 * D], F32)
    w_in = din("w_in", [D, 9216], F32)
    w_co = din("w_conv_out", [1024, D], F32)
    w_ao = din("w_attn_out", [1024, D], F32)
    w_o = din("w_out", [D, D], F32)
    w_up = din("w_up", [D, 2 * DFF], F32)
    w_dn = din("w_down", [DFF, D], F32)
    pvd = din("pv", [128, NPV], F32)
    ropeC = din("ropeC", [32, NALL], F32)
    ropeS = din("ropeS", [32, NALL], F32)
    pbd = din("pb", [128, 18 * NBLK], F32)
    vald = din("val", [128, 18 * NBLK], F32)
    ealld = din("eall", [NBLK, NBLK * 128], BF16)
    trid = din("tri", [128, 4 * 512], BF16)
    identd = din("ident", [128, 128], F32)
    identbd = din("identb", [128, 128], BF16)
    permd = din("perm", [32, 32], F32)
    outd = nc.dram_tensor("out", [CH, D], F32, kind="ExternalOutput").ap()

    ws_in = dint("ws_in", [72, 128, 16, 128], BF16)
    ws_co = dint("ws_co", [16, 128, 8, 128], BF16)
    ws_ao = dint("ws_ao", [16, 128, 8, 128], BF16)
    ws_o = dint("ws_o", [16, 128, 16, 128], BF16)
    ws_up = dint("ws_up", [88, 128, 16, 128], BF16)
    ws_dn = dint("ws_dn", [16, 128, 44, 128], BF16)
    ksc = dint("ksc", [8, 128, NALL], BF16)
    vsc = dint("vsc", [8, 128, NKT, 128], BF16)
    qsc = dint("qsc", [8, 128, NOWN], BF16)
    osc = dint("osc", [8, 128, NOWN], BF16)
    modsc = dint("modsc", [1, 6 * D], F32)

    with ExitStack() as gst:
        P = Prog(nc, gst)

        def gsb(name, shape, dt):
            return gst.enter_context(nc.sbuf_tensor(name, shape, dt))

        ps = gst.enter_context(nc.psum_tensor("ps", [128, 8, 512], F32))
        pv = gsb("pvs", [128, NPV], F32)
        par = gsb("par", [128, NPAR], F32)
        ident = gsb("ident_s", [128, 128], F32)
        identb = gsb("identb_s", [128, 128], BF16)
        onesb = gsb("onesb", [128, 128], BF16)
        ones1k = gsb("ones1k", [128, 128], F32)
        onesf = gsb("onesf", [128, 128], F32)
        ones2k = gsb("ones2k", [128, 128], F32)
        perm = gsb("perm_s", [32, 32], F32)
        kmT = gsb("kmT", [128, 8, NBLK], BF16)

        with ExitStack() as st:
            def sb(name, shape, dt):
                return st.enter_context(nc.sbuf_tensor(name, shape, dt))
            cact = sb("cact", [128, 16], F32)
            wa = sb("wa", [128, 2, 4096], F32)
            bada = sb("bada", [1, 6 * D], F32)
            modrow = sb("modrow", [1, 4096], F32)
            modT = sb("modT", [128, 96], F32)
            stg = sb("stg", [128, 6, 1024], F32)
            stb = sb("stb", [128, 6, 1024], BF16)
            P.new_phase("a")
            sc = P.sem("ldc")
            P.dma("sync", pv[:], pvd, sc)
            P.dma("sync", ident[:], identd, sc)
            P.dma("sync", identb[:], identbd, sc)
            P.dma("sync", perm[:], permd, sc)
            P.dma("sync", cact[:], cc, sc)
            tconst = P.dma("sync", bada[:], b_ada, sc)
            P.op("vector", lambda e: e.memset(onesb[:], 1.0))
            P.op("vector", lambda e: e.memset(ones1k[:], 1.0 / 1024.0))
            P.op("vector", lambda e: e.memset(onesf[:], 1.0))
            P.op("vector", lambda e: e.memset(ones2k[:], 1.0 / 2048.0))
            tca = P.op("scalar", lambda e: e.activation(out=cact[:], in_=cact[:], func=AF.Silu), waits=[tconst])
            wring = Ring(P, "wa", [wa[:, 0, :], wa[:, 1, :]])
            smod = P.sem("modw")
            ev = None
            store = None
            for p in range(3):
                tk = None
                for k in range(16):
                    i = wring.next()
                    tl = P.dma("sync", wring.slots[i], w_ada[k * 128:(k + 1) * 128, p * 4096:(p + 1) * 4096],
                               wring.sems[i], waits=wring.free[i])
                    for b in range(8):
                        tk = P.op("tensor", lambda e, b=b, i=i, k=k: e.matmul(
                            ps[0:1, b, :], lhsT=cact[:, k:k + 1], rhs=wring.slots[i][:, b * 512:(b + 1) * 512],
                            start=(k == 0), stop=(k == 15)), waits=[tl, tca, ev], mark=(b == 7))
                    wring.free[i] = [tk]
                for b in range(8):
                    ev = P.op("vector", lambda e, b=b, p=p: e.tensor_tensor(
                        out=modrow[0:1, b * 512:(b + 1) * 512], in0=ps[0:1, b, :],
                        in1=bada[0:1, p * 4096 + b * 512:p * 4096 + (b + 1) * 512], op=ALU.add),
                        waits=[tk, tconst, store])
                tt_ = None
                for c in range(32):
                    tt_ = P.op("tensor", lambda e, c=c: e.transpose(ps[:, 7, c:c + 1], modrow[0:1, c * 128:(c + 1) * 128],
                                                                     ident[0:1, 0:1]), waits=[ev, tconst])
                ev = P.op("vector", lambda e, p=p: e.tensor_copy(out=modT[:, p * 32:(p + 1) * 32], in_=ps[:, 7, 0:32]), waits=[tt_])
                store = ev
            tmod = ev
            w0 = [tmod, tconst]
            P.op("vector", lambda e: e.tensor_scalar(out=par[:, P_SC1:P_SC1 + 16], in0=modT[:, 16:32], scalar1=1.0,
                                                     scalar2=None, op0=ALU.add), waits=w0)
            P.op("vector", lambda e: e.tensor_copy(out=par[:, P_SH1:P_SH1 + 16], in_=modT[:, 0:16]))
            P.op("vector", lambda e: e.tensor_scalar(out=par[:, P_G1S:P_G1S + 16], in0=modT[:, 32:48], scalar1=1.0,
                                                     scalar2=1.0 / ALPHA, op0=ALU.add, op1=ALU.mult))
            P.op("vector", lambda e: e.tensor_scalar(out=par[:, P_G2S:P_G2S + 16], in0=modT[:, 80:96], scalar1=1.0,
                                                     scalar2=1.0 / ALPHA, op0=ALU.add, op1=ALU.mult))
            t1 = P.op("vector", lambda e: e.tensor_scalar(out=par[:, P_TMP:P_TMP + 16], in0=modT[:, 64:80], scalar1=1.0,
                                                          scalar2=None, op0=ALU.add))
            t2 = P.op("vector", lambda e: e.tensor_tensor(out=par[:, P_A2:P_A2 + 16], in0=pv[:, O_LN1G:O_LN1G + 16],
                                                          in1=par[:, P_TMP:P_TMP + 16], op=ALU.mult), waits=[t1])
            t3 = P.op("vector", lambda e: e.tensor_tensor(out=par[:, P_B2:P_B2 + 16], in0=pv[:, O_LN1B:O_LN1B + 16],
                                                          in1=par[:, P_TMP:P_TMP + 16], op=ALU.mult), waits=[t1])
            P.op("vector", lambda e: e.tensor_tensor(out=par[:, P_B2:P_B2 + 16], in0=par[:, P_B2:P_B2 + 16],
                                                     in1=modT[:, 48:64], op=ALU.add), waits=[t3])
            sring = Ring(P, "stg", [stg[:, i, :] for i in range(6)])
            bring = Ring(P, "stb", [stb[:, i, :] for i in range(6)])
            ci = 0
            for (src, dst, K, N) in ((w_in, ws_in, D, 9216), (w_co, ws_co, 1024, D), (w_ao, ws_ao, 1024, D),
                                     (w_o, ws_o, D, D), (w_up, ws_up, D, 2 * DFF), (w_dn, ws_dn, DFF, D)):
                for kk in range(K // 128):
                    for cs in range(N // 1024):
                        i = sring.next()
                        j = bring.next()
                        tl = P.dma("sync", sring.slots[i], src[kk * 128:(kk + 1) * 128, cs * 1024:(cs + 1) * 1024],
                                   sring.sems[i], waits=sring.free[i])
                        eng = ("vector", "gpsimd")[ci % 2]
                        ci += 1
                        tcst = P.op(eng, lambda e, i=i, j=j: e.tensor_copy(out=bring.slots[j], in_=sring.slots[i]),
                                    waits=[tl] + bring.free[j])
                        sring.free[i] = [tcst]
                        ts = P.dma("scalar", dst[cs * 8:(cs + 1) * 8, :, kk, :].rearrange("c p d -> p c d"),
                                   bring.slots[j].rearrange("p (c d) -> p c d", d=128), bring.sems[j], waits=[tcst])
                        bring.free[j] = [ts]
            P.barrier("a")
            P.flush()

        with ExitStack() as st:
            def sb(name, shape, dt):
                return st.enter_context(nc.sbuf_tensor(name, shape, dt))
            wq = sb("wq", [128, 8, 16, 128], BF16)
            wk = sb("wk", [128, 8, 16, 128], BF16)
            wv = sb("wv", [128, 8, 16, 128], BF16)
            xin = sb("xin", [128, 4, D], F32)
            uT = sb("uT", [128, 2, 16, 512], BF16)
            kf = sb("kf", [128, 3, 512], F32)
            r1 = sb("r1", [32, 3, 512], F32)
            r2 = sb("r2", [32, 3, 512], F32)
            kbf = sb("kbf", [128, 4, 512], BF16)
            ctb = sb("ctb", [32, 2, 512], F32)
            stb2 = sb("stb2", [32, 2, 512], F32)
            vbf = sb("vbf", [128, 3, 1024], BF16)
            kms = sb("kms", [128, 8, NBLK], F32)
            P.new_phase("k")
            sw = P.sem("ldw")
            P.dma("sync", wq[:], ws_in[16:24].rearrange("c p k d -> p c k d"), sw)
            P.dma("sync", wk[:], ws_in[24:32].rearrange("c p k d -> p c k d"), sw)
            tw = P.dma("sync", wv[:], ws_in[32:40].rearrange("c p k d -> p c k d"), sw)
            xring = Ring(P, "xin", [xin[:, i, :] for i in range(4)])
            tpb = BankRing([0, 1, 2])
            prb = BankRing([3, 4, 5])
            swb = BankRing([6, 7])
            kfr = Ring(P, "kf", [0, 1, 2])
            kbr = Ring(P, "kbf", [kbf[:, i, :] for i in range(4)])
            vbr = Ring(P, "vbf", [vbf[:, i, :] for i in range(3)])
            ropr = Ring(P, "rope", [0, 1])
            uT_free = [[], []]
            evi = 0
            for t in range(NT_ALL):
                tok0 = t * 512
                u = t % 2
                xt = []
                for sub in range(4):
                    i = xring.next()
                    tl = P.dma("sync", xring.slots[i], xa[tok0 + sub * 128:tok0 + (sub + 1) * 128, :], xring.sems[i],
                               waits=xring.free[i])
                    xt.append((i, tl))
                ri = ropr.next()
                P.dma("gpsimd", ctb[:, ri, :], ropeC[:, tok0:tok0 + 512], ropr.sems[ri], waits=ropr.free[ri])
                trope = P.dma("gpsimd", stb2[:, ri, :], ropeS[:, tok0:tok0 + 512], ropr.sems[ri], waits=ropr.free[ri])
                uready = []
                tpe = None
                for c in range(16):
                    b = tpb.next()
                    for sub in range(4):
                        i, tl = xt[sub]
                        tpe = P.op("tensor", lambda e, b=b, sub=sub, i=i, c=c: e.transpose(
                            ps[:, b, sub * 128:(sub + 1) * 128], xin[:, i, c * 128:(c + 1) * 128], ident[:]),
                            waits=[tl] + tpb.free[b], mark=(sub == 3))
                    if evi % 2 == 0:
                        te = P.op("scalar", lambda e, b=b, c=c, u=u: e.activation(
                            out=uT[:, u, c, :], in_=ps[:, b, :], func=AF.Identity,
                            scale=par[:, P_SC1 + c:P_SC1 + c + 1], bias=par[:, P_SH1 + c:P_SH1 + c + 1]),
                            waits=[tpe] + uT_free[u])
                    else:
                        te = P.op("vector", lambda e, b=b, c=c, u=u: e.tensor_scalar(
                            out=uT[:, u, c, :], in0=ps[:, b, :], scalar1=par[:, P_SC1 + c:P_SC1 + c + 1],
                            scalar2=par[:, P_SH1 + c:P_SH1 + c + 1], op0=ALU.mult, op1=ALU.add),
                            waits=[tpe] + uT_free[u])
                    evi += 1
                    tpb.free[b] = [te]
                    uready = (uready + [te])[-2:]
                for (i, tl) in xt:
                    xring.free[i] = [tpe]
                last_rope = None
                for kind in (("k", "q") if t < NT_OWN else ("k",)):
                    wmat = wk if kind == "k" else wq
                    for h in range(8):
                        b = prb.next()
                        tpe = None
                        for k in range(16):
                            tpe = P.op("tensor", lambda e, b=b, h=h, k=k, u=u, wmat=wmat: e.matmul(
                                ps[:, b, :], lhsT=wmat[:, h, k, :], rhs=uT[:, u, k, :], start=(k == 0), stop=(k == 15)),
                                waits=uready + [tw] + prb.free[b], mark=(k == 15))
                        fi = kfr.next()
                        ta = P.op("scalar", lambda e, b=b, fi=fi: e.activation(out=kf[:, fi, :], in_=ps[:, b, :], func=AF.Copy),
                                  waits=[tpe] + kfr.free[fi])
                        prb.free[b] = [ta]
                        sbk = swb.next()
                        tsw = P.op("tensor", lambda e, sbk=sbk, fi=fi: e.matmul(
                            ps[0:32, sbk, :], lhsT=perm[:, :], rhs=kf[0:32, fi, :], start=True, stop=True),
                            waits=[ta] + swb.free[sbk])
                        ta1 = P.op("vector", lambda e, fi=fi, ri=ri: e.tensor_tensor(
                            out=r1[:, fi, :], in0=kf[0:32, fi, :], in1=ctb[:, ri, :], op=ALU.mult), waits=[ta, trope])
                        ta2 = P.op("vector", lambda e, fi=fi, ri=ri, sbk=sbk: e.tensor_tensor(
                            out=r2[:, fi, :], in0=ps[0:32, sbk, :], in1=stb2[:, ri, :], op=ALU.mult), waits=[tsw])
                        swb.free[sbk] = [ta2]
                        ta3 = P.op("vector", lambda e, fi=fi: e.tensor_tensor(
                            out=kf[0:32, fi, :], in0=r1[:, fi, :], in1=r2[:, fi, :], op=ALU.add), waits=[ta1, ta2, tsw])
                        last_rope = ta3
                        ki = kbr.next()
                        if kind == "k":
                            ta4 = P.op("vector", lambda e, fi=fi, h=h, t=t: e.tensor_reduce(
                                out=kms[:, h, 2 * t:2 * t + 2], in_=kf[:, fi, :].rearrange("p (a b) -> p a b", b=256),
                                axis=AX.X, op=ALU.add), waits=[ta3])
                            tcs = P.op("gpsimd", lambda e, fi=fi, ki=ki: e.tensor_copy(out=kbf[:, ki, :], in_=kf[:, fi, :]),
                                       waits=[ta3] + kbr.free[ki])
                            kfr.free[fi] = [tcs, ta4]
                            ts = P.dma("sync", ksc[h, :, tok0:tok0 + 512], kbf[:, ki, :], kbr.sems[ki], waits=[tcs])
                        else:
                            tcs = P.op("gpsimd", lambda e, fi=fi, ki=ki: e.tensor_scalar(
                                out=kbf[:, ki, :], in0=kf[:, fi, :], scalar1=QSCALE, scalar2=None, op0=ALU.mult),
                                waits=[ta3] + kbr.free[ki])
                            kfr.free[fi] = [tcs]
                            ts = P.dma("sync", qsc[h, :, tok0:tok0 + 512], kbf[:, ki, :], kbr.sems[ki], waits=[tcs])
                        kbr.free[ki] = [ts]
                ropr.free[ri] = [last_rope]
                for sub in range(4):
                    vi = vbr.next()
                    tvs = []
                    for half in range(2):
                        b = prb.next()
                        tpe = None
                        for k in range(16):
                            tpe = P.op("tensor", lambda e, b=b, k=k, u=u, sub=sub, half=half: e.matmul(
                                ps[:, b, :].rearrange("p (c d) -> p c d", d=128),
                                lhsT=uT[:, u, k, sub * 128:(sub + 1) * 128], rhs=wv[:, half * 4:(half + 1) * 4, k, :],
                                start=(k == 0), stop=(k == 15)), waits=uready + [tw] + prb.free[b], mark=(k == 15))
                        if half == 0:
                            te = P.op("scalar", lambda e, b=b, vi=vi: e.activation(
                                out=vbf[:, vi, 0:512], in_=ps[:, b, :], func=AF.Copy), waits=[tpe] + vbr.free[vi])
                        else:
                            te = P.op("vector", lambda e, b=b, vi=vi: e.tensor_copy(
                                out=vbf[:, vi, 512:1024], in_=ps[:, b, :]), waits=[tpe] + vbr.free[vi])
                        prb.free[b] = [te]
                        tvs.append(te)
                    ts = P.dma("scalar", vsc[:, :, t * 4 + sub, :].rearrange("h p d -> p h d"),
                               vbf[:, vi, :].rearrange("p (h d) -> p h d", d=128), vbr.sems[vi], waits=tvs)
                    vbr.free[vi] = [ts]
                uT_free[u] = [tpe]
            P.op("vector", lambda e: e.tensor_scalar(out=kmT[:], in0=kms[:], scalar1=1.0 / 256.0, scalar2=None, op0=ALU.mult),
                 waits=[(P.pg["vector"], P.pg["vector"].v)])
            P.barrier("k")
            P.flush()

        with ExitStack() as st:
            def sb(name, shape, dt):
                return st.enter_context(nc.sbuf_tensor(name, shape, dt))
            KT = sb("KT", [128, 2, NALL], BF16)
            VV = sb("VV", [128, 2, NKT, 128], BF16)
            eall = sb("eall_s", [NBLK, NBLK, 128], BF16)
            tri = sb("tri_s", [128, 4, 512], BF16)
            pbs = sb("pbs", [128, 18, NBLK], F32)
            vals = sb("vals", [128, 18, NBLK], F32)
            qT = sb("qT", [128, 2, 512], BF16)
            gm = sb("gm", [128, 4, NBLK], F32)
            ge = sb("ge", [128, 4, NBLK], F32)
            top8 = sb("top8", [128, 4, 8], F32)
            selT = sb("selT", [NBLK, 2, 512], BF16)
            pT = sb("pT", [128, 8, 512], BF16)
            acc = sb("acc", [128, 2, 512], F32)
            rinv = sb("rinv", [128, 512], F32)
            oT = sb("oT", [128, 2, 512], BF16)
            P.new_phase("A")
            sc = P.sem("ldA")
            P.dma("sync", eall[:], ealld.rearrange("p (j t) -> p j t", t=128), sc)
            P.dma("sync", tri[:], trid.rearrange("p (j t) -> p j t", t=512), sc)
            P.dma("sync", pbs[:], pbd.rearrange("p (j t) -> p j t", t=NBLK), sc)
            tcon = P.dma("sync", vals[:], vald.rearrange("p (j t) -> p j t", t=NBLK), sc)
            kvr = Ring(P, "kv", [0, 1])
            qr = Ring(P, "qT", [0, 1])
            sbk = BankRing([0, 1, 2])
            ptr = Ring(P, "pT", list(range(8)))
            orr = Ring(P, "oT", [0, 1])
            O_B, R_B, G_B, ST_B = 3, 4, 5, 6
            S = {"or_free": [], "g_free": [], "st_free": [], "gi": 0, "last_pe": None}
            selT_free = [[], []]
            ge_free = [[], [], [], []]
            acc_free = [[], []]
            kv_tok = {}
            kv_slot = {}

            def load_kv(h):
                kv = kvr.next()
                kv_slot[h] = kv
                tkv = None
                for pc in range(4):
                    a0_, a1_ = pc * (NALL // 4), (pc + 1) * (NALL // 4)
                    P.dma("gpsimd", KT[:, kv, a0_:a1_], ksc[h, :, a0_:a1_], kvr.sems[kv], waits=kvr.free[kv])
                for pc in range(4):
                    a0_, a1_ = pc * (NKT // 4), (pc + 1) * (NKT // 4)
                    tkv = P.dma("gpsimd", VV[:, kv, a0_:a1_, :], vsc[h, :, a0_:a1_, :], kvr.sems[kv], waits=kvr.free[kv])
                kv_tok[h] = tkv

            G = {}

            QL = {}

            def qload(idx):
                h, mt = idx // NT_OWN, idx % NT_OWN
                qi = qr.next()
                tq = P.dma("sync", qT[:, qi, :], qsc[h, :, mt * 512:(mt + 1) * 512], qr.sems[qi], waits=qr.free[qi])
                QL[idx] = (qi, tq)

            GA = {}

            def gatingA(idx):
                h, mt = idx // NT_OWN, idx % NT_OWN
                qi, tq = QL[idx]
                d6s = []
                for sub in range(4):
                    row = 2 * mt + sub // 2
                    g = sub
                    tg = P.op("tensor", lambda e, qi=qi, sub=sub, h=h: e.matmul(
                        ps[:, G_B, sub * 128:sub * 128 + NBLK], lhsT=qT[:, qi, sub * 128:(sub + 1) * 128],
                        rhs=kmT[:, h, :], start=True, stop=True), waits=[tq] + S["g_free"])
                    d1 = P.op("vector", lambda e, g=g, sub=sub, row=row: e.tensor_tensor(
                        out=gm[:, g, :], in0=ps[:, G_B, sub * 128:sub * 128 + NBLK], in1=pbs[:, row, :], op=ALU.add),
                        waits=[tg, tcon])
                    S["g_free"] = [d1]
                    d2 = P.op("vector", lambda e, g=g: e.max(out=top8[:, g, :], in_=gm[:, g, :]), waits=[d1])
                    d3 = P.op("vector", lambda e, g=g: e.tensor_scalar(
                        out=ge[:, g, :], in0=gm[:, g, :], scalar1=top8[:, g, 2:3], scalar2=None, op0=ALU.is_ge),
                        waits=[d2] + ge_free[g])
                    d4 = P.op("vector", lambda e, g=g, row=row: e.tensor_tensor(
                        out=ge[:, g, :], in0=ge[:, g, :], in1=vals[:, row, :], op=ALU.mult), waits=[d3])
                    d5 = P.op("vector", lambda e, g=g: e.tensor_scalar(
                        out=ge[:, g, :], in0=ge[:, g, :], scalar1=-NEG, scalar2=NEG, op0=ALU.mult, op1=ALU.add),
                        waits=[d4])
                    d6 = P.op("vector", lambda e, g=g, row=row: e.memset(ge[:, g, row:row + 1], 0.0), waits=[d5])
                    d6s.append(d6)
                GA[idx] = d6s

            def gatingB(idx):
                qi, tq = QL.pop(idx)
                d6s = GA.pop(idx)
                sp = idx % 2
                tts = []
                for sub in range(4):
                    g = sub
                    tt = P.op("tensor", lambda e, g=g, sub=sub: e.transpose(
                        ps[0:NBLK, ST_B, sub * 128:(sub + 1) * 128], ge[:, g, :], ident[:]), waits=[d6s[sub]] + S["st_free"])
                    ge_free[g] = [tt]
                    tts.append(tt)
                tsel = P.op("vector", lambda e, sp=sp: e.tensor_copy(out=selT[:, sp, :], in_=ps[0:NBLK, ST_B, :]),
                            waits=tts + selT_free[sp])
                S["st_free"] = [tsel]
                G[idx] = (qi, tq, sp, tsel)

            NQ = 8 * NT_OWN
            load_kv(0)
            load_kv(1)
            qload(0)
            gatingA(0)
            gatingB(0)
            for idx in range(NQ):
                h, mt = idx // NT_OWN, idx % NT_OWN
                kv = kv_slot[h]
                tkv = kv_tok[h]
                qi, tq, sp, tsel = G.pop(idx)
                ai = idx % 2
                own = list(range(0, 2)) if mt == 0 else list(range(2, 2 * mt + 2))
                blocks = own + list(range(18, NBLK))
                tiles = [(jj, k2) for jj in blocks for k2 in range(2)]
                n = len(tiles)
                Dp = 2
                ptoks = [None] * n
                pslot = [None] * n
                tadd = None
                if idx + 1 < NQ:
                    qload(idx + 1)
                for it in range(n + Dp):
                    if it == n - 24 and idx + 1 < NQ:
                        gatingA(idx + 1)
                    if it == n - 6 and idx + 1 < NQ:
                        gatingB(idx + 1)
                    if it < n:
                        jj, k2 = tiles[it]
                        kti = jj * 2 + k2
                        b = sbk.next()
                        diag = (jj == 2 * mt) or (jj == 2 * mt + 1)
                        P.op("tensor", lambda e, b=b, kv=kv, kti=kti, qi=qi: e.matmul(
                            ps[:, b, :], lhsT=KT[:, kv, kti * 128:(kti + 1) * 128], rhs=qT[:, qi, :], start=True, stop=False),
                            waits=[tq, tkv] + sbk.free[b], mark=False)
                        tpe = P.op("tensor", lambda e, b=b, jj=jj, sp=sp, diag=diag: e.matmul(
                            ps[:, b, :], lhsT=eall[:, jj, :], rhs=selT[:, sp, :], start=False, stop=(not diag)),
                            waits=[tsel, tcon], mark=(not diag))
                        if diag:
                            tix = (jj - 2 * mt) * 2 + k2
                            tpe = P.op("tensor", lambda e, b=b, tix=tix: e.matmul(
                                ps[:, b, :], lhsT=identb[:, :], rhs=tri[:, tix, :], start=False, stop=True))
                        pi = ptr.next()
                        pslot[it] = pi
                        tex = P.op("scalar", lambda e, b=b, pi=pi: e.activation(out=pT[:, pi, :], in_=ps[:, b, :], func=AF.Exp),
                                   waits=[tpe] + ptr.free[pi])
                        sbk.free[b] = [tex]
                        ptoks[it] = tex
                    j = it - Dp
                    if j >= 0:
                        jj, k2 = tiles[j]
                        kti = jj * 2 + k2
                        pi = pslot[j]
                        tpv = P.op("tensor", lambda e, kv=kv, kti=kti, pi=pi, j=j, n=n: e.matmul(
                            ps[:, O_B, :], lhsT=VV[:, kv, kti, :], rhs=pT[:, pi, :], start=(j == 0), stop=(j == n - 1)),
                            waits=[ptoks[j]] + (S["or_free"] if j == 0 else []))
                        if j == 0:
                            tadd = P.op("vector", lambda e, pi=pi, ai=ai: e.tensor_copy(out=acc[:, ai, :], in_=pT[:, pi, :]),
                                        waits=[ptoks[j]] + acc_free[ai])
                        else:
                            tadd = P.op("vector", lambda e, pi=pi, ai=ai: e.tensor_tensor(
                                out=acc[:, ai, :], in0=acc[:, ai, :], in1=pT[:, pi, :], op=ALU.add), waits=[ptoks[j], tadd])
                        ptr.free[pi] = [tpv, tadd]
                        S["last_pe"] = tpv
                last_pe = S["last_pe"]
                tr = P.op("tensor", lambda e, ai=ai: e.matmul(ps[:, R_B, :], lhsT=onesf[:, :], rhs=acc[:, ai, :], start=True, stop=True),
                          waits=[tadd] + S["or_free"])
                acc_free[ai] = [tr]
                qr.free[qi] = [tr]
                selT_free[sp] = [tr]
                oi = orr.next()
                n1 = P.op("vector", lambda e: e.reciprocal(out=rinv[:], in_=ps[:, R_B, :]), waits=[tr])
                n2 = P.op("vector", lambda e, oi=oi: e.tensor_tensor(out=oT[:, oi, :], in0=ps[:, O_B, :], in1=rinv[:], op=ALU.mult),
                          waits=[n1, last_pe] + orr.free[oi])
                S["or_free"] = [n2]
                ts = P.dma("sync", osc[h, :, mt * 512:(mt + 1) * 512], oT[:, oi, :], orr.sems[oi], waits=[n2])
                orr.free[oi] = [ts]
                if mt == NT_OWN - 1:
                    kvr.free[kv] = [tr]
                    if h + 2 < 8:
                        load_kv(h + 2)
            P.barrier("A")
            P.flush()

        with ExitStack() as st:
            def sb(name, shape, dt):
                return st.enter_context(nc.sbuf_tensor(name, shape, dt))
            xin = sb("xin2", [128, 2, D], F32)
            ost = sb("ost", [128, D], F32)
            xT = sb("xT", [128, 16, 512], F32)
            uA = sb("uA", [128, 16, 512], BF16)
            uB = sb("uB", [128, 16, 512], BF16)
            big = sb("big", [128, 44, 512], BF16)
            hcb = sb("hcb", [128, 2, 542], F32)
            hhist = sb("hhist", [128, 8, 30], F32)
            cv = sb("cv", [128, 8, 512], F32)
            tmp = sb("tmp", [128, 4, 512], F32)
            stat = sb("stat", [128, 4, 512], F32)
            abuf = sb("abuf", [128, 2, 514], F32)
            ahist = sb("ahist", [128, 44, 2], F32)
            NSL = 6
            wsl = sb("wsl", [128, NSL, 16, 128], BF16)
            WS = {"in": ws_in, "co": ws_co, "ao": ws_ao, "o": ws_o, "up": ws_up, "dn": ws_dn}

            def emit_mf(reqs, dry):
                P.dry = dry
                P.new_phase("M")
                rec = []
                wsems = [P.sem(f"ws{i}") for i in range(NSL)]
                wfree = [[] for _ in range(NSL)]
                wstate = {"issued": 0, "next": 0, "toks": {}}

                def wget(req):
                    if dry:
                        rec.append(req)
                        return 0, None
                    i = wstate["next"]
                    assert reqs[i] == req, (i, reqs[i], req)
                    wstate["next"] += 1
                    lim = min(len(reqs), i + NSL)
                    while wstate["issued"] < lim:
                        j = wstate["issued"]
                        dn, c, k0, nk = reqs[j]
                        s = j % NSL
                        wstate["toks"][j] = P.dma("sync", wsl[:, s, 0:nk, :], WS[dn][c, :, k0:k0 + nk, :], wsems[s], waits=wfree[s])
                        wstate["issued"] += 1
                    return i % NSL, wstate["toks"].pop(i)

                mmb = BankRing([0, 1, 2, 3, 4, 5])
                tmr = Ring(P, "tmp", [0, 1, 2, 3])
                xring = Ring(P, "xin2", [0, 1])
                hcr = Ring(P, "hcb", [0, 1])
                abr = Ring(P, "abuf", [0, 1])
                s_o = P.sem("ldo")
                s_out = P.sem("stout")
                ST = {"ost_free": []}
                flag = pv[:, O_FLAG:O_FLAG + 1]
                vtok = lambda: (P.pg["vector"], P.pg["vector"].v)
                atok = lambda: (P.pg["scalar"], P.pg["scalar"].v)
                ptok = lambda: (P.pg["tensor"], P.pg["tensor"].v)
                P.op("vector", lambda e: e.memset(hhist[:], 0.0))
                P.op("vector", lambda e: e.memset(ahist[:], 0.0))
                P.op("vector", lambda e: e.memset(hcb[:], 0.0))
                P.op("vector", lambda e: e.memset(abuf[:], 0.0))

                def group(nk, lhs_fn, rhs_fn, extra_waits):
                    b = mmb.next()
                    tpe = None
                    for k in range(nk):
                        tpe = P.op("tensor", lambda e, b=b, k=k: e.matmul(ps[:, b, :], lhsT=lhs_fn(k), rhs=rhs_fn(k),
                                                                         start=(k == 0), stop=(k == nk - 1)),
                                   waits=list(extra_waits) + mmb.free[b], mark=(k == nk - 1))
                    return b, tpe

                def wgroup(req, rhs_fn, extra_waits):
                    s, tl = wget(req)
                    b, tpe = group(req[3], lambda k, s=s: wsl[:, s, k, :], rhs_fn, list(extra_waits) + [tl])
                    wfree[s] = [tpe]
                    return b, tpe

                def ln_stats(nch, src_fn, ones_t, eps, waits):
                    MB, QB = 6, 7
                    tm = None
                    tq = None
                    for c in range(nch):
                        ti = tmr.next()
                        tsq = P.op("scalar", lambda e, c=c, ti=ti: e.activation(out=tmp[:, ti, :], in_=src_fn(c), func=AF.Square),
                                   waits=list(waits) + tmr.free[ti])
                        tm = P.op("tensor", lambda e, c=c: e.matmul(ps[:, MB, :], lhsT=ones_t[:, :], rhs=src_fn(c),
                                                                    start=(c == 0), stop=(c == nch - 1)), waits=list(waits))
                        tq = P.op("tensor", lambda e, c=c, ti=ti: e.matmul(ps[:, QB, :], lhsT=ones_t[:, :], rhs=tmp[:, ti, :],
                                                                          start=(c == 0), stop=(c == nch - 1)), waits=[tsq])
                        tmr.free[ti] = [tq]
                    a = P.op("vector", lambda e: e.tensor_copy(out=stat[:, 0, :], in_=ps[:, MB, :]), waits=[tm, tq])
                    a = P.op("vector", lambda e: e.tensor_tensor(out=stat[:, 2, :], in0=stat[:, 0, :], in1=stat[:, 0, :], op=ALU.mult), waits=[a])
                    a = P.op("vector", lambda e: e.tensor_tensor(out=stat[:, 2, :], in0=ps[:, QB, :], in1=stat[:, 2, :], op=ALU.subtract), waits=[a])
                    a = P.op("vector", lambda e: e.tensor_scalar(out=stat[:, 2, :], in0=stat[:, 2, :], scalar1=0.0, scalar2=eps,
                                                                 op0=ALU.max, op1=ALU.add), waits=[a])
                    a2 = P.op("scalar", lambda e: e.activation(out=stat[:, 2, :], in_=stat[:, 2, :], func=AF.Sqrt), waits=[a])
                    a = P.op("vector", lambda e: e.reciprocal(out=stat[:, 1, :], in_=stat[:, 2, :]), waits=[a2])
                    return a

                FR = {}

                def front(t):
                    tok0 = t * 512
                    w_prev = [ptok(), vtok(), atok()]
                    evi = 0
                    for sub in range(4):
                        i = xring.next()
                        tl = P.dma("sync", xin[:, i, :], xa[tok0 + sub * 128:tok0 + (sub + 1) * 128, :], xring.sems[i],
                                   waits=xring.free[i])
                        tpe = None
                        for c4 in range(4):
                            b = mmb.next()
                            for cq in range(4):
                                c = c4 * 4 + cq
                                tpe = P.op("tensor", lambda e, b=b, cq=cq, i=i, c=c: e.transpose(
                                    ps[:, b, cq * 128:(cq + 1) * 128], xin[:, i, c * 128:(c + 1) * 128], ident[:]),
                                    waits=[tl] + mmb.free[b], mark=(cq == 3))
                            tes = []
                            for cq in range(4):
                                c = c4 * 4 + cq
                                if evi % 2 == 0:
                                    te = P.op("scalar", lambda e, b=b, cq=cq, c=c, sub=sub: e.activation(
                                        out=uA[:, c, sub * 128:(sub + 1) * 128], in_=ps[:, b, cq * 128:(cq + 1) * 128], func=AF.Identity,
                                        scale=par[:, P_SC1 + c:P_SC1 + c + 1], bias=par[:, P_SH1 + c:P_SH1 + c + 1]),
                                        waits=[tpe] + w_prev)
                                else:
                                    te = P.op("vector", lambda e, b=b, cq=cq, c=c, sub=sub: e.tensor_scalar(
                                        out=uA[:, c, sub * 128:(sub + 1) * 128], in0=ps[:, b, cq * 128:(cq + 1) * 128],
                                        scalar1=par[:, P_SC1 + c:P_SC1 + c + 1], scalar2=par[:, P_SH1 + c:P_SH1 + c + 1],
                                        op0=ALU.mult, op1=ALU.add), waits=[tpe] + w_prev)
                                tes.append(te)
                            evi += 1
                            mmb.free[b] = tes[-1:]
                        xring.free[i] = [tpe]
                        yield
                    tu = [vtok(), atok()]
                    for i in range(8):
                        bl, tl_ = wgroup(("in", i, 0, 16), lambda k: uA[:, k, :], tu)
                        bg, tg_ = wgroup(("in", 8 + i, 0, 16), lambda k: uA[:, k, :], tu)
                        ti = tmr.next()
                        hi = hcr.next()
                        s1 = P.op("scalar", lambda e, bg=bg, ti=ti: e.activation(out=tmp[:, ti, :], in_=ps[:, bg, :], func=AF.Sigmoid),
                                  waits=[tg_] + tmr.free[ti])
                        mmb.free[bg] = [s1]
                        h0 = P.op("vector", lambda e, hi=hi, i=i: e.tensor_copy(out=hcb[:, hi, 0:30], in_=hhist[:, i, :]),
                                  waits=hcr.free[hi])
                        h1 = P.op("vector", lambda e, bl=bl, ti=ti, hi=hi: e.tensor_tensor(
                            out=hcb[:, hi, 30:542], in0=ps[:, bl, :], in1=tmp[:, ti, :], op=ALU.mult), waits=[tl_, s1, h0])
                        mmb.free[bl] = [h1]
                        tmr.free[ti] = [h1]
                        h2 = P.op("vector", lambda e, hi=hi, i=i: e.tensor_copy(out=hhist[:, i, :], in_=hcb[:, hi, 512:542]), waits=[h1])
                        if t == 0:
                            h2 = P.op("vector", lambda e, i=i: e.tensor_scalar(out=hhist[:, i, :], in0=hhist[:, i, :], scalar1=flag,
                                                                               scalar2=None, op0=ALU.mult), waits=[h2])
                        yield
                        a = P.op("vector", lambda e, hi=hi, i=i: e.tensor_scalar(
                            out=cv[:, i, :], in0=hcb[:, hi, 0:512], scalar1=pv[:, O_CW + i * 31:O_CW + i * 31 + 1],
                            scalar2=pv[:, O_CDB + i:O_CDB + i + 1], op0=ALU.mult, op1=ALU.add), waits=[h1, h2] + w_prev)
                        for j in range(1, 31):
                            a = P.op("vector", lambda e, hi=hi, i=i, j=j: e.scalar_tensor_tensor(
                                out=cv[:, i, :], in0=hcb[:, hi, j:j + 512], scalar=pv[:, O_CW + i * 31 + j:O_CW + i * 31 + j + 1],
                                in1=cv[:, i, :], op0=ALU.mult, op1=ALU.add), waits=[a])
                            if j % 8 == 0:
                                yield
                        hcr.free[hi] = [a]
                        yield
                    FR[t] = [vtok(), atok(), ptok()]

                for _ in front(0):
                    pass
                for t in range(NT_OWN):
                    tok0 = t * 512
                    tfront = FR.pop(t)
                    big_w = [ptok(), vtok(), atok()]
                    to = None
                    for h in range(8):
                        to = P.dma("gpsimd", big[:, 24 + h, :], osc[h, :, tok0:tok0 + 512], s_o, waits=big_w)
                    xw = [ptok(), vtok(), atok()]
                    for sub in range(4):
                        i = xring.next()
                        tl = P.dma("sync", xin[:, i, :], xa[tok0 + sub * 128:tok0 + (sub + 1) * 128, :], xring.sems[i],
                                   waits=xring.free[i])
                        tpe = None
                        for c4 in range(4):
                            b = mmb.next()
                            for cq in range(4):
                                c = c4 * 4 + cq
                                tpe = P.op("tensor", lambda e, b=b, cq=cq, i=i, c=c: e.transpose(
                                    ps[:, b, cq * 128:(cq + 1) * 128], xin[:, i, c * 128:(c + 1) * 128], ident[:]),
                                    waits=[tl] + mmb.free[b], mark=(cq == 3))
                            te = P.op("vector", lambda e, b=b, c4=c4, sub=sub: e.tensor_copy(
                                out=xT[:, c4 * 4:(c4 + 1) * 4, sub * 128:(sub + 1) * 128],
                                in_=ps[:, b, :].rearrange("p (c d) -> p c d", d=128)), waits=[tpe] + xw)
                            mmb.free[b] = [te]
                        xring.free[i] = [tpe]
                    tx = [vtok(), atok()]
                    tu = tfront
                    tst = ln_stats(8, lambda c: cv[:, c, :], ones1k, EPS_LN, tfront)
                    for i in range(8):
                        a = P.op("vector", lambda e, i=i: e.tensor_tensor(out=cv[:, i, :], in0=cv[:, i, :], in1=stat[:, 0, :], op=ALU.subtract),
                                 waits=[tst, ptok()])
                        a = P.op("vector", lambda e, i=i: e.tensor_tensor(out=cv[:, i, :], in0=cv[:, i, :], in1=stat[:, 1, :], op=ALU.mult), waits=[a])
                        P.op("scalar", lambda e, i=i: e.activation(out=big[:, i, :], in_=cv[:, i, :], func=AF.Silu,
                                                                   scale=pv[:, O_CLG + i:O_CLG + i + 1], bias=pv[:, O_CLB + i:O_CLB + i + 1]),
                             waits=[a] + big_w)
                    thn = atok()
                    for dc in range(16):
                        b1, t1_ = wgroup(("co", dc, 0, 8), lambda k: big[:, k, :], [thn])
                        b2, t2_ = wgroup(("ao", dc, 0, 8), lambda k: big[:, 24 + k, :], [to])
                        b3, t3_ = wgroup(("in", 40 + dc, 0, 16), lambda k: uA[:, k, :], tu)
                        b4, t4_ = wgroup(("in", 56 + dc, 0, 16), lambda k: uA[:, k, :], tu)
                        i3 = tmr.next()
                        s3 = P.op("scalar", lambda e, b3=b3, i3=i3: e.activation(out=tmp[:, i3, :], in_=ps[:, b3, :], func=AF.Sigmoid),
                                  waits=[t3_] + tmr.free[i3])
                        mmb.free[b3] = [s3]
                        i4 = tmr.next()
                        s4 = P.op("scalar", lambda e, b4=b4, i4=i4: e.activation(out=tmp[:, i4, :], in_=ps[:, b4, :], func=AF.Sigmoid),
                                  waits=[t4_] + tmr.free[i4])
                        mmb.free[b4] = [s4]
                        a1 = P.op("vector", lambda e, b1=b1, i3=i3, dc=dc: e.scalar_tensor_tensor(
                            out=tmp[:, i3, :], in0=ps[:, b1, :], scalar=pv[:, O_BCO + dc:O_BCO + dc + 1], in1=tmp[:, i3, :],
                            op0=ALU.add, op1=ALU.mult), waits=[t1_, s3])
                        mmb.free[b1] = [a1]
                        a2 = P.op("vector", lambda e, b2=b2, i4=i4: e.tensor_tensor(
                            out=tmp[:, i4, :], in0=ps[:, b2, :], in1=tmp[:, i4, :], op=ALU.mult), waits=[t2_, s4])
                        mmb.free[b2] = [a2]
                        a3 = P.op("vector", lambda e, i3=i3, i4=i4, dc=dc: e.tensor_tensor(
                            out=big[:, 8 + dc, :], in0=tmp[:, i3, :], in1=tmp[:, i4, :], op=ALU.add), waits=[a1, a2] + big_w)
                        tmr.free[i3] = [a3]
                        tmr.free[i4] = [a3]
                    tm_ = vtok()
                    for dc in range(16):
                        b, tp_ = wgroup(("o", dc, 0, 16), lambda k: big[:, 8 + k, :], [tm_])
                        a = P.op("vector", lambda e, b=b, dc=dc: e.scalar_tensor_tensor(
                            out=xT[:, dc, :], in0=ps[:, b, :], scalar=par[:, P_G1S + dc:P_G1S + dc + 1], in1=xT[:, dc, :],
                            op0=ALU.mult, op1=ALU.add), waits=[tp_] + tx)
                        mmb.free[b] = [a]
                    tz = vtok()
                    tst = ln_stats(16, lambda c: xT[:, c, :], ones2k, EPS_DN, [tz])
                    for c in range(16):
                        a = P.op("vector", lambda e, c=c: e.tensor_tensor(out=xT[:, c, :], in0=xT[:, c, :], in1=stat[:, 0, :], op=ALU.subtract),
                                 waits=[tst, ptok()])
                        a = P.op("vector", lambda e, c=c: e.tensor_tensor(out=xT[:, c, :], in0=xT[:, c, :], in1=stat[:, 1, :], op=ALU.mult), waits=[a])
                        a5 = P.op("scalar", lambda e, c=c: e.activation(out=uB[:, c, :], in_=xT[:, c, :], func=AF.Identity,
                                                                        scale=par[:, P_A2 + c:P_A2 + c + 1], bias=par[:, P_B2 + c:P_B2 + c + 1]),
                                  waits=[a, ptok()])
                        P.op("vector", lambda e, c=c: e.tensor_scalar(out=xT[:, c, :], in0=xT[:, c, :], scalar1=pv[:, O_LN1G + c:O_LN1G + c + 1],
                                                                      scalar2=pv[:, O_LN1B + c:O_LN1B + c + 1], op0=ALU.mult, op1=ALU.add),
                             waits=[a, a5])
                    tu2 = [vtok(), atok()]
                    nxt = front(t + 1) if t + 1 < NT_OWN else None
                    for j in range(44):
                        ba, ta_ = wgroup(("up", j, 0, 16), lambda k: uB[:, k, :], tu2)
                        bv, tv_ = wgroup(("up", 44 + j, 0, 16), lambda k: uB[:, k, :], tu2)
                        ai = abr.next()
                        c0 = P.op("vector", lambda e, ai=ai, j=j: e.tensor_copy(out=abuf[:, ai, 0:2], in_=ahist[:, j, :]), waits=abr.free[ai])
                        c1 = P.op("scalar", lambda e, ba=ba, ai=ai: e.activation(out=abuf[:, ai, 2:514], in_=ps[:, ba, :], func=AF.Copy),
                                  waits=[ta_] + abr.free[ai])
                        mmb.free[ba] = [c1]
                        c2 = P.op("vector", lambda e, ai=ai, j=j: e.tensor_copy(out=ahist[:, j, :], in_=abuf[:, ai, 512:514]), waits=[c1, c0])
                        if t == 0:
                            c2 = P.op("vector", lambda e, j=j: e.tensor_scalar(out=ahist[:, j, :], in0=ahist[:, j, :], scalar1=flag,
                                                                               scalar2=None, op0=ALU.mult), waits=[c2])
                        ti = tmr.next()
                        a = P.op("vector", lambda e, ai=ai, ti=ti, j=j: e.tensor_scalar(
                            out=tmp[:, ti, :], in0=abuf[:, ai, 0:512], scalar1=pv[:, O_FW + 3 * j:O_FW + 3 * j + 1],
                            scalar2=pv[:, O_FB + j:O_FB + j + 1], op0=ALU.mult, op1=ALU.add), waits=[c0, c1, c2] + tmr.free[ti])
                        a = P.op("vector", lambda e, ai=ai, ti=ti, j=j: e.scalar_tensor_tensor(
                            out=tmp[:, ti, :], in0=abuf[:, ai, 1:513], scalar=pv[:, O_FW + 3 * j + 1:O_FW + 3 * j + 2], in1=tmp[:, ti, :],
                            op0=ALU.mult, op1=ALU.add), waits=[a])
                        a = P.op("vector", lambda e, ai=ai, ti=ti, j=j: e.scalar_tensor_tensor(
                            out=tmp[:, ti, :], in0=abuf[:, ai, 2:514], scalar=pv[:, O_FW + 3 * j + 2:O_FW + 3 * j + 3], in1=tmp[:, ti, :],
                            op0=ALU.mult, op1=ALU.add), waits=[a])
                        abr.free[ai] = [a]
                        s_ = P.op("scalar", lambda e, ti=ti: e.activation(out=tmp[:, ti, :], in_=tmp[:, ti, :], func=AF.Silu), waits=[a])
                        hh = P.op("vector", lambda e, bv=bv, ti=ti, j=j: e.tensor_tensor(
                            out=big[:, j, :], in0=ps[:, bv, :], in1=tmp[:, ti, :], op=ALU.mult),
                            waits=[tv_, s_, tm_, ptok()] if j < 32 else [tv_, s_])
                        mmb.free[bv] = [hh]
                        tmr.free[ti] = [hh]
                        if nxt is not None and next(nxt, "done") == "done":
                            nxt = None
                    if nxt is not None:
                        for _ in nxt:
                            pass
                    th = vtok()
                    if t == 0:
                        continue
                    for dc in range(16):
                        b = mmb.next()
                        tpe = None
                        for part, (k0, nk) in enumerate(((0, 16), (16, 16), (32, 12))):
                            s, tl = wget(("dn", dc, k0, nk))
                            for k in range(nk):
                                tpe = P.op("tensor", lambda e, b=b, s=s, k=k, k0=k0, part=part, nk=nk: e.matmul(
                                    ps[:, b, :], lhsT=wsl[:, s, k, :], rhs=big[:, k0 + k, :],
                                    start=(part == 0 and k == 0), stop=(part == 2 and k == nk - 1)),
                                    waits=[tl, th] + mmb.free[b], mark=(k == nk - 1))
                            wfree[s] = [tpe]
                        a = P.op("vector", lambda e, b=b, dc=dc: e.scalar_tensor_tensor(
                            out=xT[:, dc, :], in0=ps[:, b, :], scalar=par[:, P_G2S + dc:P_G2S + dc + 1], in1=xT[:, dc, :],
                            op0=ALU.mult, op1=ALU.add), waits=[tpe])
                        mmb.free[b] = [a]
                    tz = vtok()
                    tst = ln_stats(16, lambda c: xT[:, c, :], ones2k, EPS_DN, [tz])
                    for c in range(16):
                        a = P.op("vector", lambda e, c=c: e.tensor_tensor(out=xT[:, c, :], in0=xT[:, c, :], in1=stat[:, 0, :], op=ALU.subtract),
                                 waits=[tst, ptok()])
                        a = P.op("vector", lambda e, c=c: e.tensor_tensor(out=xT[:, c, :], in0=xT[:, c, :], in1=stat[:, 1, :], op=ALU.mult), waits=[a])
                        P.op("scalar", lambda e, c=c: e.activation(out=xT[:, c, :], in_=xT[:, c, :], func=AF.Identity,
                                                                   scale=pv[:, O_LN2G + c:O_LN2G + c + 1], bias=pv[:, O_LN2B + c:O_LN2B + c + 1]),
                             waits=[a])
                    tfin = atok()
                    for sub in range(4):
                        for c4 in range(4):
                            b = mmb.next()
                            tpe = None
                            for cq in range(4):
                                c = c4 * 4 + cq
                                tpe = P.op("tensor", lambda e, b=b, cq=cq, c=c, sub=sub: e.transpose(
                                    ps[:, b, cq * 128:(cq + 1) * 128], xT[:, c, sub * 128:(sub + 1) * 128], ident[:]),
                                    waits=[tfin] + mmb.free[b], mark=(cq == 3))
                            if c4 % 2 == 0:
                                te = P.op("vector", lambda e, b=b, c4=c4: e.tensor_copy(out=ost[:, c4 * 512:(c4 + 1) * 512], in_=ps[:, b, :]),
                                          waits=[tpe] + ST["ost_free"])
                            else:
                                te = P.op("scalar", lambda e, b=b, c4=c4: e.activation(out=ost[:, c4 * 512:(c4 + 1) * 512], in_=ps[:, b, :], func=AF.Copy),
                                          waits=[tpe] + ST["ost_free"])
                            mmb.free[b] = [te]
                        r0 = (t - 1) * 512 + sub * 128
                        ts = P.dma("sync", outd[r0:r0 + 128, :], ost[:, :], s_out, waits=[vtok(), atok()])
                        ST["ost_free"] = [ts]
                return rec

            reqs = emit_mf(None, True)
            emit_mf(reqs, False)
            P.barrier("M")
            P.flush()
    return nc


def _bf16(a):
    return np.asarray(a, dtype=np.float32).astype(ml_dtypes.bfloat16)


def _host_consts():
    ident = np.eye(128, dtype=np.float32)
    perm = np.zeros((32, 32), np.float32)
    for m in range(32):
        perm[(m + 16) % 32, m] = 1.0
    eall = np.zeros((NBLK, NBLK, 128), np.float32)
    for j in range(NBLK):
        eall[j, j, :] = 1.0
    tri = np.zeros((128, 4, 512), np.float32)
    tt = np.arange(128)[:, None]
    for half in range(2):
        for k2 in range(2):
            ql = np.arange(256)[None, :]
            m = (k2 * 128 + tt) > ql
            tri[:, half * 2 + k2, half * 256:(half + 1) * 256] = np.where(m, NEG, 0.0)
    return ident, perm, _bf16(eall.reshape(NBLK, NBLK * 128)), _bf16(tri.reshape(128, 4 * 512))


def kernel(x, c, w_ada, b_ada, w_in, conv_dw_w, conv_dw_b, conv_ln_g, conv_ln_b, w_conv_out, b_conv_out, w_attn_out,
           w_out, ln1_g, ln1_b, w_up, ffn_dw_w, ffn_dw_b, w_down, ln2_g, ln2_b):
    f32 = np.float32
    x = np.asarray(x, f32)
    c = np.asarray(c, f32)
    ident, perm, eall, tri = _host_consts()

    def fm(v, nch):
        return np.ascontiguousarray(np.asarray(v, f32).reshape(nch, 128).T)

    L = 0
    cw = np.asarray(conv_dw_w, f32)[L]
    fw = np.asarray(ffn_dw_w, f32)[L]
    pv_base = np.concatenate([
        fm(ln1_g[L], 16), fm(ln1_b[L], 16), fm(ln2_g[L], 16), fm(ln2_b[L], 16), fm(b_conv_out[L], 16),
        fm(conv_dw_b[L], 8), fm(conv_ln_g[L], 8), fm(conv_ln_b[L], 8),
        np.ascontiguousarray(cw.reshape(31, 8, 128).transpose(2, 1, 0)).reshape(128, 8 * 31),
        np.ascontiguousarray(fw.reshape(3, 44, 128).transpose(2, 1, 0)).reshape(128, 44 * 3),
        fm(ffn_dw_b[L], 44)], axis=1).astype(f32)
    inv_freq = (np.float32(500000.0) ** (-np.arange(0, 32, 2, dtype=f32) / np.float32(32))).astype(f32)
    shared = {
        "w_ada": np.ascontiguousarray(np.asarray(w_ada, f32)[L]), "b_ada": np.ascontiguousarray(np.asarray(b_ada, f32)[L][None, :]),
        "w_in": np.ascontiguousarray(np.asarray(w_in, f32)[L]), "w_conv_out": np.ascontiguousarray(np.asarray(w_conv_out, f32)[L]),
        "w_attn_out": np.ascontiguousarray(np.asarray(w_attn_out, f32)[L]), "w_out": np.ascontiguousarray(np.asarray(w_out, f32)[L]),
        "w_up": np.ascontiguousarray(np.asarray(w_up, f32)[L]), "w_down": np.ascontiguousarray(np.asarray(w_down, f32)[L]),
        "eall": eall, "tri": tri, "ident": ident, "identb": _bf16(ident), "perm": perm,
    }
    in_maps = []
    for core in range(8):
        s, r = core // 4, core % 4
        own0 = r * CH
        others = [q for q in range(4) if q != r]
        pos = np.concatenate([np.arange(own0 - HALO, own0 + CH)] + [np.arange(q * CH, (q + 1) * CH) for q in others])
        valid_tok = pos >= 0
        xa = np.zeros((NALL, D), f32)
        xa[valid_tok] = x[s][pos[valid_tok]]
        posf = np.where(valid_tok, pos, 0).astype(f32)
        ang = posf[None, :] * inv_freq[:, None]
        cs_, sn_ = np.cos(ang).astype(f32), np.sin(ang).astype(f32)
        ropeC = np.concatenate([cs_, cs_], 0)
        ropeS = np.concatenate([-sn_, sn_], 0)
        gblk = pos[::256] // 256
        gblk = np.where(pos[::256] >= 0, gblk, -10)
        pb = np.zeros((18, NBLK), f32)
        val = np.zeros((18, NBLK), f32)
        for row in range(18):
            gb = gblk[row]
            for jj in range(NBLK):
                ok = (jj >= 2) and (gblk[jj] >= 0) and (gblk[jj] < gb)
                val[row, jj] = 1.0 if ok else 0.0
                pb[row, jj] = 0.0 if ok else -1e30
        pvc = np.concatenate([pv_base, np.full((128, 1), 0.0 if r == 0 else 1.0, f32)], axis=1)
        m = dict(shared)
        m.update({
            "xa": xa, "cc": fm(c[s], 16), "pv": np.ascontiguousarray(pvc),
            "ropeC": np.ascontiguousarray(ropeC), "ropeS": np.ascontiguousarray(ropeS),
            "pb": np.ascontiguousarray(np.broadcast_to(pb.reshape(1, -1), (128, 18 * NBLK))),
            "val": np.ascontiguousarray(np.broadcast_to(val.reshape(1, -1), (128, 18 * NBLK))),
        })
        in_maps.append(m)
    nc = build()
    res = run_bass_kernel_spmd(nc, in_maps, core_ids=list(range(8)))
    out = np.zeros((2, SEQ, D), f32)
    for core in range(8):
        s, r = core // 4, core % 4
        out[s, r * CH:(r + 1) * CH] = np.asarray(res.results[core]["out"], f32)
    return out
```

```python
import numpy as np
import ml_dtypes
from contextlib import ExitStack
import concourse.bass as bass
import concourse.mybir as mybir
from concourse.bass_utils import run_bass_kernel_spmd

F32 = mybir.dt.float32
BF16 = mybir.dt.bfloat16
AF = mybir.ActivationFunctionType
ALU = mybir.AluOpType
AX = mybir.AxisListType

D = 2048
SEQ = 16384
CH = 4096
HALO = 512
NOWN = CH + HALO
NALL = SEQ + HALO
NT_ALL = NALL // 512
NT_OWN = NOWN // 512
NBLK = NALL // 256
NKT = NALL // 128
DFF = 5632
ALPHA = 2.0 ** 0.25
EPS_LN = 1e-5
EPS_DN = 1e-5 / (ALPHA * ALPHA)
QSCALE = 128.0 ** -0.5
NEG = -30000.0
ENG = ("sync", "scalar", "vector", "gpsimd", "tensor")

O_LN1G, O_LN1B, O_LN2G, O_LN2B, O_BCO, O_CDB, O_CLG, O_CLB, O_CW, O_FW, O_FB, O_FLAG, NPV = \
    0, 16, 32, 48, 64, 80, 88, 96, 104, 352, 484, 528, 529
P_SC1, P_SH1, P_G1S, P_A2, P_B2, P_G2S, P_TMP, NPAR = 0, 16, 32, 48, 64, 80, 96, 128


class Sem:
    __slots__ = ("h", "v")


class Prog:
    def __init__(self, nc, stack):
        self.nc = nc
        self.stack = stack
        self.q = {e: [] for e in ENG}
        self.waited = {e: {} for e in ENG}
        self.pg = {}
        self.dma_out = []
        self.nsem = 0
        self.abs = {e: [] for e in ENG}
        self.semval = {}
        self.check = False
        self.dry = False

    def sem(self, name):
        s = Sem()
        if self.dry:
            s.h = None
            s.v = 0
            return s
        self.nsem += 1
        s.h = self.stack.enter_context(self.nc.semaphore(f"{name}_{self.nsem}"))
        s.v = 0
        return s

    def new_phase(self, tag):
        self.pg = {e: self.sem(f"pg{tag}{e[:2]}") for e in ("scalar", "vector", "gpsimd", "tensor")}
        self.waited = {e: {} for e in ENG}
        self.dma_out = []

    def _waits(self, eng, waits):
        if self.dry:
            return
        for w in waits:
            if w is None:
                continue
            s, v = w
            if v <= 0 or self.waited[eng].get(id(s), 0) >= v:
                continue
            self.waited[eng][id(s)] = v
            self.abs[eng].append(('w', id(s), v))
            self.q[eng].append(lambda e, s=s, v=v: e.wait_ge(s.h, v))

    def op(self, eng, fn, waits=(), mark=True):
        if self.dry:
            return (self.pg[eng], 0) if mark else None
        self._waits(eng, waits)
        if mark:
            s = self.pg[eng]
            s.v += 1
            self.abs[eng].append(('i', id(s), 1))
            self.q[eng].append(lambda e, fn=fn, s=s: fn(e).then_inc(s.h, 1))
            return (s, s.v)
        self.q[eng].append(lambda e, fn=fn: fn(e))
        return None

    def dma(self, eng, out, in_, sem, waits=(), **kw):
        if self.dry:
            return (sem, 0)
        self._waits(eng, waits)
        sem.v += 16
        self.abs[eng].append(('i', id(sem), 16))
        self.q[eng].append(lambda e, out=out, in_=in_, sem=sem, kw=kw: e.dma_start(out=out, in_=in_, **kw).then_inc(sem.h, 16))
        tok = (sem, sem.v)
        self.dma_out.append(tok)
        return tok

    def barrier(self, tag):
        b = self.sem(f"bar{tag}")
        for e in ENG:
            if e in self.pg and self.pg[e].v > 0:
                self._waits(e, [(self.pg[e], self.pg[e].v)])
        last = {}
        for t in self.dma_out:
            last[id(t[0])] = t
        self._waits("sync", list(last.values()))
        for e in ENG:
            self.abs[e].append(('i', id(b), 1))
            self.q[e].append(lambda en, b=b: en.sem_inc(b.h, 1))
        for e in ENG:
            self.abs[e].append(('w', id(b), len(ENG)))
            self.q[e].append(lambda en, b=b: en.wait_ge(b.h, len(ENG)))

    def simulate(self):
        pc = {e: 0 for e in ENG}
        val = self.semval
        prog = True
        while prog:
            prog = False
            for e in ENG:
                q = self.abs[e]
                while pc[e] < len(q):
                    k, sid, v = q[pc[e]]
                    if k == 'w':
                        if val.get(sid, 0) < v:
                            break
                    else:
                        val[sid] = val.get(sid, 0) + v
                    pc[e] += 1
                    prog = True
        stuck = {e: (pc[e], len(self.abs[e])) for e in ENG if pc[e] < len(self.abs[e])}
        self.abs = {e: [] for e in ENG}
        return stuck

    def flush(self):
        if self.check:
            st = self.simulate()
            print("SIM stuck:", st)
        with self.nc.Block() as block:
            for name in ENG:
                ops = self.q[name]
                if not ops:
                    continue

                def run(e, ops=ops):
                    for f in ops:
                        f(e)
                getattr(block, name)(run)
        self.q = {e: [] for e in ENG}


class Ring:
    def __init__(self, P, name, slots):
        self.slots = slots
        self.sems = [P.sem(f"{name}{i}") for i in range(len(slots))]
        self.free = [[] for _ in slots]
        self.i = 0

    def next(self):
        i = self.i % len(self.slots)
        self.i += 1
        return i


class BankRing:
    def __init__(self, banks):
        self.banks = list(banks)
        self.free = {b: [] for b in banks}
        self.i = 0

    def next(self):
        b = self.banks[self.i % len(self.banks)]
        self.i += 1
        return b


def build():
    nc = bass.Bass("TRN2", target_bir_lowering=False)

    def din(name, shape, dt):
        return nc.dram_tensor(name, shape, dt, kind="ExternalInput").ap()

    def dint(name, shape, dt):
        return nc.dram_tensor(name, shape, dt, kind="Internal").ap()

    xa = din("xa", [NALL, D], F32)
    cc = din("cc", [128, 16], F32)
    w_ada = din("w_ada", [D, 6 * D], F32)
    b_ada = din("b_ada", [1, 6 * D], F32)
    w_in = din("w_in", [D, 9216], F32)
    w_co = din("w_conv_out", [1024, D], F32)
    w_ao = din("w_attn_out", [1024, D], F32)
    w_o = din("w_out", [D, D], F32)
    w_up = din("w_up", [D, 2 * DFF], F32)
    w_dn = din("w_down", [DFF, D], F32)
    pvd = din("pv", [128, NPV], F32)
    ropeC = din("ropeC", [32, NALL], F32)
    ropeS = din("ropeS", [32, NALL], F32)
    pbd = din("pb", [128, 18 * NBLK], F32)
    vald = din("val", [128, 18 * NBLK], F32)
    ealld = din("eall", [NBLK, NBLK * 128], BF16)
    trid = din("tri", [128, 4 * 512], BF16)
    identd = din("ident", [128, 128], F32)
    identbd = din("identb", [128, 128], BF16)
    permd = din("perm", [32, 32], F32)
    outd = nc.dram_tensor("out", [CH, D], F32, kind="ExternalOutput").ap()

    ws_in = dint("ws_in", [72, 128, 16, 128], BF16)
    ws_co = dint("ws_co", [16, 128, 8, 128], BF16)
    ws_ao = dint("ws_ao", [16, 128, 8, 128], BF16)
    ws_o = dint("ws_o", [16, 128, 16, 128], BF16)
    ws_up = dint("ws_up", [88, 128, 16, 128], BF16)
    ws_dn = dint("ws_dn", [16, 128, 44, 128], BF16)
    ksc = dint("ksc", [8, 128, NALL], BF16)
    vsc = dint("vsc", [8, 128, NKT, 128], BF16)
    qsc = dint("qsc", [8, 128, NOWN], BF16)
    osc = dint("osc", [8, 128, NOWN], BF16)
    modsc = dint("modsc", [1, 6 * D], F32)

    with ExitStack() as gst:
        P = Prog(nc, gst)

        def gsb(name, shape, dt):
            return gst.enter_context(nc.sbuf_tensor(name, shape, dt))

        ps = gst.enter_context(nc.psum_tensor("ps", [128, 8, 512], F32))
        pv = gsb("pvs", [128, NPV], F32)
        par = gsb("par", [128, NPAR], F32)
        ident = gsb("ident_s", [128, 128], F32)
        identb = gsb("identb_s", [128, 128], BF16)
        onesb = gsb("onesb", [128, 128], BF16)
        ones1k = gsb("ones1k", [128, 128], F32)
        onesf = gsb("onesf", [128, 128], F32)
        ones2k = gsb("ones2k", [128, 128], F32)
        perm = gsb("perm_s", [32, 32], F32)
        kmT = gsb("kmT", [128, 8, NBLK], BF16)

        with ExitStack() as st:
            def sb(name, shape, dt):
                return st.enter_context(nc.sbuf_tensor(name, shape, dt))
            cact = sb("cact", [128, 16], F32)
            wa = sb("wa", [128, 2, 4096], F32)
            bada = sb("bada", [1, 6 * D], F32)
            modrow = sb("modrow", [1, 4096], F32)
            modT = sb("modT", [128, 96], F32)
            stg = sb("stg", [128, 6, 1024], F32)
            stb = sb("stb", [128, 6, 1024], BF16)
            P.new_phase("a")
            sc = P.sem("ldc")
            P.dma("sync", pv[:], pvd, sc)
            P.dma("sync", ident[:], identd, sc)
            P.dma("sync", identb[:], identbd, sc)
            P.dma("sync", perm[:], permd, sc)
            P.dma("sync", cact[:], cc, sc)
            tconst = P.dma("sync", bada[:], b_ada, sc)
            P.op("vector", lambda e: e.memset(onesb[:], 1.0))
            P.op("vector", lambda e: e.memset(ones1k[:], 1.0 / 1024.0))
            P.op("vector", lambda e: e.memset(onesf[:], 1.0))
            P.op("vector", lambda e: e.memset(ones2k[:], 1.0 / 2048.0))
            tca = P.op("scalar", lambda e: e.activation(out=cact[:], in_=cact[:], func=AF.Silu), waits=[tconst])
            wring = Ring(P, "wa", [wa[:, 0, :], wa[:, 1, :]])
            smod = P.sem("modw")
            ev = None
            store = None
            for p in range(3):
                tk = None
                for k in range(16):
                    i = wring.next()
                    tl = P.dma("sync", wring.slots[i], w_ada[k * 128:(k + 1) * 128, p * 4096:(p + 1) * 4096],
                               wring.sems[i], waits=wring.free[i])
                    for b in range(8):
                        tk = P.op("tensor", lambda e, b=b, i=i, k=k: e.matmul(
                            ps[0:1, b, :], lhsT=cact[:, k:k + 1], rhs=wring.slots[i][:, b * 512:(b + 1) * 512],
                            start=(k == 0), stop=(k == 15)), waits=[tl, tca, ev], mark=(b == 7))
                    wring.free[i] = [tk]
                for b in range(8):
                    ev = P.op("vector", lambda e, b=b, p=p: e.tensor_tensor(
                        out=modrow[0:1, b * 512:(b + 1) * 512], in0=ps[0:1, b, :],
                        in1=bada[0:1, p * 4096 + b * 512:p * 4096 + (b + 1) * 512], op=ALU.add),
                        waits=[tk, tconst, store])
                tt_ = None
                for c in range(32):
                    tt_ = P.op("tensor", lambda e, c=c: e.transpose(ps[:, 7, c:c + 1], modrow[0:1, c * 128:(c + 1) * 128],
                                                                     ident[0:1, 0:1]), waits=[ev, tconst])
                ev = P.op("vector", lambda e, p=p: e.tensor_copy(out=modT[:, p * 32:(p + 1) * 32], in_=ps[:, 7, 0:32]), waits=[tt_])
                store = ev
            tmod = ev
            w0 = [tmod, tconst]
            P.op("vector", lambda e: e.tensor_scalar(out=par[:, P_SC1:P_SC1 + 16], in0=modT[:, 16:32], scalar1=1.0,
                                                     scalar2=None, op0=ALU.add), waits=w0)
            P.op("vector", lambda e: e.tensor_copy(out=par[:, P_SH1:P_SH1 + 16], in_=modT[:, 0:16]))
            P.op("vector", lambda e: e.tensor_scalar(out=par[:, P_G1S:P_G1S + 16], in0=modT[:, 32:48], scalar1=1.0,
                                                     scalar2=1.0 / ALPHA, op0=ALU.add, op1=ALU.mult))
            P.op("vector", lambda e: e.tensor_scalar(out=par[:, P_G2S:P_G2S + 16], in0=modT[:, 80:96], scalar1=1.0,
                                                     scalar2=1.0 / ALPHA, op0=ALU.add, op1=ALU.mult))
            t1 = P.op("vector", lambda e: e.tensor_scalar(out=par[:, P_TMP:P_TMP + 16], in0=modT[:, 64:80], scalar1=1.0,
                                                          scalar2=None, op0=ALU.add))
            t2 = P.op("vector", lambda e: e.tensor_tensor(out=par[:, P_A2:P_A2 + 16], in0=pv[:, O_LN1G:O_LN1G + 16],
                                                          in1=par[:, P_TMP:P_TMP + 16], op=ALU.mult), waits=[t1])
            t3 = P.op("vector", lambda e: e.tensor_tensor(out=par[:, P_B2:P_B2 + 16], in0=pv[:, O_LN1B:O_LN1B + 16],
                                                          in1=par[:, P_TMP:P_TMP + 16], op=ALU.mult), waits=[t1])
            P.op("vector", lambda e: e.tensor_tensor(out=par[:, P_B2:P_B2 + 16], in0=par[:, P_B2:P_B2 + 16],
                                                     in1=modT[:, 48:64], op=ALU.add), waits=[t3])
            sring = Ring(P, "stg", [stg[:, i, :] for i in range(6)])
            bring = Ring(P, "stb", [stb[:, i, :] for i in range(6)])
            ci = 0
            for (src, dst, K, N) in ((w_in, ws_in, D, 9216), (w_co, ws_co, 1024, D), (w_ao, ws_ao, 1024, D),
                                     (w_o, ws_o, D, D), (w_up, ws_up, D, 2 * DFF), (w_dn, ws_dn, DFF, D)):
                for kk in range(K // 128):
                    for cs in range(N // 1024):
                        i = sring.next()
                        j = bring.next()
                        tl = P.dma("sync", sring.slots[i], src[kk * 128:(kk + 1) * 128, cs * 1024:(cs + 1) * 1024],
                                   sring.sems[i], waits=sring.free[i])
                        eng = ("vector", "gpsimd")[ci % 2]
                        ci += 1
                        tcst = P.op(eng, lambda e, i=i, j=j: e.tensor_copy(out=bring.slots[j], in_=sring.slots[i]),
                                    waits=[tl] + bring.free[j])
                        sring.free[i] = [tcst]
                        ts = P.dma("scalar", dst[cs * 8:(cs + 1) * 8, :, kk, :].rearrange("c p d -> p c d"),
                                   bring.slots[j].rearrange("p (c d) -> p c d", d=128), bring.sems[j], waits=[tcst])
                        bring.free[j] = [ts]
            P.barrier("a")
            P.flush()

        with ExitStack() as st:
            def sb(name, shape, dt):
                return st.enter_context(nc.sbuf_tensor(name, shape, dt))
            wq = sb("wq", [128, 8, 16, 128], BF16)
            wk = sb("wk", [128, 8, 16, 128], BF16)
            wv = sb("wv", [128, 8, 16, 128], BF16)
            xin = sb("xin", [128, 4, D], F32)
            uT = sb("uT", [128, 2, 16, 512], BF16)
            kf = sb("kf", [128, 3, 512], F32)
            r1 = sb("r1", [32, 3, 512], F32)
            r2 = sb("r2", [32, 3, 512], F32)
            kbf = sb("kbf", [128, 4, 512], BF16)
            ctb = sb("ctb", [32, 2, 512], F32)
            stb2 = sb("stb2", [32, 2, 512], F32)
            vbf = sb("vbf", [128, 3, 1024], BF16)
            kms = sb("kms", [128, 8, NBLK], F32)
            P.new_phase("k")
            sw = P.sem("ldw")
            P.dma("sync", wq[:], ws_in[16:24].rearrange("c p k d -> p c k d"), sw)
            P.dma("sync", wk[:], ws_in[24:32].rearrange("c p k d -> p c k d"), sw)
            tw = P.dma("sync", wv[:], ws_in[32:40].rearrange("c p k d -> p c k d"), sw)
            xring = Ring(P, "xin", [xin[:, i, :] for i in range(4)])
            tpb = BankRing([0, 1, 2])
            prb = BankRing([3, 4, 5])
            swb = BankRing([6, 7])
            kfr = Ring(P, "kf", [0, 1, 2])
            kbr = Ring(P, "kbf", [kbf[:, i, :] for i in range(4)])
            vbr = Ring(P, "vbf", [vbf[:, i, :] for i in range(3)])
            ropr = Ring(P, "rope", [0, 1])
            uT_free = [[], []]
            evi = 0
            for t in range(NT_ALL):
                tok0 = t * 512
                u = t % 2
                xt = []
                for sub in range(4):
                    i = xring.next()
                    tl = P.dma("sync", xring.slots[i], xa[tok0 + sub * 128:tok0 + (sub + 1) * 128, :], xring.sems[i],
                               waits=xring.free[i])
                    xt.append((i, tl))
                ri = ropr.next()
                P.dma("gpsimd", ctb[:, ri, :], ropeC[:, tok0:tok0 + 512], ropr.sems[ri], waits=ropr.free[ri])
                trope = P.dma("gpsimd", stb2[:, ri, :], ropeS[:, tok0:tok0 + 512], ropr.sems[ri], waits=ropr.free[ri])
                uready = []
                tpe = None
                for c in range(16):
                    b = tpb.next()
                    for sub in range(4):
                        i, tl = xt[sub]
                        tpe = P.op("tensor", lambda e, b=b, sub=sub, i=i, c=c: e.transpose(
                            ps[:, b, sub * 128:(sub + 1) * 128], xin[:, i, c * 128:(c + 1) * 128], ident[:]),
                            waits=[tl] + tpb.free[b], mark=(sub == 3))
                    if evi % 2 == 0:
                        te = P.op("scalar", lambda e, b=b, c=c, u=u: e.activation(
                            out=uT[:, u, c, :], in_=ps[:, b, :], func=AF.Identity,
                            scale=par[:, P_SC1 + c:P_SC1 + c + 1], bias=par[:, P_SH1 + c:P_SH1 + c + 1]),
                            waits=[tpe] + uT_free[u])
                    else:
                        te = P.op("vector", lambda e, b=b, c=c, u=u: e.tensor_scalar(
                            out=uT[:, u, c, :], in0=ps[:, b, :], scalar1=par[:, P_SC1 + c:P_SC1 + c + 1],
                            scalar2=par[:, P_SH1 + c:P_SH1 + c + 1], op0=ALU.mult, op1=ALU.add),
                            waits=[tpe] + uT_free[u])
                    evi += 1
                    tpb.free[b] = [te]
                    uready = (uready + [te])[-2:]
                for (i, tl) in xt:
                    xring.free[i] = [tpe]
                last_rope = None
                for kind in (("k", "q") if t < NT_OWN else ("k",)):
                    wmat = wk if kind == "k" else wq
                    for h in range(8):
                        b = prb.next()
                        tpe = None
                        for k in range(16):
                            tpe = P.op("tensor", lambda e, b=b, h=h, k=k, u=u, wmat=wmat: e.matmul(
                                ps[:, b, :], lhsT=wmat[:, h, k, :], rhs=uT[:, u, k, :], start=(k == 0), stop=(k == 15)),
                                waits=uready + [tw] + prb.free[b], mark=(k == 15))
                        fi = kfr.next()
                        ta = P.op("scalar", lambda e, b=b, fi=fi: e.activation(out=kf[:, fi, :], in_=ps[:, b, :], func=AF.Copy),
                                  waits=[tpe] + kfr.free[fi])
                        prb.free[b] = [ta]
                        sbk = swb.next()
                        tsw = P.op("tensor", lambda e, sbk=sbk, fi=fi: e.matmul(
                            ps[0:32, sbk, :], lhsT=perm[:, :], rhs=kf[0:32, fi, :], start=True, stop=True),
                            waits=[ta] + swb.free[sbk])
                        ta1 = P.op("vector", lambda e, fi=fi, ri=ri: e.tensor_tensor(
                            out=r1[:, fi, :], in0=kf[0:32, fi, :], in1=ctb[:, ri, :], op=ALU.mult), waits=[ta, trope])
                        ta2 = P.op("vector", lambda e, fi=fi, ri=ri, sbk=sbk: e.tensor_tensor(
                            out=r2[:, fi, :], in0=ps[0:32, sbk, :], in1=stb2[:, ri, :], op=ALU.mult), waits=[tsw])
                        swb.free[sbk] = [ta2]
                        ta3 = P.op("vector", lambda e, fi=fi: e.tensor_tensor(
                            out=kf[0:32, fi, :], in0=r1[:, fi, :], in1=r2[:, fi, :], op=ALU.add), waits=[ta1, ta2, tsw])
                        last_rope = ta3
                        ki = kbr.next()
                        if kind == "k":
                            ta4 = P.op("vector", lambda e, fi=fi, h=h, t=t: e.tensor_reduce(
                                out=kms[:, h, 2 * t:2 * t + 2], in_=kf[:, fi, :].rearrange("p (a b) -> p a b", b=256),
                                axis=AX.X, op=ALU.add), waits=[ta3])
                            tcs = P.op("gpsimd", lambda e, fi=fi, ki=ki: e.tensor_copy(out=kbf[:, ki, :], in_=kf[:, fi, :]),
                                       waits=[ta3] + kbr.free[ki])
                            kfr.free[fi] = [tcs, ta4]
                            ts = P.dma("sync", ksc[h, :, tok0:tok0 + 512], kbf[:, ki, :], kbr.sems[ki], waits=[tcs])
                        else:
                            tcs = P.op("gpsimd", lambda e, fi=fi, ki=ki: e.tensor_scalar(
                                out=kbf[:, ki, :], in0=kf[:, fi, :], scalar1=QSCALE, scalar2=None, op0=ALU.mult),
                                waits=[ta3] + kbr.free[ki])
                            kfr.free[fi] = [tcs]
                            ts = P.dma("sync", qsc[h, :, tok0:tok0 + 512], kbf[:, ki, :], kbr.sems[ki], waits=[tcs])
                        kbr.free[ki] = [ts]
                ropr.free[ri] = [last_rope]
                for sub in range(4):
                    vi = vbr.next()
                    tvs = []
                    for half in range(2):
                        b = prb.next()
                        tpe = None
                        for k in range(16):
                            tpe = P.op("tensor", lambda e, b=b, k=k, u=u, sub=sub, half=half: e.matmul(
                                ps[:, b, :].rearrange("p (c d) -> p c d", d=128),
                                lhsT=uT[:, u, k, sub * 128:(sub + 1) * 128], rhs=wv[:, half * 4:(half + 1) * 4, k, :],
                                start=(k == 0), stop=(k == 15)), waits=uready + [tw] + prb.free[b], mark=(k == 15))
                        if half == 0:
                            te = P.op("scalar", lambda e, b=b, vi=vi: e.activation(
                                out=vbf[:, vi, 0:512], in_=ps[:, b, :], func=AF.Copy), waits=[tpe] + vbr.free[vi])
                        else:
                            te = P.op("vector", lambda e, b=b, vi=vi: e.tensor_copy(
                                out=vbf[:, vi, 512:1024], in_=ps[:, b, :]), waits=[tpe] + vbr.free[vi])
                        prb.free[b] = [te]
                        tvs.append(te)
                    ts = P.dma("scalar", vsc[:, :, t * 4 + sub, :].rearrange("h p d -> p h d"),
                               vbf[:, vi, :].rearrange("p (h d) -> p h d", d=128), vbr.sems[vi], waits=tvs)
                    vbr.free[vi] = [ts]
                uT_free[u] = [tpe]
            P.op("vector", lambda e: e.tensor_scalar(out=kmT[:], in0=kms[:], scalar1=1.0 / 256.0, scalar2=None, op0=ALU.mult),
                 waits=[(P.pg["vector"], P.pg["vector"].v)])
            P.barrier("k")
            P.flush()

        with ExitStack() as st:
            def sb(name, shape, dt):
                return st.enter_context(nc.sbuf_tensor(name, shape, dt))
            KT = sb("KT", [128, 2, NALL], BF16)
            VV = sb("VV", [128, 2, NKT, 128], BF16)
            eall = sb("eall_s", [NBLK, NBLK, 128], BF16)
            tri = sb("tri_s", [128, 4, 512], BF16)
            pbs = sb("pbs", [128, 18, NBLK], F32)
            vals = sb("vals", [128, 18, NBLK], F32)
            qT = sb("qT", [128, 2, 512], BF16)
            gm = sb("gm", [128, 4, NBLK], F32)
            ge = sb("ge", [128, 4, NBLK], F32)
            top8 = sb("top8", [128, 4, 8], F32)
            selT = sb("selT", [NBLK, 2, 512], BF16)
            pT = sb("pT", [128, 8, 512], BF16)
            acc = sb("acc", [128, 4, 512], F32)
            rinv = sb("rinv", [128, 512], F32)
            oT = sb("oT", [128, 2, 512], BF16)
            P.new_phase("A")
            sc = P.sem("ldA")
            P.dma("sync", eall[:], ealld.rearrange("p (j t) -> p j t", t=128), sc)
            P.dma("sync", tri[:], trid.rearrange("p (j t) -> p j t", t=512), sc)
            P.dma("sync", pbs[:], pbd.rearrange("p (j t) -> p j t", t=NBLK), sc)
            tcon = P.dma("sync", vals[:], vald.rearrange("p (j t) -> p j t", t=NBLK), sc)
            kvr = Ring(P, "kv", [0, 1])
            qr = Ring(P, "qT", [0, 1])
            sbk = BankRing([0, 1, 2, 7])
            ptr = Ring(P, "pT", list(range(8)))
            orr = Ring(P, "oT", [0, 1])
            O_B, R_B, G_B, ST_B = 3, 4, 5, 6
            S = {"or_free": [], "g_free": [], "st_free": [], "gi": 0, "last_pe": None}
            selT_free = [[], []]
            ge_free = [[], [], [], []]
            acc_free = [[], []]
            kv_tok = {}
            kv_slot = {}

            def load_kv(h):
                kv = kvr.next()
                kv_slot[h] = kv
                tkv = None
                for pc in range(4):
                    a0_, a1_ = pc * (NALL // 4), (pc + 1) * (NALL // 4)
                    P.dma("gpsimd", KT[:, kv, a0_:a1_], ksc[h, :, a0_:a1_], kvr.sems[kv], waits=kvr.free[kv])
                for pc in range(4):
                    a0_, a1_ = pc * (NKT // 4), (pc + 1) * (NKT // 4)
                    tkv = P.dma("gpsimd", VV[:, kv, a0_:a1_, :], vsc[h, :, a0_:a1_, :], kvr.sems[kv], waits=kvr.free[kv])
                kv_tok[h] = tkv

            G = {}

            QL = {}

            def qload(idx):
                h, mt = idx // NT_OWN, idx % NT_OWN
                qi = qr.next()
                tq = P.dma("sync", qT[:, qi, :], qsc[h, :, mt * 512:(mt + 1) * 512], qr.sems[qi], waits=qr.free[qi])
                QL[idx] = (qi, tq)

            GA = {}

            def gatingA(idx):
                h, mt = idx // NT_OWN, idx % NT_OWN
                qi, tq = QL[idx]
                d6s = []
                for sub in range(4):
                    row = 2 * mt + sub // 2
                    g = sub
                    tg = P.op("tensor", lambda e, qi=qi, sub=sub, h=h: e.matmul(
                        ps[:, G_B, sub * 128:sub * 128 + NBLK], lhsT=qT[:, qi, sub * 128:(sub + 1) * 128],
                        rhs=kmT[:, h, :], start=True, stop=True), waits=[tq] + S["g_free"])
                    d1 = P.op("vector", lambda e, g=g, sub=sub, row=row: e.tensor_tensor(
                        out=gm[:, g, :], in0=ps[:, G_B, sub * 128:sub * 128 + NBLK], in1=pbs[:, row, :], op=ALU.add),
                        waits=[tg, tcon])
                    S["g_free"] = [d1]
                    d2 = P.op("vector", lambda e, g=g: e.max(out=top8[:, g, :], in_=gm[:, g, :]), waits=[d1])
                    d3 = P.op("vector", lambda e, g=g: e.tensor_scalar(
                        out=ge[:, g, :], in0=gm[:, g, :], scalar1=top8[:, g, 2:3], scalar2=None, op0=ALU.is_ge),
                        waits=[d2] + ge_free[g])
                    d4 = P.op("vector", lambda e, g=g, row=row: e.tensor_tensor(
                        out=ge[:, g, :], in0=ge[:, g, :], in1=vals[:, row, :], op=ALU.mult), waits=[d3])
                    d5 = P.op("vector", lambda e, g=g: e.tensor_scalar(
                        out=ge[:, g, :], in0=ge[:, g, :], scalar1=-NEG, scalar2=NEG, op0=ALU.mult, op1=ALU.add),
                        waits=[d4])
                    d6 = P.op("vector", lambda e, g=g, row=row: e.memset(ge[:, g, row:row + 1], 0.0), waits=[d5])
                    d6s.append(d6)
                GA[idx] = d6s

            def gatingB(idx):
                qi, tq = QL.pop(idx)
                d6s = GA.pop(idx)
                sp = idx % 2
                tts = []
                for sub in range(4):
                    g = sub
                    tt = P.op("tensor", lambda e, g=g, sub=sub: e.transpose(
                        ps[0:NBLK, ST_B, sub * 128:(sub + 1) * 128], ge[:, g, :], ident[:]), waits=[d6s[sub]] + S["st_free"])
                    ge_free[g] = [tt]
                    tts.append(tt)
                tsel = P.op("vector", lambda e, sp=sp: e.tensor_copy(out=selT[:, sp, :], in_=ps[0:NBLK, ST_B, :]),
                            waits=tts + selT_free[sp])
                S["st_free"] = [tsel]
                G[idx] = (qi, tq, sp, tsel)

            NQ = 8 * NT_OWN
            load_kv(0)
            load_kv(1)
            qload(0)
            gatingA(0)
            gatingB(0)
            for idx in range(NQ):
                h, mt = idx // NT_OWN, idx % NT_OWN
                kv = kv_slot[h]
                tkv = kv_tok[h]
                qi, tq, sp, tsel = G.pop(idx)
                ai = idx % 2
                own = list(range(0, 2)) if mt == 0 else list(range(2, 2 * mt + 2))
                blocks = own + list(range(18, NBLK))
                tiles = [(jj, k2) for jj in blocks for k2 in range(2)]
                n = len(tiles)
                Dp = 3
                ptoks = [None] * n
                pslot = [None] * n
                tadds = [None, None]
                if idx + 1 < NQ:
                    qload(idx + 1)
                for it in range(n + Dp):
                    if it == n - 24 and idx + 1 < NQ:
                        gatingA(idx + 1)
                    if it == n - 6 and idx + 1 < NQ:
                        gatingB(idx + 1)
                    if it < n:
                        jj, k2 = tiles[it]
                        kti = jj * 2 + k2
                        b = sbk.next()
                        diag = (jj == 2 * mt) or (jj == 2 * mt + 1)
                        P.op("tensor", lambda e, b=b, kv=kv, kti=kti, qi=qi: e.matmul(
                            ps[:, b, :], lhsT=KT[:, kv, kti * 128:(kti + 1) * 128], rhs=qT[:, qi, :], start=True, stop=False),
                            waits=[tq, tkv] + sbk.free[b], mark=False)
                        tpe = P.op("tensor", lambda e, b=b, jj=jj, sp=sp, diag=diag: e.matmul(
                            ps[:, b, :], lhsT=eall[:, jj, :], rhs=selT[:, sp, :], start=False, stop=(not diag)),
                            waits=[tsel, tcon], mark=(not diag))
                        if diag:
                            tix = (jj - 2 * mt) * 2 + k2
                            tpe = P.op("tensor", lambda e, b=b, tix=tix: e.matmul(
                                ps[:, b, :], lhsT=identb[:, :], rhs=tri[:, tix, :], start=False, stop=True))
                        pi = ptr.next()
                        pslot[it] = pi
                        tex = P.op("scalar", lambda e, b=b, pi=pi: e.activation(out=pT[:, pi, :], in_=ps[:, b, :], func=AF.Exp),
                                   waits=[tpe] + ptr.free[pi])
                        sbk.free[b] = [tex]
                        ptoks[it] = tex
                    j = it - Dp
                    if j >= 0:
                        jj, k2 = tiles[j]
                        kti = jj * 2 + k2
                        pi = pslot[j]
                        tpv = P.op("tensor", lambda e, kv=kv, kti=kti, pi=pi, j=j, n=n: e.matmul(
                            ps[:, O_B, :], lhsT=VV[:, kv, kti, :], rhs=pT[:, pi, :], start=(j == 0), stop=(j == n - 1)),
                            waits=[ptoks[j]] + (S["or_free"] if j == 0 else []))
                        pr_ = j % 2
                        aj = ai * 2 + pr_
                        if j < 2:
                            tadds[pr_] = P.op("vector", lambda e, pi=pi, aj=aj: e.tensor_copy(out=acc[:, aj, :], in_=pT[:, pi, :]),
                                              waits=[ptoks[j]] + acc_free[ai])
                        else:
                            tadds[pr_] = P.op("vector", lambda e, pi=pi, aj=aj: e.tensor_tensor(
                                out=acc[:, aj, :], in0=acc[:, aj, :], in1=pT[:, pi, :], op=ALU.add), waits=[ptoks[j], tadds[pr_]])
                        ptr.free[pi] = [tpv, tadds[pr_]]
                        S["last_pe"] = tpv
                last_pe = S["last_pe"]
                P.op("tensor", lambda e, ai=ai: e.matmul(ps[:, R_B, :], lhsT=onesf[:, :], rhs=acc[:, ai * 2, :], start=True, stop=False),
                     waits=[tadds[0]] + S["or_free"], mark=False)
                tr = P.op("tensor", lambda e, ai=ai: e.matmul(ps[:, R_B, :], lhsT=onesf[:, :], rhs=acc[:, ai * 2 + 1, :], start=False, stop=True),
                          waits=[tadds[1]])
                acc_free[ai] = [tr]
                qr.free[qi] = [tr]
                selT_free[sp] = [tr]
                oi = orr.next()
                n1 = P.op("vector", lambda e: e.reciprocal(out=rinv[:], in_=ps[:, R_B, :]), waits=[tr])
                n2 = P.op("vector", lambda e, oi=oi: e.tensor_tensor(out=oT[:, oi, :], in0=ps[:, O_B, :], in1=rinv[:], op=ALU.mult),
                          waits=[n1, last_pe] + orr.free[oi])
                S["or_free"] = [n2]
                ts = P.dma("sync", osc[h, :, mt * 512:(mt + 1) * 512], oT[:, oi, :], orr.sems[oi], waits=[n2])
                orr.free[oi] = [ts]
                if mt == NT_OWN - 1:
                    kvr.free[kv] = [tr]
                    if h + 2 < 8:
                        load_kv(h + 2)
            P.barrier("A")
            P.flush()

        with ExitStack() as st:
            def sb(name, shape, dt):
                return st.enter_context(nc.sbuf_tensor(name, shape, dt))
            xin = sb("xin2", [128, 2, D], F32)
            ost = sb("ost", [128, D], F32)
            xT = sb("xT", [128, 16, 512], F32)
            uA = sb("uA", [128, 16, 512], BF16)
            uB = sb("uB", [128, 16, 512], BF16)
            big = sb("big", [128, 44, 512], BF16)
            hcb = sb("hcb", [128, 2, 542], F32)
            hhist = sb("hhist", [128, 8, 30], F32)
            cv = sb("cv", [128, 8, 512], F32)
            cv2 = sb("cv2", [128, 512], F32)
            tmp = sb("tmp", [128, 4, 512], F32)
            stat = sb("stat", [128, 4, 512], F32)
            abuf = sb("abuf", [128, 2, 514], F32)
            ahist = sb("ahist", [128, 44, 2], F32)
            NSL = 6
            wsl = sb("wsl", [128, NSL, 16, 128], BF16)
            WS = {"in": ws_in, "co": ws_co, "ao": ws_ao, "o": ws_o, "up": ws_up, "dn": ws_dn}

            def emit_mf(reqs, dry):
                P.dry = dry
                P.new_phase("M")
                rec = []
                wsems = [P.sem(f"ws{i}") for i in range(NSL)]
                wfree = [[] for _ in range(NSL)]
                wstate = {"issued": 0, "next": 0, "toks": {}}

                def wget(req):
                    if dry:
                        rec.append(req)
                        return 0, None
                    i = wstate["next"]
                    assert reqs[i] == req, (i, reqs[i], req)
                    wstate["next"] += 1
                    lim = min(len(reqs), i + NSL)
                    while wstate["issued"] < lim:
                        j = wstate["issued"]
                        dn, c, k0, nk = reqs[j]
                        s = j % NSL
                        wstate["toks"][j] = P.dma("sync", wsl[:, s, 0:nk, :], WS[dn][c, :, k0:k0 + nk, :], wsems[s], waits=wfree[s])
                        wstate["issued"] += 1
                    return i % NSL, wstate["toks"].pop(i)

                mmb = BankRing([0, 1, 2, 3, 4, 5])
                tmr = Ring(P, "tmp", [0, 1, 2, 3])
                xring = Ring(P, "xin2", [0, 1])
                hcr = Ring(P, "hcb", [0, 1])
                abr = Ring(P, "abuf", [0, 1])
                s_o = P.sem("ldo")
                s_out = P.sem("stout")
                ST = {"ost_free": []}
                flag = pv[:, O_FLAG:O_FLAG + 1]
                vtok = lambda: (P.pg["vector"], P.pg["vector"].v)
                atok = lambda: (P.pg["scalar"], P.pg["scalar"].v)
                ptok = lambda: (P.pg["tensor"], P.pg["tensor"].v)
                P.op("vector", lambda e: e.memset(hhist[:], 0.0))
                P.op("vector", lambda e: e.memset(ahist[:], 0.0))
                P.op("vector", lambda e: e.memset(hcb[:], 0.0))
                P.op("vector", lambda e: e.memset(abuf[:], 0.0))

                def group(nk, lhs_fn, rhs_fn, extra_waits):
                    b = mmb.next()
                    tpe = None
                    for k in range(nk):
                        tpe = P.op("tensor", lambda e, b=b, k=k: e.matmul(ps[:, b, :], lhsT=lhs_fn(k), rhs=rhs_fn(k),
                                                                         start=(k == 0), stop=(k == nk - 1)),
                                   waits=list(extra_waits) + mmb.free[b], mark=(k == nk - 1))
                    return b, tpe

                def wgroup(req, rhs_fn, extra_waits):
                    s, tl = wget(req)
                    b, tpe = group(req[3], lambda k, s=s: wsl[:, s, k, :], rhs_fn, list(extra_waits) + [tl])
                    wfree[s] = [tpe]
                    return b, tpe

                def ln_stats(nch, src_fn, ones_t, eps, waits):
                    MB, QB = 6, 7
                    tm = None
                    tq = None
                    for c in range(nch):
                        ti = tmr.next()
                        tsq = P.op("scalar", lambda e, c=c, ti=ti: e.activation(out=tmp[:, ti, :], in_=src_fn(c), func=AF.Square),
                                   waits=list(waits) + tmr.free[ti])
                        tm = P.op("tensor", lambda e, c=c: e.matmul(ps[:, MB, :], lhsT=ones_t[:, :], rhs=src_fn(c),
                                                                    start=(c == 0), stop=(c == nch - 1)), waits=list(waits))
                        tq = P.op("tensor", lambda e, c=c, ti=ti: e.matmul(ps[:, QB, :], lhsT=ones_t[:, :], rhs=tmp[:, ti, :],
                                                                          start=(c == 0), stop=(c == nch - 1)), waits=[tsq])
                        tmr.free[ti] = [tq]
                    a = P.op("vector", lambda e: e.tensor_copy(out=stat[:, 0, :], in_=ps[:, MB, :]), waits=[tm, tq])
                    a = P.op("vector", lambda e: e.tensor_tensor(out=stat[:, 2, :], in0=stat[:, 0, :], in1=stat[:, 0, :], op=ALU.mult), waits=[a])
                    a = P.op("vector", lambda e: e.tensor_tensor(out=stat[:, 2, :], in0=ps[:, QB, :], in1=stat[:, 2, :], op=ALU.subtract), waits=[a])
                    a = P.op("vector", lambda e: e.tensor_scalar(out=stat[:, 2, :], in0=stat[:, 2, :], scalar1=0.0, scalar2=eps,
                                                                 op0=ALU.max, op1=ALU.add), waits=[a])
                    a2 = P.op("scalar", lambda e: e.activation(out=stat[:, 2, :], in_=stat[:, 2, :], func=AF.Sqrt), waits=[a])
                    a = P.op("vector", lambda e: e.reciprocal(out=stat[:, 1, :], in_=stat[:, 2, :]), waits=[a2])
                    return a

                FR = {}
                FRS = {"cv2": []}

                def front(t):
                    tok0 = t * 512
                    w_prev = [ptok(), vtok(), atok()]
                    evi = 0
                    for sub in range(4):
                        i = xring.next()
                        tl = P.dma("sync", xin[:, i, :], xa[tok0 + sub * 128:tok0 + (sub + 1) * 128, :], xring.sems[i],
                                   waits=xring.free[i])
                        tpe = None
                        for c4 in range(4):
                            b = mmb.next()
                            for cq in range(4):
                                c = c4 * 4 + cq
                                tpe = P.op("tensor", lambda e, b=b, cq=cq, i=i, c=c: e.transpose(
                                    ps[:, b, cq * 128:(cq + 1) * 128], xin[:, i, c * 128:(c + 1) * 128], ident[:]),
                                    waits=[tl] + mmb.free[b], mark=(cq == 3))
                            tes = []
                            for cq in range(4):
                                c = c4 * 4 + cq
                                if evi % 2 == 0:
                                    te = P.op("scalar", lambda e, b=b, cq=cq, c=c, sub=sub: e.activation(
                                        out=uA[:, c, sub * 128:(sub + 1) * 128], in_=ps[:, b, cq * 128:(cq + 1) * 128], func=AF.Identity,
                                        scale=par[:, P_SC1 + c:P_SC1 + c + 1], bias=par[:, P_SH1 + c:P_SH1 + c + 1]),
                                        waits=[tpe] + w_prev)
                                else:
                                    te = P.op("vector", lambda e, b=b, cq=cq, c=c, sub=sub: e.tensor_scalar(
                                        out=uA[:, c, sub * 128:(sub + 1) * 128], in0=ps[:, b, cq * 128:(cq + 1) * 128],
                                        scalar1=par[:, P_SC1 + c:P_SC1 + c + 1], scalar2=par[:, P_SH1 + c:P_SH1 + c + 1],
                                        op0=ALU.mult, op1=ALU.add), waits=[tpe] + w_prev)
                                tes.append(te)
                            evi += 1
                            mmb.free[b] = tes[-1:]
                        xring.free[i] = [tpe]
                        yield
                    tu = [vtok(), atok()]
                    for i in range(8):
                        bl, tl_ = wgroup(("in", i, 0, 16), lambda k: uA[:, k, :], tu)
                        bg, tg_ = wgroup(("in", 8 + i, 0, 16), lambda k: uA[:, k, :], tu)
                        ti = tmr.next()
                        hi = hcr.next()
                        s1 = P.op("scalar", lambda e, bg=bg, ti=ti: e.activation(out=tmp[:, ti, :], in_=ps[:, bg, :], func=AF.Sigmoid),
                                  waits=[tg_] + tmr.free[ti])
                        mmb.free[bg] = [s1]
                        h0 = P.op("vector", lambda e, hi=hi, i=i: e.tensor_copy(out=hcb[:, hi, 0:30], in_=hhist[:, i, :]),
                                  waits=hcr.free[hi])
                        h1 = P.op("vector", lambda e, bl=bl, ti=ti, hi=hi: e.tensor_tensor(
                            out=hcb[:, hi, 30:542], in0=ps[:, bl, :], in1=tmp[:, ti, :], op=ALU.mult), waits=[tl_, s1, h0])
                        mmb.free[bl] = [h1]
                        tmr.free[ti] = [h1]
                        h2 = P.op("vector", lambda e, hi=hi, i=i: e.tensor_copy(out=hhist[:, i, :], in_=hcb[:, hi, 512:542]), waits=[h1])
                        if t == 0:
                            h2 = P.op("vector", lambda e, i=i: e.tensor_scalar(out=hhist[:, i, :], in0=hhist[:, i, :], scalar1=flag,
                                                                               scalar2=None, op0=ALU.mult), waits=[h2])
                        yield
                        aE = P.op("vector", lambda e, hi=hi, i=i: e.tensor_scalar(
                            out=cv[:, i, :], in0=hcb[:, hi, 0:512], scalar1=pv[:, O_CW + i * 31:O_CW + i * 31 + 1],
                            scalar2=pv[:, O_CDB + i:O_CDB + i + 1], op0=ALU.mult, op1=ALU.add), waits=[h1, h2] + w_prev)
                        aO = P.op("vector", lambda e, hi=hi, i=i: e.tensor_scalar(
                            out=cv2[:, :], in0=hcb[:, hi, 1:513], scalar1=pv[:, O_CW + i * 31 + 1:O_CW + i * 31 + 2],
                            scalar2=None, op0=ALU.mult), waits=[h1, h2] + FRS["cv2"])
                        for j in range(2, 31):
                            if j % 2 == 0:
                                aE = P.op("vector", lambda e, hi=hi, i=i, j=j: e.scalar_tensor_tensor(
                                    out=cv[:, i, :], in0=hcb[:, hi, j:j + 512], scalar=pv[:, O_CW + i * 31 + j:O_CW + i * 31 + j + 1],
                                    in1=cv[:, i, :], op0=ALU.mult, op1=ALU.add), waits=[aE])
                            else:
                                aO = P.op("vector", lambda e, hi=hi, i=i, j=j: e.scalar_tensor_tensor(
                                    out=cv2[:, :], in0=hcb[:, hi, j:j + 512], scalar=pv[:, O_CW + i * 31 + j:O_CW + i * 31 + j + 1],
                                    in1=cv2[:, :], op0=ALU.mult, op1=ALU.add), waits=[aO])
                            if j % 8 == 0:
                                yield
                        a = P.op("vector", lambda e, i=i: e.tensor_tensor(out=cv[:, i, :], in0=cv[:, i, :], in1=cv2[:, :], op=ALU.add),
                                 waits=[aE, aO])
                        FRS["cv2"] = [a]
                        hcr.free[hi] = [a]
                        yield
                    FR[t] = [vtok(), atok(), ptok()]

                for _ in front(0):
                    pass
                for t in range(NT_OWN):
                    tok0 = t * 512
                    tfront = FR.pop(t)
                    big_w = [ptok(), vtok(), atok()]
                    to = None
                    for h in range(8):
                        to = P.dma("gpsimd", big[:, 24 + h, :], osc[h, :, tok0:tok0 + 512], s_o, waits=big_w)
                    xw = [ptok(), vtok(), atok()]
                    for sub in range(4):
                        i = xring.next()
                        tl = P.dma("sync", xin[:, i, :], xa[tok0 + sub * 128:tok0 + (sub + 1) * 128, :], xring.sems[i],
                                   waits=xring.free[i])
                        tpe = None
                        for c4 in range(4):
                            b = mmb.next()
                            for cq in range(4):
                                c = c4 * 4 + cq
                                tpe = P.op("tensor", lambda e, b=b, cq=cq, i=i, c=c: e.transpose(
                                    ps[:, b, cq * 128:(cq + 1) * 128], xin[:, i, c * 128:(c + 1) * 128], ident[:]),
                                    waits=[tl] + mmb.free[b], mark=(cq == 3))
                            te = P.op("vector", lambda e, b=b, c4=c4, sub=sub: e.tensor_copy(
                                out=xT[:, c4 * 4:(c4 + 1) * 4, sub * 128:(sub + 1) * 128],
                                in_=ps[:, b, :].rearrange("p (c d) -> p c d", d=128)), waits=[tpe] + xw)
                            mmb.free[b] = [te]
                        xring.free[i] = [tpe]
                    tx = [vtok(), atok()]
                    tu = tfront
                    tst = ln_stats(8, lambda c: cv[:, c, :], ones1k, EPS_LN, tfront)
                    for i in range(8):
                        a = P.op("vector", lambda e, i=i: e.tensor_tensor(out=cv[:, i, :], in0=cv[:, i, :], in1=stat[:, 0, :], op=ALU.subtract),
                                 waits=[tst, ptok()])
                        a = P.op("vector", lambda e, i=i: e.tensor_tensor(out=cv[:, i, :], in0=cv[:, i, :], in1=stat[:, 1, :], op=ALU.mult), waits=[a])
                        P.op("scalar", lambda e, i=i: e.activation(out=big[:, i, :], in_=cv[:, i, :], func=AF.Silu,
                                                                   scale=pv[:, O_CLG + i:O_CLG + i + 1], bias=pv[:, O_CLB + i:O_CLB + i + 1]),
                             waits=[a] + big_w)
                    thn = atok()
                    for dc in range(16):
                        b1, t1_ = wgroup(("co", dc, 0, 8), lambda k: big[:, k, :], [thn])
                        b2, t2_ = wgroup(("ao", dc, 0, 8), lambda k: big[:, 24 + k, :], [to])
                        b3, t3_ = wgroup(("in", 40 + dc, 0, 16), lambda k: uA[:, k, :], tu)
                        b4, t4_ = wgroup(("in", 56 + dc, 0, 16), lambda k: uA[:, k, :], tu)
                        i3 = tmr.next()
                        s3 = P.op("scalar", lambda e, b3=b3, i3=i3: e.activation(out=tmp[:, i3, :], in_=ps[:, b3, :], func=AF.Sigmoid),
                                  waits=[t3_] + tmr.free[i3])
                        mmb.free[b3] = [s3]
                        i4 = tmr.next()
                        s4 = P.op("scalar", lambda e, b4=b4, i4=i4: e.activation(out=tmp[:, i4, :], in_=ps[:, b4, :], func=AF.Sigmoid),
                                  waits=[t4_] + tmr.free[i4])
                        mmb.free[b4] = [s4]
                        a1 = P.op("vector", lambda e, b1=b1, i3=i3, dc=dc: e.scalar_tensor_tensor(
                            out=tmp[:, i3, :], in0=ps[:, b1, :], scalar=pv[:, O_BCO + dc:O_BCO + dc + 1], in1=tmp[:, i3, :],
                            op0=ALU.add, op1=ALU.mult), waits=[t1_, s3])
                        mmb.free[b1] = [a1]
                        a2 = P.op("vector", lambda e, b2=b2, i4=i4: e.tensor_tensor(
                            out=tmp[:, i4, :], in0=ps[:, b2, :], in1=tmp[:, i4, :], op=ALU.mult), waits=[t2_, s4])
                        mmb.free[b2] = [a2]
                        a3 = P.op("vector", lambda e, i3=i3, i4=i4, dc=dc: e.tensor_tensor(
                            out=big[:, 8 + dc, :], in0=tmp[:, i3, :], in1=tmp[:, i4, :], op=ALU.add), waits=[a1, a2] + big_w)
                        tmr.free[i3] = [a3]
                        tmr.free[i4] = [a3]
                    tm_ = vtok()
                    for dc in range(16):
                        b, tp_ = wgroup(("o", dc, 0, 16), lambda k: big[:, 8 + k, :], [tm_])
                        a = P.op("vector", lambda e, b=b, dc=dc: e.scalar_tensor_tensor(
                            out=xT[:, dc, :], in0=ps[:, b, :], scalar=par[:, P_G1S + dc:P_G1S + dc + 1], in1=xT[:, dc, :],
                            op0=ALU.mult, op1=ALU.add), waits=[tp_] + tx)
                        mmb.free[b] = [a]
                    tz = vtok()
                    tst = ln_stats(16, lambda c: xT[:, c, :], ones2k, EPS_DN, [tz])
                    for c in range(16):
                        a = P.op("vector", lambda e, c=c: e.tensor_tensor(out=xT[:, c, :], in0=xT[:, c, :], in1=stat[:, 0, :], op=ALU.subtract),
                                 waits=[tst, ptok()])
                        a = P.op("vector", lambda e, c=c: e.tensor_tensor(out=xT[:, c, :], in0=xT[:, c, :], in1=stat[:, 1, :], op=ALU.mult), waits=[a])
                        a5 = P.op("scalar", lambda e, c=c: e.activation(out=uB[:, c, :], in_=xT[:, c, :], func=AF.Identity,
                                                                        scale=par[:, P_A2 + c:P_A2 + c + 1], bias=par[:, P_B2 + c:P_B2 + c + 1]),
                                  waits=[a, ptok()])
                        P.op("vector", lambda e, c=c: e.tensor_scalar(out=xT[:, c, :], in0=xT[:, c, :], scalar1=pv[:, O_LN1G + c:O_LN1G + c + 1],
                                                                      scalar2=pv[:, O_LN1B + c:O_LN1B + c + 1], op0=ALU.mult, op1=ALU.add),
                             waits=[a, a5])
                    tu2 = [vtok(), atok()]
                    nxt = front(t + 1) if t + 1 < NT_OWN else None
                    for j in range(44):
                        ba, ta_ = wgroup(("up", j, 0, 16), lambda k: uB[:, k, :], tu2)
                        bv, tv_ = wgroup(("up", 44 + j, 0, 16), lambda k: uB[:, k, :], tu2)
                        ai = abr.next()
                        c0 = P.op("vector", lambda e, ai=ai, j=j: e.tensor_copy(out=abuf[:, ai, 0:2], in_=ahist[:, j, :]), waits=abr.free[ai])
                        c1 = P.op("scalar", lambda e, ba=ba, ai=ai: e.activation(out=abuf[:, ai, 2:514], in_=ps[:, ba, :], func=AF.Copy),
                                  waits=[ta_] + abr.free[ai])
                        mmb.free[ba] = [c1]
                        c2 = P.op("vector", lambda e, ai=ai, j=j: e.tensor_copy(out=ahist[:, j, :], in_=abuf[:, ai, 512:514]), waits=[c1, c0])
                        if t == 0:
                            c2 = P.op("vector", lambda e, j=j: e.tensor_scalar(out=ahist[:, j, :], in0=ahist[:, j, :], scalar1=flag,
                                                                               scalar2=None, op0=ALU.mult), waits=[c2])
                        ti = tmr.next()
                        a = P.op("vector", lambda e, ai=ai, ti=ti, j=j: e.tensor_scalar(
                            out=tmp[:, ti, :], in0=abuf[:, ai, 0:512], scalar1=pv[:, O_FW + 3 * j:O_FW + 3 * j + 1],
                            scalar2=pv[:, O_FB + j:O_FB + j + 1], op0=ALU.mult, op1=ALU.add), waits=[c0, c1, c2] + tmr.free[ti])
                        a = P.op("vector", lambda e, ai=ai, ti=ti, j=j: e.scalar_tensor_tensor(
                            out=tmp[:, ti, :], in0=abuf[:, ai, 1:513], scalar=pv[:, O_FW + 3 * j + 1:O_FW + 3 * j + 2], in1=tmp[:, ti, :],
                            op0=ALU.mult, op1=ALU.add), waits=[a])
                        a = P.op("vector", lambda e, ai=ai, ti=ti, j=j: e.scalar_tensor_tensor(
                            out=tmp[:, ti, :], in0=abuf[:, ai, 2:514], scalar=pv[:, O_FW + 3 * j + 2:O_FW + 3 * j + 3], in1=tmp[:, ti, :],
                            op0=ALU.mult, op1=ALU.add), waits=[a])
                        abr.free[ai] = [a]
                        s_ = P.op("scalar", lambda e, ti=ti: e.activation(out=tmp[:, ti, :], in_=tmp[:, ti, :], func=AF.Silu), waits=[a])
                        hh = P.op("vector", lambda e, bv=bv, ti=ti, j=j: e.tensor_tensor(
                            out=big[:, j, :], in0=ps[:, bv, :], in1=tmp[:, ti, :], op=ALU.mult),
                            waits=[tv_, s_, tm_, ptok()] if j < 32 else [tv_, s_])
                        mmb.free[bv] = [hh]
                        tmr.free[ti] = [hh]
                        if nxt is not None and next(nxt, "done") == "done":
                            nxt = None
                    if nxt is not None:
                        for _ in nxt:
                            pass
                    th = vtok()
                    if t == 0:
                        continue
                    for dc in range(16):
                        b = mmb.next()
                        tpe = None
                        for part, (k0, nk) in enumerate(((0, 16), (16, 16), (32, 12))):
                            s, tl = wget(("dn", dc, k0, nk))
                            for k in range(nk):
                                tpe = P.op("tensor", lambda e, b=b, s=s, k=k, k0=k0, part=part, nk=nk: e.matmul(
                                    ps[:, b, :], lhsT=wsl[:, s, k, :], rhs=big[:, k0 + k, :],
                                    start=(part == 0 and k == 0), stop=(part == 2 and k == nk - 1)),
                                    waits=[tl, th] + mmb.free[b], mark=(k == nk - 1))
                            wfree[s] = [tpe]
                        a = P.op("vector", lambda e, b=b, dc=dc: e.scalar_tensor_tensor(
                            out=xT[:, dc, :], in0=ps[:, b, :], scalar=par[:, P_G2S + dc:P_G2S + dc + 1], in1=xT[:, dc, :],
                            op0=ALU.mult, op1=ALU.add), waits=[tpe])
                        mmb.free[b] = [a]
                    tz = vtok()
                    tst = ln_stats(16, lambda c: xT[:, c, :], ones2k, EPS_DN, [tz])
                    for c in range(16):
                        a = P.op("vector", lambda e, c=c: e.tensor_tensor(out=xT[:, c, :], in0=xT[:, c, :], in1=stat[:, 0, :], op=ALU.subtract),
                                 waits=[tst, ptok()])
                        a = P.op("vector", lambda e, c=c: e.tensor_tensor(out=xT[:, c, :], in0=xT[:, c, :], in1=stat[:, 1, :], op=ALU.mult), waits=[a])
                        P.op("scalar", lambda e, c=c: e.activation(out=xT[:, c, :], in_=xT[:, c, :], func=AF.Identity,
                                                                   scale=pv[:, O_LN2G + c:O_LN2G + c + 1], bias=pv[:, O_LN2B + c:O_LN2B + c + 1]),
                             waits=[a])
                    tfin = atok()
                    for sub in range(4):
                        for c4 in range(4):
                            b = mmb.next()
                            tpe = None
                            for cq in range(4):
                                c = c4 * 4 + cq
                                tpe = P.op("tensor", lambda e, b=b, cq=cq, c=c, sub=sub: e.transpose(
                                    ps[:, b, cq * 128:(cq + 1) * 128], xT[:, c, sub * 128:(sub + 1) * 128], ident[:]),
                                    waits=[tfin] + mmb.free[b], mark=(cq == 3))
                            if c4 % 2 == 0:
                                te = P.op("vector", lambda e, b=b, c4=c4: e.tensor_copy(out=ost[:, c4 * 512:(c4 + 1) * 512], in_=ps[:, b, :]),
                                          waits=[tpe] + ST["ost_free"])
                            else:
                                te = P.op("scalar", lambda e, b=b, c4=c4: e.activation(out=ost[:, c4 * 512:(c4 + 1) * 512], in_=ps[:, b, :], func=AF.Copy),
                                          waits=[tpe] + ST["ost_free"])
                            mmb.free[b] = [te]
                        r0 = (t - 1) * 512 + sub * 128
                        ts = P.dma("sync", outd[r0:r0 + 128, :], ost[:, :], s_out, waits=[vtok(), atok()])
                        ST["ost_free"] = [ts]
                return rec

            reqs = emit_mf(None, True)
            emit_mf(reqs, False)
            P.barrier("M")
            P.flush()
    return nc


def _bf16(a):
    return np.asarray(a, dtype=np.float32).astype(ml_dtypes.bfloat16)


def _host_consts():
    ident = np.eye(128, dtype=np.float32)
    perm = np.zeros((32, 32), np.float32)
    for m in range(32):
        perm[(m + 16) % 32, m] = 1.0
    eall = np.zeros((NBLK, NBLK, 128), np.float32)
    for j in range(NBLK):
        eall[j, j, :] = 1.0
    tri = np.zeros((128, 4, 512), np.float32)
    tt = np.arange(128)[:, None]
    for half in range(2):
        for k2 in range(2):
            ql = np.arange(256)[None, :]
            m = (k2 * 128 + tt) > ql
            tri[:, half * 2 + k2, half * 256:(half + 1) * 256] = np.where(m, NEG, 0.0)
    return ident, perm, _bf16(eall.reshape(NBLK, NBLK * 128)), _bf16(tri.reshape(128, 4 * 512))


def kernel(x, c, w_ada, b_ada, w_in, conv_dw_w, conv_dw_b, conv_ln_g, conv_ln_b, w_conv_out, b_conv_out, w_attn_out,
           w_out, ln1_g, ln1_b, w_up, ffn_dw_w, ffn_dw_b, w_down, ln2_g, ln2_b):
    f32 = np.float32
    x = np.asarray(x, f32)
    c = np.asarray(c, f32)
    ident, perm, eall, tri = _host_consts()

    def fm(v, nch):
        return np.ascontiguousarray(np.asarray(v, f32).reshape(nch, 128).T)

    L = 0
    cw = np.asarray(conv_dw_w, f32)[L]
    fw = np.asarray(ffn_dw_w, f32)[L]
    pv_base = np.concatenate([
        fm(ln1_g[L], 16), fm(ln1_b[L], 16), fm(ln2_g[L], 16), fm(ln2_b[L], 16), fm(b_conv_out[L], 16),
        fm(conv_dw_b[L], 8), fm(conv_ln_g[L], 8), fm(conv_ln_b[L], 8),
        np.ascontiguousarray(cw.reshape(31, 8, 128).transpose(2, 1, 0)).reshape(128, 8 * 31),
        np.ascontiguousarray(fw.reshape(3, 44, 128).transpose(2, 1, 0)).reshape(128, 44 * 3),
        fm(ffn_dw_b[L], 44)], axis=1).astype(f32)
    inv_freq = (np.float32(500000.0) ** (-np.arange(0, 32, 2, dtype=f32) / np.float32(32))).astype(f32)
    shared = {
        "w_ada": np.ascontiguousarray(np.asarray(w_ada, f32)[L]), "b_ada": np.ascontiguousarray(np.asarray(b_ada, f32)[L][None, :]),
        "w_in": np.ascontiguousarray(np.asarray(w_in, f32)[L]), "w_conv_out": np.ascontiguousarray(np.asarray(w_conv_out, f32)[L]),
        "w_attn_out": np.ascontiguousarray(np.asarray(w_attn_out, f32)[L]), "w_out": np.ascontiguousarray(np.asarray(w_out, f32)[L]),
        "w_up": np.ascontiguousarray(np.asarray(w_up, f32)[L]), "w_down": np.ascontiguousarray(np.asarray(w_down, f32)[L]),
        "eall": eall, "tri": tri, "ident": ident, "identb": _bf16(ident), "perm": perm,
    }
    in_maps = []
    for core in range(8):
        s, r = core // 4, core % 4
        own0 = r * CH
        others = [q for q in range(4) if q != r]
        pos = np.concatenate([np.arange(own0 - HALO, own0 + CH)] + [np.arange(q * CH, (q + 1) * CH) for q in others])
        valid_tok = pos >= 0
        xa = np.zeros((NALL, D), f32)
        xa[valid_tok] = x[s][pos[valid_tok]]
        posf = np.where(valid_tok, pos, 0).astype(f32)
        ang = posf[None, :] * inv_freq[:, None]
        cs_, sn_ = np.cos(ang).astype(f32), np.sin(ang).astype(f32)
        ropeC = np.concatenate([cs_, cs_], 0)
        ropeS = np.concatenate([-sn_, sn_], 0)
        gblk = pos[::256] // 256
        gblk = np.where(pos[::256] >= 0, gblk, -10)
        pb = np.zeros((18, NBLK), f32)
        val = np.zeros((18, NBLK), f32)
        for row in range(18):
            gb = gblk[row]
            for jj in range(NBLK):
                ok = (jj >= 2) and (gblk[jj] >= 0) and (gblk[jj] < gb)
                val[row, jj] = 1.0 if ok else 0.0
                pb[row, jj] = 0.0 if ok else -1e30
        pvc = np.concatenate([pv_base, np.full((128, 1), 0.0 if r == 0 else 1.0, f32)], axis=1)
        m = dict(shared)
        m.update({
            "xa": xa, "cc": fm(c[s], 16), "pv": np.ascontiguousarray(pvc),
            "ropeC": np.ascontiguousarray(ropeC), "ropeS": np.ascontiguousarray(ropeS),
            "pb": np.ascontiguousarray(np.broadcast_to(pb.reshape(1, -1), (128, 18 * NBLK))),
            "val": np.ascontiguousarray(np.broadcast_to(val.reshape(1, -1), (128, 18 * NBLK))),
        })
        in_maps.append(m)
    nc = build()
    res = run_bass_kernel_spmd(nc, in_maps, core_ids=list(range(8)))
    out = np.zeros((2, SEQ, D), f32)
    for core in range(8):
        s, r = core // 4, core % 4
        out[s, r * CH:(r + 1) * CH] = np.asarray(res.results[core]["out"], f32)
    return out
```
